# Optimizing a Trainium2 kernel written in Bass

```python
import math
import jax, jax.numpy as jnp
from jax import lax
import numpy as np

D_MODEL = 2048
BATCH = 16
SEQ = 2048
DEPTH = 2

MIX_WIDTH = D_MODEL
A_HEAD_DIM = 64
A_WIDTH = 3 * D_MODEL // 8
A_Q_HEADS = A_WIDTH // A_HEAD_DIM
A_KV_HEADS = A_Q_HEADS // 3
A_GROUP = A_Q_HEADS // A_KV_HEADS
A_KV_WIDTH = A_KV_HEADS * A_HEAD_DIM
WINDOW = 128
ATTN_BLOCK = 128
ROPE_THETA = 500000.0
ROT_DIM = A_HEAD_DIM // 4
B_WIDTH = D_MODEL // 4
B_CONV_WIDTH = 31
C_HEAD_DIM = 128
C_HEADS = (MIX_WIDTH - A_WIDTH - B_WIDTH) // C_HEAD_DIM
C_WIDTH = C_HEADS * C_HEAD_DIM
C_CONV_WIDTH = 4
CHUNK = 64

EPS = 1e-6
MAX_POS_OFFSET = 4096

COL_SPLITS = (A_WIDTH, A_KV_WIDTH, A_KV_WIDTH, A_WIDTH,
              2 * B_WIDTH, B_WIDTH,
              3 * C_WIDTH, C_HEADS, C_HEADS, C_WIDTH)
IN_COLS = sum(COL_SPLITS)

kernel_name = "hybrid_swa_conformer_gdn_parallel"


def _split_cols(p):
    idx = []
    s = 0
    for n in COL_SPLITS[:-1]:
        s += n
        idx.append(s)
    return jnp.split(p, idx, axis=-1)


def rms_norm(x, w):
    x32 = x.astype(jnp.float32)
    y = x32 * lax.rsqrt(jnp.mean(x32 * x32, axis=-1, keepdims=True) + EPS)
    return (y * w.astype(jnp.float32)).astype(x.dtype)


def layer_norm(x, w, b):
    x32 = x.astype(jnp.float32)
    mu = jnp.mean(x32, axis=-1, keepdims=True)
    var = jnp.mean(jnp.square(x32 - mu), axis=-1, keepdims=True)
    y = (x32 - mu) * lax.rsqrt(var + EPS)
    return (y * w.astype(jnp.float32) + b.astype(jnp.float32)).astype(x.dtype)


def l2_norm(x):
    return x * lax.rsqrt(jnp.sum(x * x, axis=-1, keepdims=True) + EPS)


def causal_depthwise_conv(x, w):
    K, C = w.shape
    return lax.conv_general_dilated(
        x, w[:, None, :].astype(x.dtype), window_strides=(1,), padding=[(K - 1, 0)],
        dimension_numbers=('NWC', 'WIO', 'NWC'), feature_group_count=C)


def rope_tables(positions):
    inv_freq = ROPE_THETA ** (-jnp.arange(0, ROT_DIM, 2, dtype=jnp.float32) / ROT_DIM)
    ang = positions.astype(jnp.float32)[..., None] * inv_freq
    return jnp.cos(ang)[:, :, None, :], jnp.sin(ang)[:, :, None, :]


def apply_partial_rope(x, cos, sin):
    half = ROT_DIM // 2
    x1 = x[..., :half].astype(jnp.float32)
    x2 = x[..., half:ROT_DIM].astype(jnp.float32)
    rot = jnp.concatenate([x1 * cos - x2 * sin, x2 * cos + x1 * sin], axis=-1)
    return jnp.concatenate([rot.astype(x.dtype), x[..., ROT_DIM:]], axis=-1)


def sliding_window_sink_attention(q, k, v, sinks):
    bsz, T = q.shape[0], q.shape[1]
    nb = T // ATTN_BLOCK
    qb = q.reshape(bsz, nb, ATTN_BLOCK, A_KV_HEADS, A_GROUP, A_HEAD_DIM)
    kb = k.reshape(bsz, nb, ATTN_BLOCK, A_KV_HEADS, A_HEAD_DIM)
    vb = v.reshape(bsz, nb, ATTN_BLOCK, A_KV_HEADS, A_HEAD_DIM)
    prev = lambda t: jnp.concatenate([jnp.zeros_like(t[:, :1]), t[:, :-1]], axis=1)
    kk = jnp.concatenate([prev(kb), kb], axis=2)
    vv = jnp.concatenate([prev(vb), vb], axis=2)
    s = jnp.einsum('bnqhgd,bnkhd->bnhgqk', qb, kk).astype(jnp.float32) * (A_HEAD_DIM ** -0.5)
    qi = jnp.arange(ATTN_BLOCK)[:, None]
    kj = jnp.arange(2 * ATTN_BLOCK)[None, :]
    dist = qi + ATTN_BLOCK - kj
    band = (dist >= 0) & (dist < WINDOW)
    not_pad = (jnp.arange(nb)[:, None, None] > 0) | (kj[None] >= ATTN_BLOCK)
    valid = band[None] & not_pad
    s = jnp.where(valid[None, :, None, None], s, -jnp.inf)
    sink = sinks.reshape(A_KV_HEADS, A_GROUP).astype(jnp.float32)[None, None, :, :, None, None]
    sink = jnp.broadcast_to(sink, s.shape[:-1] + (1,))
    p = jax.nn.softmax(jnp.concatenate([s, sink], axis=-1), axis=-1)[..., :-1]
    o = jnp.einsum('bnhgqk,bnkhd->bnqhgd', p.astype(v.dtype), vv)
    return o.reshape(bsz, T, A_Q_HEADS * A_HEAD_DIM)


def conformer_conv_module(u, conv_w, conv_b, ln_w, ln_b, pw_w, pw_b):
    a, gt = jnp.split(u, 2, axis=-1)
    h = a * jax.nn.sigmoid(gt)
    h = causal_depthwise_conv(h, conv_w) + conv_b.astype(h.dtype)
    h = jax.nn.silu(layer_norm(h, ln_w, ln_b))
    return h @ pw_w + pw_b


def gated_delta_rule_chunked(q, k, v, g, beta):
    bsz, T, H, Dk = q.shape
    Dv = v.shape[-1]
    n = T // CHUNK
    q = q * (Dk ** -0.5)

    def to_chunks(t):
        return t.reshape(bsz, n, CHUNK, H, t.shape[-1]).transpose(0, 1, 3, 2, 4)

    qc, kc, vc = to_chunks(q), to_chunks(k), to_chunks(v)
    bc = beta.reshape(bsz, n, CHUNK, H).transpose(0, 1, 3, 2)
    gc = jnp.cumsum(g.reshape(bsz, n, CHUNK, H).transpose(0, 1, 3, 2), axis=-1)
    idx = jnp.arange(CHUNK)
    incl = idx[:, None] >= idx[None, :]
    strict = idx[:, None] > idx[None, :]
    decay = jnp.exp(jnp.where(incl, gc[..., :, None] - gc[..., None, :], -jnp.inf))
    kb = kc * bc[..., None]
    lower = jnp.where(strict, jnp.einsum('bnhid,bnhjd->bnhij', kb, kc) * decay, 0.0)
    eye = jnp.eye(CHUNK, dtype=lower.dtype)
    rhs = jnp.concatenate([vc * bc[..., None], kb * jnp.exp(gc)[..., None]], axis=-1)
    sol = lax.linalg.triangular_solve(lower + eye, rhs, left_side=True, lower=True,
                                      unit_diagonal=True)
    u, w = sol[..., :Dv], sol[..., Dv:]
    intra = jnp.where(incl, jnp.einsum('bnhid,bnhjd->bnhij', qc, kc) * decay, 0.0)

    def step(S, xs):
        q_i, k_i, u_i, w_i, g_i, a_i = xs
        v_new = u_i - jnp.einsum('bhck,bhkv->bhcv', w_i, S)
        o_i = (jnp.einsum('bhck,bhkv->bhcv', q_i * jnp.exp(g_i)[..., None], S)
               + jnp.einsum('bhij,bhjv->bhiv', a_i, v_new))
        g_last = g_i[..., -1:]
        S = (S * jnp.exp(g_last)[..., None]
             + jnp.einsum('bhck,bhcv->bhkv', k_i * jnp.exp(g_last - g_i)[..., None], v_new))
        return S, o_i

    S0 = jnp.zeros((bsz, H, Dk, Dv), q.dtype)
    xs = tuple(jnp.moveaxis(t, 1, 0) for t in (qc, kc, u, w, gc, intra))
    _, o = lax.scan(step, S0, xs)
    return o.transpose(1, 0, 3, 2, 4).reshape(bsz, T, H, Dv)


def gated_deltanet(qkv_raw, b_raw, a_raw, conv_w, a_log, dt_bias):
    bsz, T = qkv_raw.shape[0], qkv_raw.shape[1]
    qkv = jax.nn.silu(causal_depthwise_conv(qkv_raw, conv_w)).astype(jnp.float32)
    q, k, v = jnp.split(qkv, 3, axis=-1)
    q = l2_norm(q.reshape(bsz, T, C_HEADS, C_HEAD_DIM))
    k = l2_norm(k.reshape(bsz, T, C_HEADS, C_HEAD_DIM))
    v = v.reshape(bsz, T, C_HEADS, C_HEAD_DIM)
    beta = jax.nn.sigmoid(b_raw.astype(jnp.float32))
    g = -jnp.exp(a_log.astype(jnp.float32)) * jax.nn.softplus(
        a_raw.astype(jnp.float32) + dt_bias.astype(jnp.float32))
    return gated_delta_rule_chunked(q, k, v, g, beta)


def setup_inputs(seed: int = 0) -> dict:
    key = jax.random.key(seed)
    ks = jax.random.split(key, 24)
    nrm = jax.random.normal
    x = nrm(ks[0], (BATCH, SEQ, D_MODEL), jnp.float32)
    offset = jax.random.randint(ks[1], (BATCH, 1), 0, MAX_POS_OFFSET, dtype=jnp.int32)
    positions = (offset + jnp.arange(SEQ, dtype=jnp.int32)[None, :]).astype(jnp.int32)
    norm_w = 1.0 + 0.02 * nrm(ks[2], (DEPTH, D_MODEL))
    w_in = nrm(ks[3], (DEPTH, D_MODEL, IN_COLS)) * D_MODEL ** -0.5
    q_norm_w = 1.0 + 0.02 * nrm(ks[4], (DEPTH, A_HEAD_DIM))
    k_norm_w = 1.0 + 0.02 * nrm(ks[5], (DEPTH, A_HEAD_DIM))
    sinks = nrm(ks[6], (DEPTH, A_Q_HEADS))
    b_conv_w = nrm(ks[7], (DEPTH, B_CONV_WIDTH, B_WIDTH)) * B_CONV_WIDTH ** -0.5
    b_conv_b = 0.02 * nrm(ks[8], (DEPTH, B_WIDTH))
    b_ln_w = 1.0 + 0.02 * nrm(ks[9], (DEPTH, B_WIDTH))
    b_ln_b = 0.02 * nrm(ks[10], (DEPTH, B_WIDTH))
    b_pw_w = nrm(ks[11], (DEPTH, B_WIDTH, B_WIDTH)) * B_WIDTH ** -0.5
    b_pw_b = 0.02 * nrm(ks[12], (DEPTH, B_WIDTH))
    c_conv_w = nrm(ks[13], (DEPTH, C_CONV_WIDTH, 3 * C_WIDTH)) * C_CONV_WIDTH ** -0.5
    c_a_log = jnp.log(jax.random.uniform(ks[14], (DEPTH, C_HEADS), minval=1.0, maxval=16.0))
    dt = jnp.exp(jax.random.uniform(ks[15], (DEPTH, C_HEADS),
                                    minval=math.log(1e-3), maxval=math.log(1e-1)))
    c_dt_bias = dt + jnp.log(-jnp.expm1(-dt))
    c_onorm_w = 1.0 + 0.02 * nrm(ks[16], (DEPTH, C_HEAD_DIM))
    w_out = nrm(ks[17], (DEPTH, MIX_WIDTH, D_MODEL)) * MIX_WIDTH ** -0.5
    return {"x": x, "positions": positions, "norm_w": norm_w, "w_in": w_in,
            "q_norm_w": q_norm_w, "k_norm_w": k_norm_w, "sinks": sinks,
            "b_conv_w": b_conv_w, "b_conv_b": b_conv_b, "b_ln_w": b_ln_w,
            "b_ln_b": b_ln_b, "b_pw_w": b_pw_w, "b_pw_b": b_pw_b,
            "c_conv_w": c_conv_w, "c_a_log": c_a_log, "c_dt_bias": c_dt_bias,
            "c_onorm_w": c_onorm_w, "w_out": w_out}


def reference(x, positions, norm_w, w_in, q_norm_w, k_norm_w, sinks, b_conv_w, b_conv_b,
              b_ln_w, b_ln_b, b_pw_w, b_pw_b, c_conv_w, c_a_log, c_dt_bias, c_onorm_w,
              w_out):
    bsz, T = x.shape[0], x.shape[1]
    cos, sin = rope_tables(positions)
    for l in range(DEPTH):
        h = rms_norm(x, norm_w[l])
        p = h @ w_in[l]
        qa, ka, va, za, ub, zb, qkv_c, b_c, a_c, zc = _split_cols(p)
        qa = apply_partial_rope(rms_norm(qa.reshape(bsz, T, A_Q_HEADS, A_HEAD_DIM), q_norm_w[l]), cos, sin)
        ka = apply_partial_rope(rms_norm(ka.reshape(bsz, T, A_KV_HEADS, A_HEAD_DIM), k_norm_w[l]), cos, sin)
        va = va.reshape(bsz, T, A_KV_HEADS, A_HEAD_DIM)
        oa = sliding_window_sink_attention(qa, ka, va, sinks[l]) * jax.nn.silu(za)
        ob = conformer_conv_module(ub, b_conv_w[l], b_conv_b[l], b_ln_w[l], b_ln_b[l],
                                   b_pw_w[l], b_pw_b[l]) * jax.nn.silu(zb)
        oc = gated_deltanet(qkv_c, b_c, a_c, c_conv_w[l], c_a_log[l], c_dt_bias[l]).astype(x.dtype)
        oc = rms_norm(oc, c_onorm_w[l]) * jax.nn.silu(zc.reshape(bsz, T, C_HEADS, C_HEAD_DIM))
        oc = oc.reshape(bsz, T, C_WIDTH)
        y = jnp.concatenate([oa, ob, oc], axis=-1)
        x = x + y @ w_out[l]
    return x
```

```python
import numpy as np
from contextlib import ExitStack
import concourse.bass as bass
import concourse.mybir as mybir
from concourse.bass_utils import run_bass_kernel_spmd

F32 = mybir.dt.float32
BF16 = mybir.dt.bfloat16
I32 = mybir.dt.int32
AF = mybir.ActivationFunctionType
ALU = mybir.AluOpType

ENG = ['pe', 'dve', 'act', 'pool', 'sp']
EPOCH = 30000

NCORES = 8
DEPTH = 2
D = 2048
T = 2048
NSEQ = 2
NTOK = NSEQ * T
EPS = 1e-6
Q0, K0, ZA0, UBA0, UBG0, ZB0, CQ0, CK0, CV0, ZC0 = 0, 768, 1024, 1792, 2304, 2816, 3328, 4096, 4864, 5632
NFM = 6400
NCH = NFM // 128
YA0, YB0, YC0 = 0, 768, 1280


class Prog:
    def __init__(self, nc, stack):
        self.nc = nc
        self.stack = stack
        self.streams = {e: [] for e in ENG}
        self.count = {e: 0 for e in ENG}
        self.sems = {}
        self.seen = {e: {} for e in ENG}
        self.res = {}
        self.dmacnt = {}
        self.excl = set()

    def sem(self, key):
        if key not in self.sems:
            self.sems[key] = self.stack.enter_context(
                self.nc.semaphore("s%d" % len(self.sems)))
        return self.sems[key]

    def _deps(self, reads, writes, acc):
        deps = {}

        def add(d):
            for k, v in d.items():
                if deps.get(k, 0) < v:
                    deps[k] = v
        for r in reads:
            st = self.res.get(r)
            if st:
                add(st[0])
                if r in self.excl:
                    add(st[1])
        for w in writes:
            st = self.res.get(w)
            if st:
                add(st[1])
                if not acc:
                    add(st[0])
        return deps

    def _emit_waits(self, eng, deps):
        for k, v in deps.items():
            if self.seen[eng].get(k, 0) >= v:
                continue
            self.seen[eng][k] = v
            h = self.sem(k)
            self.streams[eng].append(lambda e, h=h, v=v: e.wait_ge(h, v))

    def _commit(self, reads, writes, acc, key, val):
        for r in reads:
            st = self.res.setdefault(r, [{}, {}])
            if st[1].get(key, 0) < val:
                st[1][key] = val
        for w in writes:
            st = self.res.setdefault(w, [{}, {}])
            if acc:
                if st[0].get(key, 0) < val:
                    st[0][key] = val
            else:
                st[0] = {key: val}
                st[1] = {}

    def op(self, eng, fn, reads=(), writes=(), acc=False):
        deps = self._deps(reads, writes, acc)
        if eng == 'pe':
            deps = {k: v for k, v in deps.items() if k[0] != 'pe'}
        n = self.count[eng]
        key = (eng, n // EPOCH)
        val = n % EPOCH + 1
        self._emit_waits(eng, deps)
        h = self.sem(key)
        self.streams[eng].append(lambda e, fn=fn, h=h: fn(e).then_inc(h, 1))
        self.count[eng] = n + 1
        self._commit(reads, writes, acc, key, val)

    def dma(self, q, out, in_, reads=(), writes=(), acc=False, semkey=None, **kw):
        assert len(writes) == 1
        deps = self._deps(reads, writes, acc)
        self._emit_waits(q, deps)
        key = ('dma', semkey if semkey is not None else (reads[0] if acc else writes[0]))
        cnt = self.dmacnt.get(key, 0) + 16
        self.dmacnt[key] = cnt
        h = self.sem(key)
        self.streams[q].append(
            lambda e, out=out, in_=in_, h=h, kw=kw: e.dma_start(out=out, in_=in_, **kw).then_inc(h, 16))
        self._commit(reads, writes, acc, key, cnt)

    def barrier(self):
        deps = {}
        for k, c in self.dmacnt.items():
            deps[k] = c
        for e in ENG:
            n = self.count[e]
            if n > 0:
                deps[(e, (n - 1) // EPOCH)] = (n - 1) % EPOCH + 1
        for e in ENG:
            self._emit_waits(e, dict(deps))

    def finish(self):
        deps = {}
        for k, c in self.dmacnt.items():
            deps[k] = c
        for e in ENG:
            n = self.count[e]
            if n > 0:
                deps[(e, (n - 1) // EPOCH)] = (n - 1) % EPOCH + 1
        self._emit_waits('sp', deps)

    def build(self):
        block = self.stack.enter_context(self.nc.Block())
        streams = self.streams

        @block.tensor
        def _(e):
            for f in streams['pe']:
                f(e)

        @block.vector
        def _(e):
            for f in streams['dve']:
                f(e)

        @block.scalar
        def _(e):
            for f in streams['act']:
                f(e)

        @block.gpsimd
        def _(e):
            for f in streams['pool']:
                f(e)

        @block.sync
        def _(e):
            for f in streams['sp']:
                f(e)


class Ctx:
    pass


def stage_inproj(P, nc, C, l, x_src):
    with ExitStack() as st:
        SB = lambda n, s, d: st.enter_context(nc.sbuf_tensor("p_" + str(l) + n, s, d))
        PS = lambda n, s, d: st.enter_context(nc.psum_tensor("pp_" + str(l) + n, s, d))
        hT = SB("hT", [128, 16, T], BF16)
        nw = SB("nw", [128, D], F32)
        xs = [SB("xs%d" % i, [128, D], F32) for i in range(2)]
        hb = [SB("hb%d" % i, [128, D], BF16) for i in range(2)]
        junk = SB("junk", [128, D], BF16)
        ss = [SB("ss%d" % i, [128, 1], F32) for i in range(2)]
        rs = [SB("rs%d" % i, [128, 1], F32) for i in range(2)]
        wf = [SB("wf%d" % i, [128, 16, 128], F32) for i in range(2)]
        wb = [SB("wb%d" % i, [128, 16, 128], BF16) for i in range(3)]
        wbaf = SB("wbaf", [128, 16, 12], F32)
        wbab = SB("wbab", [128, 16, 12], BF16)
        wvf = SB("wvf", [128, 16, 256], F32)
        wvb = SB("wvb", [128, 16, 256], BF16)
        ob = [SB("ob%d" % i, [128, T], BF16) for i in range(2)]
        obf = SB("obf", [12, T], F32)
        ovb = [SB("ovb%d" % i, [128, 256], BF16) for i in range(2)]
        ptr = [PS("tr%d" % i, [128, 16, 128], BF16) for i in range(2)]
        pmm = [PS("mm%d" % i, [128, 512], F32) for i in range(4)]

        P.dma('sp', nw[:], C.norm_w[l:l + 1, :].broadcast_to([128, D]), writes=['p_nw'])
        def wload(c):
            P.dma('sp', wf[c % 2][:], C.w_fm[l, c], writes=['p_wf%d' % (c % 2)])

        def wcast(c):
            P.op('pool', lambda e, c=c: e.tensor_copy(out=wb[c % 3][:], in_=wf[c % 2][:]),
                 reads=['p_wf%d' % (c % 2)], writes=['p_wb%d' % (c % 3)])

        for s in range(NSEQ):
            wload(0)
            wcast(0)
            wload(1)
            P.dma('sp', wbaf[:], C.w_ba[l], writes=['p_wbaf'])
            P.op('pool', lambda e: e.tensor_copy(out=wbab[:], in_=wbaf[:]), reads=['p_wbaf'], writes=['p_wbab'])
            P.dma('sp', wvf[:], C.w_va[l], writes=['p_wvf'])
            P.op('pool', lambda e: e.tensor_copy(out=wvb[:], in_=wvf[:]), reads=['p_wvf'], writes=['p_wvb'])
            for i in range(16):
                b = i % 2
                tok0 = s * T + i * 128
                P.dma('sp', xs[b][:], x_src[tok0:tok0 + 128, :], writes=['p_xs%d' % b])
                P.op('act', lambda e, b=b: e.activation(out=junk[:], in_=xs[b][:], func=AF.Square, accum_out=ss[b][:]),
                     reads=['p_xs%d' % b], writes=['p_junk', 'p_ss%d' % b])
                P.op('act', lambda e, b=b: e.activation(out=rs[b][:], in_=ss[b][:], func=AF.Sqrt, scale=1.0 / D, bias=C.eps_col[:, 0:1]),
                     reads=['p_ss%d' % b], writes=['p_rs%d' % b])
                P.op('dve', lambda e, b=b: e.reciprocal(out=rs[b][:], in_=rs[b][:]),
                     reads=['p_rs%d' % b], writes=['p_rs%d' % b])
                P.op('dve', lambda e, b=b: e.scalar_tensor_tensor(out=hb[b][:], in0=xs[b][:], scalar=rs[b][:, 0:1], in1=nw[:],
                                                                  op0=ALU.mult, op1=ALU.mult),
                     reads=['p_xs%d' % b, 'p_rs%d' % b, 'p_nw'], writes=['p_hb%d' % b])
                for kc in range(16):
                    P.op('pe', lambda e, b=b, kc=kc: e.transpose(out=ptr[b][:, kc, :], in_=hb[b][:, kc * 128:(kc + 1) * 128], identity=C.ident_b[:]),
                         reads=['p_hb%d' % b], writes=['p_ptr%d' % b])
                ev = 'act' if i % 2 == 0 else 'dve'
                if ev == 'act':
                    P.op('act', lambda e, b=b, i=i: e.activation(out=hT[:, :, i * 128:(i + 1) * 128], in_=ptr[b][:], func=AF.Copy),
                         reads=['p_ptr%d' % b], writes=['p_hT'], acc=True)
                else:
                    P.op('dve', lambda e, b=b, i=i: e.tensor_copy(out=hT[:, :, i * 128:(i + 1) * 128], in_=ptr[b][:]),
                         reads=['p_ptr%d' % b], writes=['p_hT'], acc=True)
            nmm = 0
            for c in range(NCH):
                w3 = c % 3
                if c + 2 < NCH:
                    wload(c + 2)
                if c + 1 < NCH:
                    wcast(c + 1)
                o2 = c % 2
                for tt in range(4):
                    pb = nmm % 4
                    nmm += 1
                    for kc in range(16):
                        P.op('pe', lambda e, w3=w3, kc=kc, tt=tt, pb=pb: e.matmul(pmm[pb][:], lhsT=wb[w3][:, kc, :], rhs=hT[:, kc, tt * 512:(tt + 1) * 512],
                                                                                start=(kc == 0), stop=(kc == 15)),
                             reads=['p_wb%d' % w3, 'p_hT'], writes=['p_mm%d' % pb])
                    if tt % 2 == 0:
                        P.op('act', lambda e, o2=o2, tt=tt, pb=pb: e.activation(out=ob[o2][:, tt * 512:(tt + 1) * 512], in_=pmm[pb][:], func=AF.Copy),
                             reads=['p_mm%d' % pb], writes=['p_ob%d' % o2], acc=True)
                    else:
                        P.op('dve', lambda e, o2=o2, tt=tt, pb=pb: e.tensor_copy(out=ob[o2][:, tt * 512:(tt + 1) * 512], in_=pmm[pb][:]),
                             reads=['p_mm%d' % pb], writes=['p_ob%d' % o2], acc=True)
                P.dma('sp', C.pT[c * 128:(c + 1) * 128, s * T:(s + 1) * T], ob[o2][:], reads=['p_ob%d' % o2], writes=['pT'], acc=True)
            for tt in range(4):
                pb = nmm % 4
                nmm += 1
                for kc in range(16):
                    P.op('pe', lambda e, kc=kc, tt=tt, pb=pb: e.matmul(pmm[pb][0:12, :], lhsT=wbab[:, kc, :], rhs=hT[:, kc, tt * 512:(tt + 1) * 512],
                                                                        start=(kc == 0), stop=(kc == 15)),
                         reads=['p_wbab', 'p_hT'], writes=['p_mm%d' % pb])
                P.op('dve', lambda e, tt=tt, pb=pb: e.tensor_copy(out=obf[:, tt * 512:(tt + 1) * 512], in_=pmm[pb][0:12, :]),
                     reads=['p_mm%d' % pb], writes=['p_obf'], acc=True)
            P.dma('sp', C.baT[:, s * T:(s + 1) * T], obf[:], reads=['p_obf'], writes=['baT'], acc=True)
            for i in range(16):
                pb = nmm % 4
                nmm += 1
                o2 = i % 2
                for kc in range(16):
                    P.op('pe', lambda e, kc=kc, i=i, pb=pb: e.matmul(pmm[pb][:, 0:256], lhsT=hT[:, kc, i * 128:(i + 1) * 128], rhs=wvb[:, kc, :],
                                                                      start=(kc == 0), stop=(kc == 15)),
                         reads=['p_wvb', 'p_hT'], writes=['p_mm%d' % pb])
                P.op('dve', lambda e, o2=o2, pb=pb: e.tensor_copy(out=ovb[o2][:], in_=pmm[pb][:, 0:256]),
                     reads=['p_mm%d' % pb], writes=['p_ovb%d' % o2])
                tok0 = s * T + i * 128
                P.dma('sp', C.va[tok0:tok0 + 128, :], ovb[o2][:], reads=['p_ovb%d' % o2], writes=['va'], acc=True)


def stage_outproj(P, nc, C, l, x_src, x_dst):
    with ExitStack() as st:
        SB = lambda n, s, d: st.enter_context(nc.sbuf_tensor("o_" + str(l) + n, s, d))
        PS = lambda n, s, d: st.enter_context(nc.psum_tensor("op_" + str(l) + n, s, d))
        wo = SB("wo", [128, 16, D], BF16)
        wf = [SB("wf%d" % i, [128, D], F32) for i in range(2)]
        yt = [SB("yt%d" % i, [128, 16, 128], BF16) for i in range(2)]
        xs = [SB("xs%d" % i, [128, D], F32) for i in range(2)]
        xo = [SB("xo%d" % i, [128, D], F32) for i in range(2)]
        pmm = [PS("mm%d" % i, [128, 512], F32) for i in range(8)]
        for kc in range(16):
            f = kc % 2
            P.dma('sp', wf[f][:], C.w_o[l, :, kc, :], writes=['o_wf%d' % f])
            P.op('pool' if kc % 2 == 0 else 'act',
                 (lambda e, f=f, kc=kc: e.tensor_copy(out=wo[:, kc, :], in_=wf[f][:])) if kc % 2 == 0 else
                 (lambda e, f=f, kc=kc: e.activation(out=wo[:, kc, :], in_=wf[f][:], func=AF.Copy)),
                 reads=['o_wf%d' % f], writes=['o_wo'], acc=True)
        def oload(i):
            b = i % 2
            tok0 = i * 128
            P.dma('sp', yt[b][:], C.yT[:, tok0:tok0 + 128].rearrange("(kc p) t -> p kc t", p=128), reads=['yT'], writes=['o_yt%d' % b])
            P.dma('sp', xs[b][:], x_src[tok0:tok0 + 128, :], reads=[C.xres_key(x_src)], writes=['o_xs%d' % b])

        nmm = 0
        NT = NTOK // 128
        oload(0)
        for i in range(NT):
            b = i % 2
            tok0 = i * 128
            if i + 1 < NT:
                oload(i + 1)
            for ct in range(4):
                pb = nmm % 8
                nmm += 1
                for kc in range(16):
                    P.op('pe', lambda e, b=b, kc=kc, ct=ct, pb=pb: e.matmul(pmm[pb][:], lhsT=yt[b][:, kc, :], rhs=wo[:, kc, ct * 512:(ct + 1) * 512],
                                                                            start=(kc == 0), stop=(kc == 15)),
                         reads=['o_yt%d' % b, 'o_wo'], writes=['o_mm%d' % pb])
                P.op('dve', lambda e, b=b, ct=ct, pb=pb: e.tensor_tensor(out=xo[b][:, ct * 512:(ct + 1) * 512], in0=xs[b][:, ct * 512:(ct + 1) * 512],
                                                                         in1=pmm[pb][:], op=ALU.add),
                     reads=['o_mm%d' % pb, 'o_xs%d' % b], writes=['o_xo%d' % b], acc=True)
            P.dma('sp', x_dst[tok0:tok0 + 128, :], xo[b][:], reads=['o_xo%d' % b], writes=[C.xres_key(x_dst)], acc=True)


def stage_conformer(P, nc, C, l):
    with ExitStack() as st:
        SB = lambda n, s, d: st.enter_context(nc.sbuf_tensor("sb_" + str(l) + n, s, d))
        PS = lambda n, s, d: st.enter_context(nc.psum_tensor("bp_" + str(l) + n, s, d))
        cw = SB("cw", [128, 4, 31], F32)
        cb = SB("cb", [128, 4], F32)
        lnw = SB("lnw", [128, 4], F32)
        lnb = SB("lnb", [128, 4], F32)
        pwbias = SB("pwbias", [128, 4], F32)
        pwf = SB("pwf", [128, 4, 512], F32)
        pwb = SB("pwb", [128, 4, 512], BF16)
        dg = SB("dg", [128, 124, 128], BF16)
        W = 542
        at = [SB("at%d" % i, [128, 4, W], BF16) for i in range(2)]
        gt = [SB("gt%d" % i, [128, 4, W], BF16) for i in range(2)]
        zt = [SB("zt%d" % i, [128, 4, 512], BF16) for i in range(2)]
        sig = SB("sig", [128, 4, W], BF16)
        hg = [SB("hg%d" % i, [128, 4, W], BF16) for i in range(2)]
        xc = [SB("xc%d" % i, [128, 4, 512], F32) for i in range(2)]
        xcb = [SB("xcb%d" % i, [128, 4, 512], BF16) for i in range(2)]
        sqb = [SB("sqb%d" % i, [128, 4, 512], BF16) for i in range(2)]
        mean = SB("mean", [128, 512], F32)
        m2 = SB("m2", [128, 512], F32)
        var = SB("var", [128, 512], F32)
        rstd = SB("rstd", [128, 512], F32)
        t1 = SB("t1", [128, 4, 512], F32)
        hs = SB("hs", [128, 4, 512], BF16)
        sz = SB("sz", [128, 4, 512], BF16)
        yb = [SB("yb%d" % i, [128, 4, 512], BF16) for i in range(2)]
        pc = [PS("c%d" % i, [128, 512], F32) for i in range(4)]
        pm = PS("m", [128, 512], F32)
        pq = PS("q", [128, 512], F32)
        po = [PS("o%d" % i, [128, 512], F32) for i in range(2)]
        P.excl.update(['b_pc%d' % i for i in range(4)] + ['b_pm', 'b_pq', 'b_po0', 'b_po1'])

        P.dma('sp', cw[:], C.b_cw[l], writes=['b_cw'])
        P.dma('sp', cb[:], C.b_cb[l], writes=['b_cb'])
        P.dma('sp', lnw[:], C.b_lnw[l], writes=['b_lnw'])
        P.dma('sp', lnb[:], C.b_lnb[l], writes=['b_lnb'])
        P.dma('sp', pwbias[:], C.b_pwb[l], writes=['b_pwbias'])
        P.dma('sp', pwf[:], C.b_pw[l], writes=['b_pwf'])
        P.op('dve', lambda e: e.tensor_copy(out=pwb[:], in_=pwf[:]), reads=['b_pwf'], writes=['b_pwb'])
        for c in range(4):
            for k in range(31):
                eng = 'pool' if (c * 31 + k) % 3 == 2 else 'dve'
                P.op(eng, lambda e, c=c, k=k: e.tensor_scalar(out=dg[:, c * 31 + k, :], in0=C.ident_f[:], scalar1=cw[:, c, k:k + 1], scalar2=None, op0=ALU.mult),
                     reads=['b_cw', 'g_ident_f'], writes=['b_dg'], acc=True)

        tiles = [(s, tt) for s in range(NSEQ) for tt in range(4)]

        def front(i):
            s, tt = tiles[i]
            b = i % 2
            tok0 = s * T + tt * 512
            if tt == 0:
                P.op('pool', lambda e, b=b: e.memset(at[b][:, :, 0:30], 0.0), writes=['b_at%d' % b])
                P.op('pool', lambda e, b=b: e.memset(gt[b][:, :, 0:30], 0.0), writes=['b_gt%d' % b])
                P.dma('sp', at[b][:, :, 30:W], C.pT[UBA0:UBA0 + 512, tok0:tok0 + 512].rearrange("(c p) t -> p c t", p=128), reads=['pT'], writes=['b_at%d' % b])
                P.dma('sp', gt[b][:, :, 30:W], C.pT[UBG0:UBG0 + 512, tok0:tok0 + 512].rearrange("(c p) t -> p c t", p=128), reads=['pT'], writes=['b_gt%d' % b])
            else:
                P.dma('sp', at[b][:], C.pT[UBA0:UBA0 + 512, tok0 - 30:tok0 + 512].rearrange("(c p) t -> p c t", p=128), reads=['pT'], writes=['b_at%d' % b])
                P.dma('sp', gt[b][:], C.pT[UBG0:UBG0 + 512, tok0 - 30:tok0 + 512].rearrange("(c p) t -> p c t", p=128), reads=['pT'], writes=['b_gt%d' % b])
            P.dma('sp', zt[b][:], C.pT[ZB0:ZB0 + 512, tok0:tok0 + 512].rearrange("(c p) t -> p c t", p=128), reads=['pT'], writes=['b_zt%d' % b])
            P.op('act', lambda e, b=b: e.activation(out=sig[:], in_=gt[b][:], func=AF.Sigmoid), reads=['b_gt%d' % b], writes=['b_sig'])
            P.op('dve', lambda e, b=b: e.tensor_tensor(out=hg[b][:], in0=at[b][:], in1=sig[:], op=ALU.mult), reads=['b_at%d' % b, 'b_sig'], writes=['b_hg%d' % b])

        def conv(i):
            b = i % 2
            for c in range(4):
                for k in range(31):
                    P.op('pe', lambda e, c=c, k=k, b=b: e.matmul(pc[c][:], lhsT=dg[:, c * 31 + k, :], rhs=hg[b][:, c, k:k + 512], start=(k == 0), stop=(k == 30)),
                         reads=['b_dg', 'b_hg%d' % b], writes=['b_pc%d' % c])
                P.op('act', lambda e, c=c, b=b: e.activation(out=xc[b][:, c, :], in_=pc[c][:], func=AF.Identity, bias=cb[:, c:c + 1]),
                     reads=['b_pc%d' % c, 'b_cb'], writes=['b_xc%d' % b], acc=True)
                P.op('dve', lambda e, c=c, b=b: e.tensor_scalar(out=xcb[b][:, c, :], in0=pc[c][:], scalar1=cb[:, c:c + 1], scalar2=None, op0=ALU.add),
                     reads=['b_pc%d' % c, 'b_cb'], writes=['b_xcb%d' % b], acc=True)
                P.op('act', lambda e, c=c, b=b: e.activation(out=sqb[b][:, c, :], in_=pc[c][:], func=AF.Square, bias=cb[:, c:c + 1]),
                     reads=['b_pc%d' % c, 'b_cb'], writes=['b_sqb%d' % b], acc=True)

        def back(i):
            s, tt = tiles[i]
            b = i % 2
            tok0 = s * T + tt * 512
            for c in range(4):
                P.op('pe', lambda e, c=c, b=b: e.matmul(pm[:], lhsT=C.ones_b[:], rhs=xcb[b][:, c, :], start=(c == 0), stop=(c == 3)),
                     reads=['b_xcb%d' % b, 'g_ones_b'], writes=['b_pm'])
            for c in range(4):
                P.op('pe', lambda e, c=c, b=b: e.matmul(pq[:], lhsT=C.ones_b[:], rhs=sqb[b][:, c, :], start=(c == 0), stop=(c == 3)),
                     reads=['b_sqb%d' % b, 'g_ones_b'], writes=['b_pq'])
            P.op('act', lambda e: e.activation(out=mean[:], in_=pm[:], func=AF.Copy, scale=1.0 / 512), reads=['b_pm'], writes=['b_mean'])
            P.op('pool', lambda e: e.tensor_tensor(out=m2[:], in0=mean[:], in1=mean[:], op=ALU.mult), reads=['b_mean'], writes=['b_m2'])
            P.op('dve', lambda e: e.scalar_tensor_tensor(out=var[:], in0=pq[:], scalar=1.0 / 512, in1=m2[:], op0=ALU.mult, op1=ALU.subtract),
                 reads=['b_pq', 'b_m2'], writes=['b_var'])
            P.op('act', lambda e: e.activation(out=rstd[:], in_=var[:], func=AF.Ln, bias=C.eps_col[:, 0:1]), reads=['b_var'], writes=['b_rstd'])
            P.op('act', lambda e: e.activation(out=rstd[:], in_=rstd[:], func=AF.Exp, scale=-0.5), reads=['b_rstd'], writes=['b_rstd'])
            P.op('act', lambda e, b=b: e.activation(out=sz[:], in_=zt[b][:], func=AF.Silu), reads=['b_zt%d' % b], writes=['b_sz'])
            for c in range(4):
                P.op('pool', lambda e, c=c, b=b: e.tensor_tensor(out=t1[:, c, :], in0=xc[b][:, c, :], in1=mean[:], op=ALU.subtract),
                     reads=['b_xc%d' % b, 'b_mean'], writes=['b_t1_%d' % c])
                P.op('dve', lambda e, c=c: e.tensor_tensor(out=t1[:, c, :], in0=t1[:, c, :], in1=rstd[:], op=ALU.mult),
                     reads=['b_t1_%d' % c, 'b_rstd'], writes=['b_t1_%d' % c])
                P.op('act', lambda e, c=c: e.activation(out=hs[:, c, :], in_=t1[:, c, :], func=AF.Silu, scale=lnw[:, c:c + 1], bias=lnb[:, c:c + 1]),
                     reads=['b_t1_%d' % c, 'b_lnw', 'b_lnb'], writes=['b_hs'], acc=True)
            for co in range(4):
                o2 = co % 2
                for ci in range(4):
                    P.op('pe', lambda e, co=co, ci=ci, o2=o2: e.matmul(po[o2][:], lhsT=pwb[:, ci, co * 128:(co + 1) * 128], rhs=hs[:, ci, :], start=(ci == 0), stop=(ci == 3)),
                         reads=['b_pwb', 'b_hs'], writes=['b_po%d' % o2])
                P.op('dve', lambda e, co=co, b=b, o2=o2: e.scalar_tensor_tensor(out=yb[b][:, co, :], in0=po[o2][:], scalar=pwbias[:, co:co + 1], in1=sz[:, co, :],
                                                                                op0=ALU.add, op1=ALU.mult),
                     reads=['b_po%d' % o2, 'b_pwbias', 'b_sz'], writes=['b_yb%d' % b], acc=True)
            P.dma('sp', C.yT[YB0:YB0 + 512, tok0:tok0 + 512].rearrange("(c p) t -> p c t", p=128), yb[b][:], reads=['b_yb%d' % b], writes=['yT'], acc=True)

        NT = len(tiles)
        front(0)
        conv(0)
        for i in range(NT):
            if i + 1 < NT:
                front(i + 1)
                conv(i + 1)
            back(i)


def stage_attn(P, nc, C, l):
    TWO_PI = 6.283185307179586
    C1 = 6.28125
    C2 = TWO_PI - C1
    MAGIC = 12582912.0
    with ExitStack() as st:
        SB = lambda n, s, d: st.enter_context(nc.sbuf_tensor("sa_" + str(l) + n, s, d))
        PS = lambda n, s, d: st.enter_context(nc.psum_tensor("ap_" + str(l) + n, s, d))
        invf = SB("invf", [64, 1], F32)
        rtf = SB("rtf", [64, 64], F32)
        rtb = SB("rtb", [64, 64], BF16)
        mkf = SB("mkf", [128, 2, 128], F32)
        mkb = SB("mkb", [128, 2, 128], BF16)
        qw = SB("qw", [64, 1], F32)
        kw = SB("kw", [64, 1], F32)
        snk = SB("snk", [1, 12], F32)
        snkb = SB("snkb", [1, 12, 128], BF16)
        posi = SB("posi", [64, T], I32)
        ang = SB("ang", [64, T], F32)
        tn = SB("tn", [64, T], F32)
        tr = SB("tr", [64, T], F32)
        cosT = SB("cosT", [64, T], F32)
        sinT = SB("sinT", [64, T], F32)
        kT = SB("kT", [64, T], BF16)
        qT = SB("qT", [64, 3, T], BF16)
        zaT = SB("zaT", [64, 3, T], BF16)
        vt = SB("vt", [128, 16, 64], BF16)
        sq = SB("sq", [64, T], BF16)
        rstd = SB("rstd", [64, T], F32)
        xn = SB("xn", [64, T], BF16)
        t1 = SB("t1", [64, T], F32)
        t2 = SB("t2", [64, T], F32)
        kr = SB("kr", [64, T], BF16)
        qr = SB("qr", [64, 3, T], BF16)
        sza = SB("sza", [64, 3, T], BF16)
        ya = SB("ya", [64, 3, T], BF16)
        pe_sb = [SB("pe%d" % i, [128, 2, 384], BF16) for i in range(2)]
        pm_sb = [SB("pm%d" % i, [128, 2, 384], BF16) for i in range(2)]
        rden = SB("rden", [64, 384], F32)
        o_sb = SB("o_sb", [64, 384], F32)
        ps = [PS("b%d" % i, [128, 512], F32) for i in range(8)]
        PK = ['a_ps%d' % i for i in range(8)]
        P.excl.update(PK)

        P.dma('sp', invf[:], C.a_invf, writes=['a_invf'])
        P.dma('sp', rtf[:], C.a_rt, writes=['a_rtf'])
        P.dma('sp', mkf[:], C.a_mask, writes=['a_mkf'])
        P.dma('sp', qw[:], C.a_qw[l], writes=['a_qw'])
        P.dma('sp', kw[:], C.a_kw[l], writes=['a_kw'])
        P.dma('sp', snk[:], C.a_snk[l], writes=['a_snk'])
        P.op('dve', lambda e: e.tensor_copy(out=rtb[:], in_=rtf[:]), reads=['a_rtf'], writes=['a_rtb'])
        P.op('dve', lambda e: e.tensor_copy(out=mkb[:], in_=mkf[:]), reads=['a_mkf'], writes=['a_mkb'])
        P.op('act', lambda e: e.activation(out=snk[:], in_=snk[:], func=AF.Exp), reads=['a_snk'], writes=['a_snk'])
        P.op('dve', lambda e: e.tensor_copy(out=snkb[:], in_=snk[:, :].unsqueeze(2).broadcast_to([1, 12, 128])), reads=['a_snk'], writes=['a_snkb'])

        def sincos(dst, shift):
            P.op('dve', lambda e: e.tensor_scalar(out=tn[:], in0=ang[:], scalar1=shift, scalar2=1.0 / TWO_PI, op0=ALU.add, op1=ALU.mult),
                 reads=['a_ang'], writes=['a_tn'])
            P.op('dve', lambda e: e.tensor_scalar(out=tn[:], in0=tn[:], scalar1=MAGIC, scalar2=None, op0=ALU.add), reads=['a_tn'], writes=['a_tn'])
            P.op('dve', lambda e: e.tensor_scalar(out=tn[:], in0=tn[:], scalar1=-MAGIC, scalar2=None, op0=ALU.add), reads=['a_tn'], writes=['a_tn'])
            P.op('dve', lambda e: e.scalar_tensor_tensor(out=tr[:], in0=tn[:], scalar=-C1, in1=ang[:], op0=ALU.mult, op1=ALU.add),
                 reads=['a_tn', 'a_ang'], writes=['a_tr'])
            P.op('dve', lambda e: e.tensor_scalar(out=tr[:], in0=tr[:], scalar1=shift, scalar2=None, op0=ALU.add), reads=['a_tr'], writes=['a_tr'])
            P.op('dve', lambda e: e.scalar_tensor_tensor(out=tr[:], in0=tn[:], scalar=-C2, in1=tr[:], op0=ALU.mult, op1=ALU.add),
                 reads=['a_tn', 'a_tr'], writes=['a_tr'])
            P.op('dve', lambda e: e.tensor_scalar(out=tr[:], in0=tr[:], scalar1=3.1415925, scalar2=-3.1415925, op0=ALU.min, op1=ALU.max),
                 reads=['a_tr'], writes=['a_tr'])
            P.op('act', lambda e: e.activation(out=dst[:], in_=tr[:], func=AF.Sin), reads=['a_tr'], writes=['a_tab'])

        def normrope(src, wcol, wkey, dst, skey, dkey):
            P.op('act', lambda e: e.activation(out=sq[:], in_=src, func=AF.Square), reads=[skey], writes=['a_sq'])
            for q4 in range(4):
                P.op('pe', lambda e, q4=q4: e.matmul(ps[q4][0:64, :], lhsT=C.ones_b[0:64, 0:64], rhs=sq[:, q4 * 512:(q4 + 1) * 512], start=True, stop=True),
                     reads=['a_sq', 'g_ones_b'], writes=[PK[q4]])
            for q4 in range(4):
                P.op('act', lambda e, q4=q4: e.activation(out=rstd[:, q4 * 512:(q4 + 1) * 512], in_=ps[q4][0:64, :], func=AF.Ln, scale=1.0 / 64, bias=C.eps_col[0:64, 0:1]),
                     reads=[PK[q4]], writes=['a_rstd'], acc=True)
            P.op('act', lambda e: e.activation(out=rstd[:], in_=rstd[:], func=AF.Exp, scale=-0.5), reads=['a_rstd'], writes=['a_rstd'])
            P.op('dve', lambda e: e.scalar_tensor_tensor(out=xn[:], in0=src, scalar=wcol[:, 0:1], in1=rstd[:], op0=ALU.mult, op1=ALU.mult),
                 reads=[skey, wkey, 'a_rstd'], writes=['a_xn'])
            for q4 in range(4):
                P.op('pe', lambda e, q4=q4: e.matmul(ps[4 + q4][0:64, :], lhsT=rtb[:], rhs=xn[:, q4 * 512:(q4 + 1) * 512], start=True, stop=True),
                     reads=['a_xn', 'a_rtb'], writes=[PK[4 + q4]])
            P.op('pool', lambda e: e.tensor_tensor(out=t1[:], in0=xn[:], in1=cosT[:], op=ALU.mult), reads=['a_xn', 'a_tab'], writes=['a_t1'])
            for q4 in range(4):
                P.op('dve', lambda e, q4=q4: e.tensor_tensor(out=t2[:, q4 * 512:(q4 + 1) * 512], in0=ps[4 + q4][0:64, :], in1=sinT[:, q4 * 512:(q4 + 1) * 512], op=ALU.mult),
                     reads=[PK[4 + q4], 'a_tab'], writes=['a_t2'], acc=True)
            P.op('pool', lambda e: e.tensor_tensor(out=dst, in0=t1[:], in1=t2[:], op=ALU.add), reads=['a_t1', 'a_t2'], writes=[dkey], acc=True)

        for s in range(NSEQ):
            P.dma('sp', posi[:], C.pos[s:s + 1, :].broadcast_to([64, T]), writes=['a_posi'])
            P.op('dve', lambda e: e.tensor_copy(out=ang[:], in_=posi[:]), reads=['a_posi'], writes=['a_ang'])
            P.op('dve', lambda e: e.tensor_scalar(out=ang[:], in0=ang[:], scalar1=invf[:, 0:1], scalar2=None, op0=ALU.mult), reads=['a_ang', 'a_invf'], writes=['a_ang'])
            sincos(sinT, 0.0)
            sincos(cosT, 1.5707963267948966)
            for g in range(4):
                c0 = s * T
                P.dma('sp', kT[:], C.pT[K0 + g * 64:K0 + (g + 1) * 64, c0:c0 + T], reads=['pT'], writes=['a_kT'])
                P.dma('sp', qT[:], C.pT[Q0 + 3 * g * 64:Q0 + 3 * (g + 1) * 64, c0:c0 + T].rearrange("(j d) t -> d j t", d=64), reads=['pT'], writes=['a_qT'])
                P.dma('sp', zaT[:], C.pT[ZA0 + 3 * g * 64:ZA0 + 3 * (g + 1) * 64, c0:c0 + T].rearrange("(j d) t -> d j t", d=64), reads=['pT'], writes=['a_zaT'])
                P.dma('sp', vt[:], C.va[c0:c0 + T, g * 64:(g + 1) * 64].rearrange("(n p) d -> p n d", p=128), reads=['va'], writes=['a_vt'])
                P.op('act', lambda e: e.activation(out=sza[:], in_=zaT[:], func=AF.Silu), reads=['a_zaT'], writes=['a_sza'])
                normrope(kT[:], kw, 'a_kw', kr[:], 'a_kT', 'a_kr')
                for j in range(3):
                    normrope(qT[:, j, :], qw, 'a_qw', qr[:, j, :], 'a_qT', 'a_qr')
                for n in range(16):
                    par = n % 2
                    pb, cb2, ob, db = PK[2 * par], PK[2 * par + 1], PK[4 + par], PK[6 + par]
                    ipb, icb, iob, idb = 2 * par, 2 * par + 1, 4 + par, 6 + par
                    qs = qr[:, :, n * 128:(n + 1) * 128]
                    P.op('pe', lambda e, n=n, icb=icb, qs=qs: e.matmul(ps[icb][:, 0:384], lhsT=kr[:, n * 128:(n + 1) * 128], rhs=qs, start=True, stop=True),
                         reads=['a_kr', 'a_qr'], writes=[cb2])
                    if n > 0:
                        P.op('pe', lambda e, n=n, ipb=ipb, qs=qs: e.matmul(ps[ipb][:, 0:384], lhsT=kr[:, (n - 1) * 128:n * 128], rhs=qs, start=True, stop=True),
                             reads=['a_kr', 'a_qr'], writes=[pb])
                    P.op('act', lambda e, par=par, icb=icb: e.activation(out=pe_sb[par][:, 1, :], in_=ps[icb][:, 0:384], func=AF.Exp, scale=0.125),
                         reads=[cb2], writes=['a_pe%d' % par], acc=True)
                    if n > 0:
                        P.op('act', lambda e, par=par, ipb=ipb: e.activation(out=pe_sb[par][:, 0, :], in_=ps[ipb][:, 0:384], func=AF.Exp, scale=0.125),
                             reads=[pb], writes=['a_pe%d' % par], acc=True)
                    lo = 0 if n > 0 else 1
                    meng = 'dve' if n % 2 == 0 else 'pool'
                    P.op(meng, lambda e, par=par, lo=lo: e.tensor_tensor(
                        out=pm_sb[par][:, lo:2, :].rearrange("p a (h q) -> p a h q", h=3),
                        in0=pe_sb[par][:, lo:2, :].rearrange("p a (h q) -> p a h q", h=3),
                        in1=mkb[:, lo:2, :].unsqueeze(2).broadcast_to([128, 2 - lo, 3, 128]), op=ALU.mult),
                        reads=['a_pe%d' % par, 'a_mkb'], writes=['a_pm%d' % par])
                    if n > 0:
                        P.op('pe', lambda e, n=n, par=par, iob=iob: e.matmul(ps[iob][0:64, 0:384], lhsT=vt[:, n - 1, :], rhs=pm_sb[par][:, 0, :], start=True, stop=False),
                             reads=['a_vt', 'a_pm%d' % par], writes=[ob])
                    P.op('pe', lambda e, n=n, par=par, iob=iob: e.matmul(ps[iob][0:64, 0:384], lhsT=vt[:, n, :], rhs=pm_sb[par][:, 1, :], start=(n == 0), stop=True),
                         reads=['a_vt', 'a_pm%d' % par], writes=[ob])
                    if n > 0:
                        P.op('pe', lambda e, par=par, idb=idb: e.matmul(ps[idb][0:64, 0:384], lhsT=C.ones_b[:, 0:64], rhs=pm_sb[par][:, 0, :], start=True, stop=False),
                             reads=['g_ones_b', 'a_pm%d' % par], writes=[db])
                    P.op('pe', lambda e, n=n, par=par, idb=idb: e.matmul(ps[idb][0:64, 0:384], lhsT=C.ones_b[:, 0:64], rhs=pm_sb[par][:, 1, :], start=(n == 0), stop=False),
                         reads=['g_ones_b', 'a_pm%d' % par], writes=[db])
                    P.op('pe', lambda e, g=g, idb=idb: e.matmul(ps[idb][0:64, 0:384], lhsT=C.ones_b[0:1, 0:64], rhs=snkb[0:1, 3 * g:3 * g + 3, :], start=False, stop=True),
                         reads=['g_ones_b', 'a_snkb'], writes=[db])
                    P.op('act', lambda e, idb=idb: e.activation(out=rden[:], in_=ps[idb][0:64, 0:384], func=AF.Ln), reads=[db], writes=['a_rden'])
                    P.op('act', lambda e: e.activation(out=rden[:], in_=rden[:], func=AF.Exp, scale=-1.0), reads=['a_rden'], writes=['a_rden'])
                    P.op('dve', lambda e, iob=iob: e.tensor_tensor(out=o_sb[:], in0=ps[iob][0:64, 0:384], in1=rden[:], op=ALU.mult), reads=[ob, 'a_rden'], writes=['a_osb'])
                    P.op('pool', lambda e, n=n: e.tensor_tensor(out=ya[:, :, n * 128:(n + 1) * 128], in0=o_sb[:].rearrange("p (h q) -> p h q", h=3),
                                                                in1=sza[:, :, n * 128:(n + 1) * 128], op=ALU.mult),
                         reads=['a_osb', 'a_sza'], writes=['a_ya'], acc=True)
                P.dma('sp', C.yT[YA0 + 3 * g * 64:YA0 + 3 * (g + 1) * 64, c0:c0 + T].rearrange("(j d) t -> d j t", d=64), ya[:], reads=['a_ya'], writes=['yT'], acc=True)


def stage_gdn(P, nc, C, l):
    NH = 6
    with ExitStack() as st:
        SB = lambda n, s, d: st.enter_context(nc.sbuf_tensor("sc_" + str(l) + n, s, d))
        PS = lambda n, s, d: st.enter_context(nc.psum_tensor("cp_" + str(l) + n, s, d))
        cwc = SB("cwc", [128, 18, 4], F32)
        dgc = SB("dgc", [128, 72, 128], BF16)
        onw = SB("onw", [128, 1], F32)
        alog = SB("alog", [6, 1], F32)
        dtb = SB("dtb", [6, 1], F32)
        lmf = SB("lmf", [128, 7, 128], F32)
        lmb = SB("lmb", [128, 7, 128], BF16)
        ngm = SB("ngm", [128, 128], F32)
        sel = SB("sel", [6, 6, 128], F32)
        onesf = SB("onesf", [6, 128], F32)
        beta1 = SB("beta", [6, T], F32)
        gg1 = SB("gg", [6, T], F32)
        gc1 = SB("gc", [6, T], F32)
        egl1 = SB("egl", [6, 16], F32)
        egd1 = SB("egd", [6, 16, 6], F32)
        S32_1 = SB("S32", [128, NH, 128], F32)
        Sb_1 = SB("Sb", [128, NH, 128], BF16)
        braw = araw = beta = gg = gc = edl = egl = egd = S32 = Sb = None
        yc = [SB("yc%d" % i, [128, NH, 512], BF16) for i in range(2)]
        zc = [SB("zc%d" % i, [128, NH, 512], BF16) for i in range(2)]
        szc = zc
        def two(n, shp, d):
            return [SB("%s%d" % (n, i), shp, d) for i in range(2)]
        xin = two("xin", [128, 18, 131], BF16)
        qkv0_1 = SB("qkv0", [128, 18, 128], F32)
        qkv0 = [qkv0_1, qkv0_1]
        sqt = two("sqt", [128, NH, 128], BF16)
        rn = two("rn", [128, NH, 128], F32)
        KT = two("KT", [128, NH, 128], BF16)
        QT = two("QT", [128, NH, 128], BF16)
        VT = two("VT", [128, NH, 128], BF16)
        KgT = two("KgT", [128, NH, 128], BF16)
        QgT = two("QgT", [128, NH, 128], BF16)
        egcb = two("egcb", [128, NH, 128], F32)
        tmpa = two("tmpa", [128, NH, 128], F32)
        LTb = two("LTb", [128, NH, 128], BF16)
        ET = two("ET", [128, NH, 128], F32)
        cols = two("cols", [128, 24], F32)
        intraT = two("intraT", [128, NH, 128], BF16)
        ktil = two("ktil", [128, NH, 128], BF16)
        BsT = two("BsT", [128, NH, 128], BF16)
        Zsb = two("Zsb", [128, NH, 128], BF16)
        XmA = two("XmA", [128, NH, 128], BF16)
        XmB = two("XmB", [128, NH, 128], BF16)
        XTA = two("XTA", [128, NH, 128], BF16)
        XTB = two("XTB", [128, NH, 128], BF16)
        rsb = two("rsb", [128, NH, 128], BF16)
        vnew = two("vnew", [128, NH, 128], BF16)
        o32 = egcb
        ro = rn
        NSLOT = 4
        slots = [PS("s%d" % i, [128, 8, 128], F32) for i in range(NSLOT)]
        SK = ['c_slot%d' % i for i in range(NSLOT)]
        P.excl.update(SK)
        slot_ctr = [0]

        def nslot():
            i = slot_ctr[0] % NSLOT
            slot_ctr[0] += 1
            return slots[i], SK[i]

        def bc(colap):
            return colap.unsqueeze(2).broadcast_to([128, NH, 128])

        P.dma('sp', cwc[:], C.c_cw[l], writes=['c_cwc'])
        P.dma('sp', onw[:], C.c_onw[l], writes=['c_onw'])
        P.dma('sp', alog[:], C.c_alog[l], writes=['c_alog'])
        P.dma('sp', dtb[:], C.c_dtb[l], writes=['c_dtb'])
        P.dma('sp', lmf[:], C.c_lm, writes=['c_lmf'])
        P.dma('sp', ngm[:], C.c_ngm, writes=['c_ngm'])
        P.dma('sp', sel[:], C.c_sel, writes=['c_sel'])
        P.op('dve', lambda e: e.tensor_copy(out=lmb[:], in_=lmf[:]), reads=['c_lmf'], writes=['c_lmb'])
        P.op('pool', lambda e: e.memset(onesf[:], 1.0), writes=['c_onesf'])
        P.op('act', lambda e: e.activation(out=alog[:], in_=alog[:], func=AF.Exp), reads=['c_alog'], writes=['c_alog'])
        P.op('dve', lambda e: e.tensor_scalar(out=alog[:], in0=alog[:], scalar1=-1.0, scalar2=None, op0=ALU.mult), reads=['c_alog'], writes=['c_alog'])
        for cc in range(18):
            for k in range(4):
                eng = 'pool' if (cc * 4 + k) % 3 == 2 else 'dve'
                P.op(eng, lambda e, cc=cc, k=k: e.tensor_scalar(out=dgc[:, cc * 4 + k, :], in0=C.ident_f[:], scalar1=cwc[:, cc, k:k + 1], scalar2=None, op0=ALU.mult),
                     reads=['c_cwc', 'g_ident_f'], writes=['c_dgc'], acc=True)

        unit = 0
        for s in range(NSEQ):
            c0 = s * T
            beta, gg, gc, edl, egl, egd, S32, Sb = beta1, gg1, gc1, gg1, egl1, egd1, S32_1, Sb_1
            P.dma('sp', beta[:], C.baT[0:6, c0:c0 + T], reads=['baT'], writes=['c_beta'])
            P.dma('sp', gg[:], C.baT[6:12, c0:c0 + T], reads=['baT'], writes=['c_gg'])
            P.op('act', lambda e: e.activation(out=beta[:], in_=beta[:], func=AF.Sigmoid), reads=['c_beta'], writes=['c_beta'])
            P.op('act', lambda e: e.activation(out=gg[:], in_=gg[:], func=AF.Exp, bias=dtb[:, 0:1]), reads=['c_gg', 'c_dtb'], writes=['c_gg'])
            P.op('act', lambda e: e.activation(out=gg[:], in_=gg[:], func=AF.Ln, bias=C.one_col[0:6, 0:1]), reads=['c_gg'], writes=['c_gg'])
            P.op('dve', lambda e: e.tensor_scalar(out=gg[:], in0=gg[:], scalar1=alog[:, 0:1], scalar2=None, op0=ALU.mult), reads=['c_gg', 'c_alog'], writes=['c_gg'])
            for c in range(16):
                P.op('dve', lambda e, c=c: e.tensor_tensor_scan(out=gc[:, c * 128:(c + 1) * 128], data0=onesf[:], data1=gg[:, c * 128:(c + 1) * 128], initial=0.0, op0=ALU.mult, op1=ALU.add),
                     reads=['c_gg', 'c_onesf'], writes=['c_gc'], acc=True)
            gc3 = gc[:].rearrange("p (c t) -> p c t", t=128)
            P.op('dve', lambda e, gc3=gc3: e.tensor_tensor(out=edl[:].rearrange("p (c t) -> p c t", t=128), in0=gc3[:, :, 127:128].broadcast_to([6, 16, 128]), in1=gc3, op=ALU.subtract),
                 reads=['c_gc', 'c_gg'], writes=['c_gg'])
            P.op('act', lambda e: e.activation(out=edl[:], in_=edl[:], func=AF.Exp), reads=['c_gg'], writes=['c_gg'])
            P.op('act', lambda e, gc3=gc3: e.activation(out=egl[:].unsqueeze(2), in_=gc3[:, :, 127:128], func=AF.Exp), reads=['c_gc'], writes=['c_egl'])
            P.op('dve', lambda e: e.tensor_tensor(out=egd[:], in0=C.ident_f[0:6, 0:6].unsqueeze(1).broadcast_to([6, 16, 6]), in1=egl[:].unsqueeze(2).broadcast_to([6, 16, 6]), op=ALU.mult),
                 reads=['c_egl', 'g_ident_f'], writes=['c_egd'])
            P.op('pool', lambda e: e.memset(S32[:], 0.0), writes=['c_S32'])
            P.op('pool', lambda e: e.memset(Sb[:], 0.0), writes=['c_Sb'])
            def pre(c, u):
                U = lambda n, u=u: 'c_%s%d' % (n, u)
                tok0 = s * T + c * 128
                y2 = (c // 4) % 2
                if c % 4 == 0:
                    t512 = s * T + (c // 4) * 512
                    P.dma('sp', zc[y2][:], C.pT[ZC0:ZC0 + 768, t512:t512 + 512].rearrange("(h p) t -> p h t", p=128), reads=['pT'], writes=['c_zc%d' % y2])
                    P.op('act', lambda e, y2=y2: e.activation(out=zc[y2][:], in_=zc[y2][:], func=AF.Silu), reads=['c_zc%d' % y2], writes=['c_zc%d' % y2])
                if c == 0:
                    P.op('pool', lambda e, u=u: e.memset(xin[u][:, :, 0:3], 0.0), writes=[U('xin')])
                    P.dma('sp', xin[u][:, :, 3:131], C.pT[CQ0:CQ0 + 2304, tok0:tok0 + 128].rearrange("(cc p) t -> p cc t", p=128), reads=['pT'], writes=[U('xin')])
                else:
                    P.dma('sp', xin[u][:], C.pT[CQ0:CQ0 + 2304, tok0 - 3:tok0 + 128].rearrange("(cc p) t -> p cc t", p=128), reads=['pT'], writes=[U('xin')])
                for grp in range(3):
                    sl, sk = nslot()
                    for hh in range(6):
                        cc = grp * 6 + hh
                        for k in range(4):
                            P.op('pe', lambda e, u=u, cc=cc, k=k, hh=hh, sl=sl: e.matmul(sl[:, hh, :], lhsT=dgc[:, cc * 4 + k, :], rhs=xin[u][:, cc, k:k + 128], start=(k == 0), stop=(k == 3)),
                                 reads=['c_dgc', U('xin')], writes=[sk])
                    P.op('act', lambda e, grp=grp, sl=sl: e.activation(out=qkv0_1[:, grp * 6:(grp + 1) * 6, :], in_=sl[:, 0:6, :], func=AF.Silu),
                         reads=[sk], writes=['c_qkv0_%d' % grp])
                    yield
                for grp, dst, scl in ((0, QT, 128.0 ** -0.5), (1, KT, 1.0)):
                    src = qkv0_1[:, grp * 6:(grp + 1) * 6, :]
                    P.op('act', lambda e, u=u, src=src: e.activation(out=sqt[u][:], in_=src, func=AF.Square), reads=['c_qkv0_%d' % grp], writes=[U('sqt')])
                    sl, sk = nslot()
                    P.op('pe', lambda e, u=u, sl=sl: e.matmul(sl[:, 0:4, :], lhsT=C.ones_b[:], rhs=sqt[u][:, 0:4, :], start=True, stop=True),
                         reads=['g_ones_b', U('sqt')], writes=[sk])
                    P.op('pe', lambda e, u=u, sl=sl: e.matmul(sl[:, 4:6, :], lhsT=C.ones_b[:], rhs=sqt[u][:, 4:6, :], start=True, stop=True),
                         reads=['g_ones_b', U('sqt')], writes=[sk])
                    P.op('act', lambda e, u=u, sl=sl: e.activation(out=rn[u][:], in_=sl[:, 0:6, :], func=AF.Ln, bias=C.eps_col[:, 0:1]), reads=[sk], writes=[U('rn')])
                    P.op('act', lambda e, u=u: e.activation(out=rn[u][:], in_=rn[u][:], func=AF.Exp, scale=-0.5), reads=[U('rn')], writes=[U('rn')])
                    P.op('dve', lambda e, u=u, src=src, dst=dst, scl=scl: e.scalar_tensor_tensor(out=dst[u][:], in0=src, scalar=scl, in1=rn[u][:], op0=ALU.mult, op1=ALU.mult),
                         reads=['c_qkv0_%d' % grp, U('rn')], writes=[U('QT' if grp == 0 else 'KT')])
                    yield
                P.op('act', lambda e, u=u: e.activation(out=VT[u][:], in_=qkv0_1[:, 12:18, :], func=AF.Copy), reads=['c_qkv0_2'], writes=[U('VT')])
                sl, sk = nslot()
                tsl = slice(c * 128, (c + 1) * 128)
                P.op('pe', lambda e, sl=sl, tsl=tsl: e.matmul(sl[:, 0, 0:6], lhsT=beta[:, tsl], rhs=C.ident_f[0:6, 0:6], start=True, stop=True),
                     reads=['c_beta', 'g_ident_f'], writes=[sk])
                P.op('pe', lambda e, sl=sl, tsl=tsl: e.matmul(sl[:, 0, 6:12], lhsT=gc[:, tsl], rhs=C.ident_f[0:6, 0:6], start=True, stop=True),
                     reads=['c_gc', 'g_ident_f'], writes=[sk])
                P.op('pe', lambda e, sl=sl, tsl=tsl: e.matmul(sl[:, 0, 12:18], lhsT=edl[:, tsl], rhs=C.ident_f[0:6, 0:6], start=True, stop=True),
                     reads=['c_gg', 'g_ident_f'], writes=[sk])
                P.op('pe', lambda e, sl=sl, c=c: e.matmul(sl[:, 0, 18:24], lhsT=onesf[:], rhs=egd[:, c, :], start=True, stop=True),
                     reads=['c_egd', 'c_onesf'], writes=[sk])
                P.op('dve', lambda e, u=u, sl=sl: e.tensor_copy(out=cols[u][:], in_=sl[:, 0, 0:24]), reads=[sk], writes=[U('cols')])
                yield
                sl, sk = nslot()
                for hh in range(6):
                    P.op('pe', lambda e, hh=hh, sl=sl, tsl=tsl: e.matmul(sl[:, hh, :], lhsT=sel[:, hh, :], rhs=gc[:, tsl], start=True, stop=True),
                         reads=['c_sel', 'c_gc'], writes=[sk])
                P.op('act', lambda e, u=u, sl=sl: e.activation(out=egcb[u][:], in_=sl[:, 0:6, :], func=AF.Exp), reads=[sk], writes=[U('egcb')])
                for hh in range(6):
                    P.op('dve', lambda e, u=u, hh=hh, sl=sl: e.scalar_tensor_tensor(out=tmpa[u][:, hh, :], in0=sl[:, hh, :], scalar=cols[u][:, 6 + hh:7 + hh], in1=ngm[:],
                                                                                    op0=ALU.subtract, op1=ALU.add),
                         reads=[sk, U('cols'), 'c_ngm'], writes=[U('tmpa')], acc=True)
                P.op('act', lambda e, u=u: e.activation(out=ET[u][:], in_=tmpa[u][:], func=AF.Exp), reads=[U('tmpa')], writes=[U('ET')])
                yield
                P.op('dve', lambda e, u=u: e.scalar_tensor_tensor(out=KgT[u][:], in0=KT[u][:], scalar=-1.0, in1=egcb[u][:], op0=ALU.mult, op1=ALU.mult),
                     reads=[U('KT'), U('egcb')], writes=[U('KgT')])
                P.op('pool', lambda e, u=u: e.tensor_tensor(out=QgT[u][:], in0=QT[u][:], in1=egcb[u][:], op=ALU.mult), reads=[U('QT'), U('egcb')], writes=[U('QgT')])
                sl, sk = nslot()
                for hh in range(6):
                    P.op('pe', lambda e, u=u, hh=hh, sl=sl: e.matmul(sl[:, hh, :], lhsT=KT[u][:, hh, :], rhs=KT[u][:, hh, :], start=True, stop=True),
                         reads=[U('KT')], writes=[sk])
                for hh in range(6):
                    P.op('dve', lambda e, u=u, hh=hh, sl=sl: e.scalar_tensor_tensor(out=LTb[u][:, hh, :], in0=sl[:, hh, :], scalar=cols[u][:, hh:hh + 1], in1=ET[u][:, hh, :],
                                                                                    op0=ALU.mult, op1=ALU.mult),
                         reads=[sk, U('cols'), U('ET')], writes=[U('LTb')], acc=True)
                yield
                sl, sk = nslot()
                for hh in range(6):
                    P.op('pe', lambda e, u=u, hh=hh, sl=sl: e.matmul(sl[:, hh, :], lhsT=KT[u][:, hh, :], rhs=QT[u][:, hh, :], start=True, stop=True),
                         reads=[U('KT'), U('QT')], writes=[sk])
                P.op('dve', lambda e, u=u, sl=sl: e.tensor_tensor(out=intraT[u][:], in0=sl[:, 0:6, :], in1=ET[u][:], op=ALU.mult), reads=[sk, U('ET')], writes=[U('intraT')])
                yield
                sl, sk = nslot()
                for hh in range(6):
                    P.op('pe', lambda e, u=u, hh=hh, sl=sl: e.matmul(sl[:, hh, :], lhsT=KT[u][:, hh, :], rhs=C.ident_b[:], start=True, stop=True),
                         reads=[U('KT'), 'g_ident_b'], writes=[sk])
                P.op('dve', lambda e, u=u, sl=sl: e.tensor_tensor(out=ktil[u][:], in0=sl[:, 0:6, :], in1=bc(cols[u][:, 12:18]), op=ALU.mult), reads=[sk, U('cols')], writes=[U('ktil')])
                yield
                identb3 = C.ident_b[:].unsqueeze(1).broadcast_to([128, NH, 128])

                def mask_level(lv):
                    eng = 'dve' if lv % 2 == 0 else 'pool'
                    P.op(eng, lambda e, u=u, lv=lv: e.tensor_tensor(out=BsT[u][:], in0=LTb[u][:], in1=lmb[:, lv:lv + 1, :].broadcast_to([128, NH, 128]), op=ALU.mult),
                         reads=[U('LTb'), 'c_lmb'], writes=[U('BsT')])
                mask_level(0)
                sl, sk = nslot()
                for hh in range(6):
                    P.op('pe', lambda e, u=u, hh=hh, sl=sl: e.matmul(sl[:, hh, :], lhsT=BsT[u][:, hh, :], rhs=C.ident_b[:], start=True, stop=True),
                         reads=[U('BsT'), 'g_ident_b'], writes=[sk])
                P.op('dve', lambda e, u=u, sl=sl, identb3=identb3: e.tensor_tensor(out=XmA[u][:], in0=identb3, in1=sl[:, 0:6, :], op=ALU.subtract),
                     reads=[sk, 'g_ident_b'], writes=[U('XmA')])
                P.op('pool', lambda e, u=u, identb3=identb3: e.tensor_tensor(out=XTA[u][:], in0=identb3, in1=BsT[u][:], op=ALU.subtract),
                     reads=[U('BsT'), 'g_ident_b'], writes=[U('XTA')])
                yield
                Xm, XT_, XmN, XTN = XmA, XTA, XmB, XTB
                kXm, kXT, kXmN, kXTN = 'XmA', 'XTA', 'XmB', 'XTB'
                for lv in range(1, 7):
                    last = (lv == 6)
                    mask_level(lv)
                    sl, sk = nslot()
                    for hh in range(6):
                        P.op('pe', lambda e, u=u, hh=hh, sl=sl, Xm=Xm: e.matmul(sl[:, hh, :], lhsT=BsT[u][:, hh, :], rhs=Xm[u][:, hh, :], start=True, stop=True),
                             reads=[U('BsT'), U(kXm)], writes=[sk])
                    P.op('act', lambda e, u=u, sl=sl: e.activation(out=Zsb[u][:], in_=sl[:, 0:6, :], func=AF.Copy), reads=[sk], writes=[U('Zsb')])
                    yield
                    if not last:
                        sl, sk = nslot()
                        for hh in range(6):
                            P.op('pe', lambda e, u=u, hh=hh, sl=sl, XT_=XT_: e.matmul(sl[:, hh, :], lhsT=XT_[u][:, hh, :], rhs=Zsb[u][:, hh, :], start=True, stop=True),
                                 reads=[U('Zsb'), U(kXT)], writes=[sk])
                        P.op('dve', lambda e, u=u, sl=sl, Xm=Xm, XmN=XmN: e.tensor_tensor(out=XmN[u][:], in0=Xm[u][:], in1=sl[:, 0:6, :], op=ALU.subtract),
                             reads=[sk, U(kXm)], writes=[U(kXmN)])
                    sl, sk = nslot()
                    for hh in range(6):
                        P.op('pe', lambda e, u=u, hh=hh, sl=sl, XT_=XT_: e.matmul(sl[:, hh, :], lhsT=Zsb[u][:, hh, :], rhs=XT_[u][:, hh, :], start=True, stop=True),
                             reads=[U('Zsb'), U(kXT)], writes=[sk])
                    P.op('dve', lambda e, u=u, sl=sl, XT_=XT_, XTN=XTN: e.tensor_tensor(out=XTN[u][:], in0=XT_[u][:], in1=sl[:, 0:6, :], op=ALU.subtract),
                         reads=[sk, U(kXT)], writes=[U(kXTN)])
                    yield
                    Xm, XT_, XmN, XTN = XmN, XTN, Xm, XT_
                    kXm, kXT, kXmN, kXTN = kXmN, kXTN, kXm, kXT
                assert kXT == 'XTA'

            def scan(c, u):
                U = lambda n, u=u: 'c_%s%d' % (n, u)
                c4 = c % 4
                y2 = (c // 4) % 2
                ykey = 'c_yc%d' % y2
                TT, kTT = XTA, 'XTA'
                sl, sk = nslot()
                for hh in range(6):
                    P.op('pe', lambda e, u=u, hh=hh, sl=sl: e.matmul(sl[:, hh, :], lhsT=VT[u][:, hh, :], rhs=C.ident_b[:], start=True, stop=False),
                         reads=[U('VT'), 'g_ident_b'], writes=[sk])
                    P.op('pe', lambda e, u=u, hh=hh, sl=sl: e.matmul(sl[:, hh, :], lhsT=KgT[u][:, hh, :], rhs=Sb[:, hh, :], start=False, stop=True),
                         reads=[U('KgT'), 'c_Sb'], writes=[sk])
                P.op('act', lambda e, u=u, sl=sl: e.activation(out=rsb[u][:], in_=sl[:, 0:6, :], func=AF.Copy), reads=[sk], writes=[U('rsb')])
                yield
                sl, sk = nslot()
                for hh in range(6):
                    P.op('pe', lambda e, u=u, hh=hh, sl=sl, TT=TT: e.matmul(sl[:, hh, :], lhsT=TT[u][:, hh, :], rhs=rsb[u][:, hh, :], start=True, stop=True),
                         reads=[U(kTT), U('rsb')], writes=[sk])
                P.op('dve', lambda e, u=u, sl=sl: e.tensor_tensor(out=vnew[u][:], in0=sl[:, 0:6, :], in1=bc(cols[u][:, 0:6]), op=ALU.mult), reads=[sk, U('cols')], writes=[U('vnew')])
                yield
                sl, sk = nslot()
                for hh in range(6):
                    P.op('pe', lambda e, u=u, hh=hh, sl=sl: e.matmul(sl[:, hh, :], lhsT=Sb[:, hh, :], rhs=QgT[u][:, hh, :], start=True, stop=False),
                         reads=[U('QgT'), 'c_Sb'], writes=[sk])
                    P.op('pe', lambda e, u=u, hh=hh, sl=sl: e.matmul(sl[:, hh, :], lhsT=vnew[u][:, hh, :], rhs=intraT[u][:, hh, :], start=False, stop=True),
                         reads=[U('vnew'), U('intraT')], writes=[sk])
                P.op('act', lambda e, u=u, sl=sl: e.activation(out=o32[u][:], in_=sl[:, 0:6, :], func=AF.Copy), reads=[sk], writes=[U('egcb')])
                sl, sk = nslot()
                for hh in range(6):
                    P.op('pe', lambda e, u=u, hh=hh, sl=sl: e.matmul(sl[:, hh, :], lhsT=ktil[u][:, hh, :], rhs=vnew[u][:, hh, :], start=True, stop=True),
                         reads=[U('ktil'), U('vnew')], writes=[sk])
                P.op('pool', lambda e, u=u: e.tensor_tensor(out=S32[:], in0=S32[:], in1=bc(cols[u][:, 18:24]), op=ALU.mult), reads=['c_S32', U('cols')], writes=['c_S32'])
                P.op('dve', lambda e, sl=sl: e.tensor_tensor(out=S32[:], in0=S32[:], in1=sl[:, 0:6, :], op=ALU.add), reads=[sk, 'c_S32'], writes=['c_S32'])
                P.op('act', lambda e: e.activation(out=Sb[:], in_=S32[:], func=AF.Copy), reads=['c_S32'], writes=['c_Sb'])
                yield
                P.op('act', lambda e, u=u: e.activation(out=sqt[u][:], in_=o32[u][:], func=AF.Square), reads=[U('egcb')], writes=[U('sqt')])
                sl, sk = nslot()
                P.op('pe', lambda e, u=u, sl=sl: e.matmul(sl[:, 0:4, :], lhsT=C.ones_b[:], rhs=sqt[u][:, 0:4, :], start=True, stop=True),
                     reads=['g_ones_b', U('sqt')], writes=[sk])
                P.op('pe', lambda e, u=u, sl=sl: e.matmul(sl[:, 4:6, :], lhsT=C.ones_b[:], rhs=sqt[u][:, 4:6, :], start=True, stop=True),
                     reads=['g_ones_b', U('sqt')], writes=[sk])
                P.op('act', lambda e, u=u, sl=sl: e.activation(out=ro[u][:], in_=sl[:, 0:6, :], func=AF.Ln, scale=1.0 / 128, bias=C.eps_col[:, 0:1]), reads=[sk], writes=[U('rn')])
                yield
                P.op('act', lambda e, u=u: e.activation(out=ro[u][:], in_=ro[u][:], func=AF.Exp, scale=-0.5), reads=[U('rn')], writes=[U('rn')])
                P.op('dve', lambda e, u=u: e.scalar_tensor_tensor(out=o32[u][:], in0=o32[u][:], scalar=onw[:, 0:1], in1=ro[u][:], op0=ALU.mult, op1=ALU.mult),
                     reads=[U('egcb'), U('rn'), 'c_onw'], writes=[U('egcb')])
                P.op('pool', lambda e, u=u, c4=c4, y2=y2: e.tensor_tensor(out=yc[y2][:, :, c4 * 128:(c4 + 1) * 128], in0=o32[u][:], in1=zc[y2][:, :, c4 * 128:(c4 + 1) * 128], op=ALU.mult),
                     reads=[U('egcb'), 'c_zc%d' % y2], writes=[ykey], acc=True)
                if c4 == 3:
                    t512 = s * T + (c // 4) * 512
                    P.dma('sp', C.yT[YC0:YC0 + 768, t512:t512 + 512].rearrange("(h p) t -> p h t", p=128), yc[y2][:], reads=[ykey], writes=['yT'], acc=True)
                yield

            def merge(ga, gb, ratio):
                da = db = False
                while not (da and db):
                    if not da:
                        try:
                            next(ga)
                        except StopIteration:
                            da = True
                    for _ in range(ratio):
                        if db:
                            break
                        try:
                            next(gb)
                        except StopIteration:
                            db = True

            for _ in pre(0, 0):
                pass
            for c in range(16):
                nxt = pre(c + 1, (c + 1) % 2) if c + 1 < 16 else iter(())
                merge(scan(c, c % 2), nxt, 5)


def yc_sel(yc, s, y2):
    return yc[s]


def build_program(debug=None):
    nc = bass.Bass("TRN2", target_bir_lowering=False)
    C = Ctx()
    din = lambda n, s, d: nc.dram_tensor(n, s, d, kind="ExternalInput").ap()
    C.x = din("x", [NTOK, D], F32)
    C.pos = din("pos", [NSEQ, T], I32)
    C.norm_w = din("norm_w", [DEPTH, D], F32)
    C.w_fm = din("w_fm", [DEPTH, NCH, 128, 16, 128], F32)
    C.w_ba = din("w_ba", [DEPTH, 128, 16, 12], F32)
    C.w_va = din("w_va", [DEPTH, 128, 16, 256], F32)
    C.w_o = din("w_o", [DEPTH, 128, 16, D], F32)
    C.ident_f_d = din("ident_f", [128, 128], F32)
    C.a_invf = din("a_invf", [64, 1], F32)
    C.a_rt = din("a_rt", [64, 64], F32)
    C.a_mask = din("a_mask", [128, 2, 128], F32)
    C.a_qw = din("a_qw", [DEPTH, 64, 1], F32)
    C.a_kw = din("a_kw", [DEPTH, 64, 1], F32)
    C.a_snk = din("a_snk", [DEPTH, 1, 12], F32)
    C.c_cw = din("c_cw", [DEPTH, 128, 18, 4], F32)
    C.c_onw = din("c_onw", [DEPTH, 128, 1], F32)
    C.c_alog = din("c_alog", [DEPTH, 6, 1], F32)
    C.c_dtb = din("c_dtb", [DEPTH, 6, 1], F32)
    C.c_lm = din("c_lm", [128, 7, 128], F32)
    C.c_ngm = din("c_ngm", [128, 128], F32)
    C.c_rsm = din("c_rsm", [6, T], F32)
    C.c_sel = din("c_sel", [6, 6, 128], F32)
    C.b_cw = din("b_cw", [DEPTH, 128, 4, 31], F32)
    C.b_cb = din("b_cb", [DEPTH, 128, 4], F32)
    C.b_lnw = din("b_lnw", [DEPTH, 128, 4], F32)
    C.b_lnb = din("b_lnb", [DEPTH, 128, 4], F32)
    C.b_pwb = din("b_pwb", [DEPTH, 128, 4], F32)
    C.b_pw = din("b_pw", [DEPTH, 128, 4, 512], F32)
    C.out = nc.dram_tensor("out", [NTOK, D], F32, kind="ExternalOutput").ap()
    dbg_in, dbg_out = debug[1] if debug else ((), ())
    scr = lambda n, s, d: nc.dram_tensor(n, s, d, kind=("ExternalInput" if n in dbg_in else "ExternalOutput" if n in dbg_out else "Internal")).ap()
    stages = debug[0] if debug else 'ALL'
    C.pT = scr("pT", [NFM, NTOK], BF16)
    C.baT = scr("baT", [12, NTOK], F32)
    C.va = scr("va", [NTOK, 256], BF16)
    C.yT = scr("yT", [D, NTOK], BF16)
    C.x1 = scr("x1", [NTOK, D], F32)
    names = {id(C.x.tensor): 'x_in', id(C.x1.tensor): 'x1', id(C.out.tensor): 'out'}
    C.xres_key = lambda ap: names[id(ap.tensor)]

    with ExitStack() as st:
        P = Prog(nc, st)
        SB = lambda n, s, d: st.enter_context(nc.sbuf_tensor("g_" + n, s, d))
        C.ident_f = SB("ident_f", [128, 128], F32)
        C.ident_b = SB("ident_b", [128, 128], BF16)
        C.eps_col = SB("eps_col", [128, 1], F32)
        P.dma('sp', C.ident_f[:], C.ident_f_d, writes=['g_ident_f'])
        P.op('dve', lambda e: e.tensor_copy(out=C.ident_b[:], in_=C.ident_f[:]), reads=['g_ident_f'], writes=['g_ident_b'])
        P.op('dve', lambda e: e.memset(C.eps_col[:], EPS), writes=['g_eps'])
        C.one_col = SB("one_col", [128, 1], F32)
        P.op('dve', lambda e: e.memset(C.one_col[:], 1.0), writes=['g_one'])
        C.ones_b = SB("ones_b", [128, 128], BF16)
        P.op('dve', lambda e: e.memset(C.ones_b[:], 1.0), writes=['g_ones_b'])

        P.barrier()
        if stages == 'ALL':
            srcs = [C.x, C.x1]
            dsts = [C.x1, C.out]
            for l in range(DEPTH):
                stage_inproj(P, nc, C, l, srcs[l])
                P.barrier()
                stage_attn(P, nc, C, l)
                P.barrier()
                stage_conformer(P, nc, C, l)
                P.barrier()
                stage_gdn(P, nc, C, l)
                P.barrier()
                stage_outproj(P, nc, C, l, srcs[l], dsts[l])
                P.barrier()
        elif stages == 'P':
            stage_inproj(P, nc, C, 0, C.x)
        elif stages == 'A':
            stage_attn(P, nc, C, 0)
        elif stages == 'C':
            stage_gdn(P, nc, C, 0)
        elif stages == 'B':
            stage_conformer(P, nc, C, 0)
        elif stages == 'O':
            stage_outproj(P, nc, C, 0, C.x, C.out)
        P.finish()
        P.build()
    return nc


def host_inputs(inputs, debug=None):
    w_in = np.asarray(inputs["w_in"], dtype=np.float32)
    w_out = np.asarray(inputs["w_out"], dtype=np.float32)
    cols = np.concatenate([
        np.arange(0, 768), np.arange(768, 1024), np.arange(1280, 2048),
        np.arange(2048, 2560), np.arange(2560, 3072), np.arange(3072, 3584),
        np.arange(3584, 4352), np.arange(4352, 5120), np.arange(5120, 5888),
        np.arange(5900, 6668)])
    assert cols.size == NFM
    wfm = w_in[:, :, cols].reshape(DEPTH, 16, 128, NCH, 128).transpose(0, 3, 2, 1, 4)
    wba = w_in[:, :, 5888:5900].reshape(DEPTH, 16, 128, 12).transpose(0, 2, 1, 3)
    wva = w_in[:, :, 1024:1280].reshape(DEPTH, 16, 128, 256).transpose(0, 2, 1, 3)
    wo = w_out.reshape(DEPTH, 16, 128, D).transpose(0, 2, 1, 3)
    shared = {
        "norm_w": np.ascontiguousarray(inputs["norm_w"], dtype=np.float32),
        "w_fm": np.ascontiguousarray(wfm), "w_ba": np.ascontiguousarray(wba),
        "w_va": np.ascontiguousarray(wva), "w_o": np.ascontiguousarray(wo),
        "ident_f": np.eye(128, dtype=np.float32),
    }
    f32 = lambda k: np.asarray(inputs[k], dtype=np.float32)
    shared["c_cw"] = np.ascontiguousarray(f32("c_conv_w").transpose(0, 2, 1).reshape(DEPTH, 18, 128, 4).transpose(0, 2, 1, 3))
    shared["c_onw"] = np.ascontiguousarray(f32("c_onorm_w").reshape(DEPTH, 128, 1))
    shared["c_alog"] = np.ascontiguousarray(f32("c_a_log").reshape(DEPTH, 6, 1))
    shared["c_dtb"] = np.ascontiguousarray(f32("c_dt_bias").reshape(DEPTH, 6, 1))
    jj_ = np.arange(128)[:, None]
    ii_ = np.arange(128)[None, :]
    lm = np.zeros((128, 7, 128), np.float32)
    for lv in range(7):
        bsz = 1 << lv
        lm[:, lv, :] = ((ii_ // (2 * bsz) == jj_ // (2 * bsz)) & (jj_ % (2 * bsz) < bsz) & (ii_ % (2 * bsz) >= bsz)).astype(np.float32)
    shared["c_lm"] = lm
    shared["c_ngm"] = np.where(ii_ >= jj_, 0.0, -30000.0).astype(np.float32)
    rsm = np.ones((6, T), np.float32)
    rsm[:, 0::128] = 0.0
    shared["c_rsm"] = rsm
    selm = np.zeros((6, 6, 128), np.float32)
    for hh in range(6):
        selm[hh, hh, :] = 1.0
    shared["c_sel"] = selm
    invf = np.zeros((64, 1), np.float32)
    fr = (np.float32(500000.0) ** (-np.arange(0, 16, 2, dtype=np.float32) / np.float32(16))).astype(np.float32)
    invf[0:8, 0] = fr
    invf[8:16, 0] = fr
    rt = np.zeros((64, 64), np.float32)
    for mm in range(8):
        rt[mm + 8, mm] = -1.0
        rt[mm, mm + 8] = 1.0
    jj = np.arange(128)[:, None]
    ii = np.arange(128)[None, :]
    shared["a_invf"] = invf
    shared["a_rt"] = rt
    shared["a_mask"] = np.ascontiguousarray(np.stack([(jj > ii), (jj <= ii)], axis=1).astype(np.float32))
    shared["a_qw"] = np.ascontiguousarray(f32("q_norm_w").reshape(DEPTH, 64, 1))
    shared["a_kw"] = np.ascontiguousarray(f32("k_norm_w").reshape(DEPTH, 64, 1))
    shared["a_snk"] = np.ascontiguousarray(f32("sinks").reshape(DEPTH, 1, 12))
    col4 = lambda a: np.ascontiguousarray(a.reshape(DEPTH, 4, 128).transpose(0, 2, 1))
    shared["b_cw"] = np.ascontiguousarray(f32("b_conv_w").transpose(0, 2, 1).reshape(DEPTH, 4, 128, 31).transpose(0, 2, 1, 3))
    shared["b_cb"] = col4(f32("b_conv_b"))
    shared["b_lnw"] = col4(f32("b_ln_w"))
    shared["b_lnb"] = col4(f32("b_ln_b"))
    shared["b_pwb"] = col4(f32("b_pw_b"))
    shared["b_pw"] = np.ascontiguousarray(f32("b_pw_w").reshape(DEPTH, 4, 128, 512).transpose(0, 2, 1, 3))
    x = np.asarray(inputs["x"], dtype=np.float32)
    pos = np.asarray(inputs["positions"], dtype=np.int32)
    in_maps = []
    for c in range(NCORES):
        m = dict(shared)
        m["x"] = np.ascontiguousarray(x[c * NSEQ:(c + 1) * NSEQ].reshape(NTOK, D))
        m["pos"] = np.ascontiguousarray(pos[c * NSEQ:(c + 1) * NSEQ])
        in_maps.append(m)
    return in_maps


def kernel(**inputs):
    in_maps = host_inputs(inputs)
    nc = build_program()
    res = run_bass_kernel_spmd(nc, in_maps, core_ids=list(range(NCORES)))
    out = np.stack([r["out"].reshape(NSEQ, T, D) for r in res.results], axis=0)
    return out.reshape(NCORES * NSEQ, T, D).astype(np.float32)
```

```python
import numpy as np
from contextlib import ExitStack
import concourse.bass as bass
import concourse.mybir as mybir
from concourse.bass_utils import run_bass_kernel_spmd

F32 = mybir.dt.float32
BF16 = mybir.dt.bfloat16
I32 = mybir.dt.int32
AF = mybir.ActivationFunctionType
ALU = mybir.AluOpType

ENG = ['pe', 'dve', 'act', 'pool', 'sp']
EPOCH = 30000

NCORES = 8
DEPTH = 2
D = 2048
T = 2048
NSEQ = 2
NTOK = NSEQ * T
EPS = 1e-6
Q0, K0, ZA0, UBA0, UBG0, ZB0, CQ0, CK0, CV0, ZC0 = 0, 768, 1024, 1792, 2304, 2816, 3328, 4096, 4864, 5632
NFM = 6400
NCH = NFM // 128
YA0, YB0, YC0 = 0, 768, 1280


class Prog:
    def __init__(self, nc, stack):
        self.nc = nc
        self.stack = stack
        self.streams = {e: [] for e in ENG}
        self.count = {e: 0 for e in ENG}
        self.sems = {}
        self.seen = {e: {} for e in ENG}
        self.res = {}
        self.dmacnt = {}
        self.excl = set()

    def sem(self, key):
        if key not in self.sems:
            self.sems[key] = self.stack.enter_context(
                self.nc.semaphore("s%d" % len(self.sems)))
        return self.sems[key]

    def _deps(self, reads, writes, acc):
        deps = {}

        def add(d):
            for k, v in d.items():
                if deps.get(k, 0) < v:
                    deps[k] = v
        for r in reads:
            st = self.res.get(r)
            if st:
                add(st[0])
                if r in self.excl:
                    add(st[1])
        for w in writes:
            st = self.res.get(w)
            if st:
                add(st[1])
                if not acc:
                    add(st[0])
        return deps

    def _emit_waits(self, eng, deps):
        for k, v in deps.items():
            if self.seen[eng].get(k, 0) >= v:
                continue
            self.seen[eng][k] = v
            h = self.sem(k)
            self.streams[eng].append(lambda e, h=h, v=v: e.wait_ge(h, v))

    def _commit(self, reads, writes, acc, key, val):
        for r in reads:
            st = self.res.setdefault(r, [{}, {}])
            if st[1].get(key, 0) < val:
                st[1][key] = val
        for w in writes:
            st = self.res.setdefault(w, [{}, {}])
            if acc:
                if st[0].get(key, 0) < val:
                    st[0][key] = val
            else:
                st[0] = {key: val}
                st[1] = {}

    def op(self, eng, fn, reads=(), writes=(), acc=False):
        deps = self._deps(reads, writes, acc)
        if eng == 'pe':
            deps = {k: v for k, v in deps.items() if k[0] != 'pe'}
        n = self.count[eng]
        key = (eng, n // EPOCH)
        val = n % EPOCH + 1
        self._emit_waits(eng, deps)
        h = self.sem(key)
        self.streams[eng].append(lambda e, fn=fn, h=h: fn(e).then_inc(h, 1))
        self.count[eng] = n + 1
        self._commit(reads, writes, acc, key, val)

    def dma(self, q, out, in_, reads=(), writes=(), acc=False, semkey=None, **kw):
        assert len(writes) == 1
        deps = self._deps(reads, writes, acc)
        self._emit_waits(q, deps)
        key = ('dma', semkey if semkey is not None else (reads[0] if acc else writes[0]))
        cnt = self.dmacnt.get(key, 0) + 16
        self.dmacnt[key] = cnt
        h = self.sem(key)
        self.streams[q].append(
            lambda e, out=out, in_=in_, h=h, kw=kw: e.dma_start(out=out, in_=in_, **kw).then_inc(h, 16))
        self._commit(reads, writes, acc, key, cnt)

    def barrier(self):
        deps = {}
        for k, c in self.dmacnt.items():
            deps[k] = c
        for e in ENG:
            n = self.count[e]
            if n > 0:
                deps[(e, (n - 1) // EPOCH)] = (n - 1) % EPOCH + 1
        for e in ENG:
            self._emit_waits(e, dict(deps))

    def finish(self):
        deps = {}
        for k, c in self.dmacnt.items():
            deps[k] = c
        for e in ENG:
            n = self.count[e]
            if n > 0:
                deps[(e, (n - 1) // EPOCH)] = (n - 1) % EPOCH + 1
        self._emit_waits('sp', deps)

    def build(self):
        block = self.stack.enter_context(self.nc.Block())
        streams = self.streams

        @block.tensor
        def _(e):
            for f in streams['pe']:
                f(e)

        @block.vector
        def _(e):
            for f in streams['dve']:
                f(e)

        @block.scalar
        def _(e):
            for f in streams['act']:
                f(e)

        @block.gpsimd
        def _(e):
            for f in streams['pool']:
                f(e)

        @block.sync
        def _(e):
            for f in streams['sp']:
                f(e)


class Ctx:
    pass


def stage_inproj(P, nc, C, l, x_src):
    with ExitStack() as st:
        SB = lambda n, s, d: st.enter_context(nc.sbuf_tensor("p_" + str(l) + n, s, d))
        PS = lambda n, s, d: st.enter_context(nc.psum_tensor("pp_" + str(l) + n, s, d))
        hT = SB("hT", [128, 16, T], BF16)
        nw = SB("nw", [128, D], F32)
        xs = [SB("xs%d" % i, [128, D], F32) for i in range(2)]
        hb = [SB("hb%d" % i, [128, D], BF16) for i in range(2)]
        junk = SB("junk", [128, D], BF16)
        ss = [SB("ss%d" % i, [128, 1], F32) for i in range(2)]
        rs = [SB("rs%d" % i, [128, 1], F32) for i in range(2)]
        wf = [SB("wf%d" % i, [128, 16, 128], F32) for i in range(2)]
        wb = [SB("wb%d" % i, [128, 16, 128], BF16) for i in range(3)]
        wbaf = SB("wbaf", [128, 16, 12], F32)
        wbab = SB("wbab", [128, 16, 12], BF16)
        wvf = SB("wvf", [128, 16, 256], F32)
        wvb = SB("wvb", [128, 16, 256], BF16)
        ob = [SB("ob%d" % i, [128, T], BF16) for i in range(2)]
        obf = SB("obf", [12, T], F32)
        ovb = [SB("ovb%d" % i, [128, 256], BF16) for i in range(2)]
        ptr = [PS("tr%d" % i, [128, 16, 128], BF16) for i in range(2)]
        pmm = [PS("mm%d" % i, [128, 512], F32) for i in range(4)]

        P.dma('sp', nw[:], C.norm_w[l:l + 1, :].broadcast_to([128, D]), writes=['p_nw'])
        def wload(c):
            P.dma('sp', wf[c % 2][:], C.w_fm[l, c], writes=['p_wf%d' % (c % 2)])

        def wcast(c):
            P.op('pool', lambda e, c=c: e.tensor_copy(out=wb[c % 3][:], in_=wf[c % 2][:]),
                 reads=['p_wf%d' % (c % 2)], writes=['p_wb%d' % (c % 3)])

        for s in range(NSEQ):
            wload(0)
            wcast(0)
            wload(1)
            P.dma('sp', wbaf[:], C.w_ba[l], writes=['p_wbaf'])
            P.op('pool', lambda e: e.tensor_copy(out=wbab[:], in_=wbaf[:]), reads=['p_wbaf'], writes=['p_wbab'])
            P.dma('sp', wvf[:], C.w_va[l], writes=['p_wvf'])
            P.op('pool', lambda e: e.tensor_copy(out=wvb[:], in_=wvf[:]), reads=['p_wvf'], writes=['p_wvb'])
            for i in range(16):
                b = i % 2
                tok0 = s * T + i * 128
                P.dma('sp', xs[b][:], x_src[tok0:tok0 + 128, :], writes=['p_xs%d' % b])
                P.op('act', lambda e, b=b: e.activation(out=junk[:], in_=xs[b][:], func=AF.Square, accum_out=ss[b][:]),
                     reads=['p_xs%d' % b], writes=['p_junk', 'p_ss%d' % b])
                P.op('act', lambda e, b=b: e.activation(out=rs[b][:], in_=ss[b][:], func=AF.Sqrt, scale=1.0 / D, bias=C.eps_col[:, 0:1]),
                     reads=['p_ss%d' % b], writes=['p_rs%d' % b])
                P.op('dve', lambda e, b=b: e.reciprocal(out=rs[b][:], in_=rs[b][:]),
                     reads=['p_rs%d' % b], writes=['p_rs%d' % b])
                P.op('dve', lambda e, b=b: e.scalar_tensor_tensor(out=hb[b][:], in0=xs[b][:], scalar=rs[b][:, 0:1], in1=nw[:],
                                                                  op0=ALU.mult, op1=ALU.mult),
                     reads=['p_xs%d' % b, 'p_rs%d' % b, 'p_nw'], writes=['p_hb%d' % b])
                for kc in range(16):
                    P.op('pe', lambda e, b=b, kc=kc: e.transpose(out=ptr[b][:, kc, :], in_=hb[b][:, kc * 128:(kc + 1) * 128], identity=C.ident_b[:]),
                         reads=['p_hb%d' % b], writes=['p_ptr%d' % b])
                ev = 'act' if i % 2 == 0 else 'dve'
                if ev == 'act':
                    P.op('act', lambda e, b=b, i=i: e.activation(out=hT[:, :, i * 128:(i + 1) * 128], in_=ptr[b][:], func=AF.Copy),
                         reads=['p_ptr%d' % b], writes=['p_hT'], acc=True)
                else:
                    P.op('dve', lambda e, b=b, i=i: e.tensor_copy(out=hT[:, :, i * 128:(i + 1) * 128], in_=ptr[b][:]),
                         reads=['p_ptr%d' % b], writes=['p_hT'], acc=True)
            nmm = 0
            for c in range(NCH):
                w3 = c % 3
                if c + 2 < NCH:
                    wload(c + 2)
                if c + 1 < NCH:
                    wcast(c + 1)
                o2 = c % 2
                for tt in range(4):
                    pb = nmm % 4
                    nmm += 1
                    for kc in range(16):
                        P.op('pe', lambda e, w3=w3, kc=kc, tt=tt, pb=pb: e.matmul(pmm[pb][:], lhsT=wb[w3][:, kc, :], rhs=hT[:, kc, tt * 512:(tt + 1) * 512],
                                                                                start=(kc == 0), stop=(kc == 15)),
                             reads=['p_wb%d' % w3, 'p_hT'], writes=['p_mm%d' % pb])
                    if tt % 2 == 0:
                        P.op('act', lambda e, o2=o2, tt=tt, pb=pb: e.activation(out=ob[o2][:, tt * 512:(tt + 1) * 512], in_=pmm[pb][:], func=AF.Copy),
                             reads=['p_mm%d' % pb], writes=['p_ob%d' % o2], acc=True)
                    else:
                        P.op('dve', lambda e, o2=o2, tt=tt, pb=pb: e.tensor_copy(out=ob[o2][:, tt * 512:(tt + 1) * 512], in_=pmm[pb][:]),
                             reads=['p_mm%d' % pb], writes=['p_ob%d' % o2], acc=True)
                P.dma('sp', C.pT[c * 128:(c + 1) * 128, s * T:(s + 1) * T], ob[o2][:], reads=['p_ob%d' % o2], writes=['pT'], acc=True)
            for tt in range(4):
                pb = nmm % 4
                nmm += 1
                for kc in range(16):
                    P.op('pe', lambda e, kc=kc, tt=tt, pb=pb: e.matmul(pmm[pb][0:12, :], lhsT=wbab[:, kc, :], rhs=hT[:, kc, tt * 512:(tt + 1) * 512],
                                                                        start=(kc == 0), stop=(kc == 15)),
                         reads=['p_wbab', 'p_hT'], writes=['p_mm%d' % pb])
                P.op('dve', lambda e, tt=tt, pb=pb: e.tensor_copy(out=obf[:, tt * 512:(tt + 1) * 512], in_=pmm[pb][0:12, :]),
                     reads=['p_mm%d' % pb], writes=['p_obf'], acc=True)
            P.dma('sp', C.baT[:, s * T:(s + 1) * T], obf[:], reads=['p_obf'], writes=['baT'], acc=True)
            for i in range(16):
                pb = nmm % 4
                nmm += 1
                o2 = i % 2
                for kc in range(16):
                    P.op('pe', lambda e, kc=kc, i=i, pb=pb: e.matmul(pmm[pb][:, 0:256], lhsT=hT[:, kc, i * 128:(i + 1) * 128], rhs=wvb[:, kc, :],
                                                                      start=(kc == 0), stop=(kc == 15)),
                         reads=['p_wvb', 'p_hT'], writes=['p_mm%d' % pb])
                P.op('dve', lambda e, o2=o2, pb=pb: e.tensor_copy(out=ovb[o2][:], in_=pmm[pb][:, 0:256]),
                     reads=['p_mm%d' % pb], writes=['p_ovb%d' % o2])
                tok0 = s * T + i * 128
                P.dma('sp', C.va[tok0:tok0 + 128, :], ovb[o2][:], reads=['p_ovb%d' % o2], writes=['va'], acc=True)


def stage_outproj(P, nc, C, l, x_src, x_dst):
    with ExitStack() as st:
        SB = lambda n, s, d: st.enter_context(nc.sbuf_tensor("o_" + str(l) + n, s, d))
        PS = lambda n, s, d: st.enter_context(nc.psum_tensor("op_" + str(l) + n, s, d))
        wo = SB("wo", [128, 16, D], BF16)
        wf = [SB("wf%d" % i, [128, D], F32) for i in range(2)]
        yt = [SB("yt%d" % i, [128, 16, 128], BF16) for i in range(2)]
        xs = [SB("xs%d" % i, [128, D], F32) for i in range(2)]
        xo = [SB("xo%d" % i, [128, D], F32) for i in range(2)]
        pmm = [PS("mm%d" % i, [128, 512], F32) for i in range(8)]
        for kc in range(16):
            f = kc % 2
            P.dma('sp', wf[f][:], C.w_o[l, :, kc, :], writes=['o_wf%d' % f])
            P.op('pool' if kc % 2 == 0 else 'act',
                 (lambda e, f=f, kc=kc: e.tensor_copy(out=wo[:, kc, :], in_=wf[f][:])) if kc % 2 == 0 else
                 (lambda e, f=f, kc=kc: e.activation(out=wo[:, kc, :], in_=wf[f][:], func=AF.Copy)),
                 reads=['o_wf%d' % f], writes=['o_wo'], acc=True)
        def oload(i):
            b = i % 2
            tok0 = i * 128
            P.dma('sp', yt[b][:], C.yT[:, tok0:tok0 + 128].rearrange("(kc p) t -> p kc t", p=128), reads=['yT'], writes=['o_yt%d' % b])
            P.dma('sp', xs[b][:], x_src[tok0:tok0 + 128, :], reads=[C.xres_key(x_src)], writes=['o_xs%d' % b])

        nmm = 0
        NT = NTOK // 128
        oload(0)
        for i in range(NT):
            b = i % 2
            tok0 = i * 128
            if i + 1 < NT:
                oload(i + 1)
            for ct in range(4):
                pb = nmm % 8
                nmm += 1
                for kc in range(16):
                    P.op('pe', lambda e, b=b, kc=kc, ct=ct, pb=pb: e.matmul(pmm[pb][:], lhsT=yt[b][:, kc, :], rhs=wo[:, kc, ct * 512:(ct + 1) * 512],
                                                                            start=(kc == 0), stop=(kc == 15)),
                         reads=['o_yt%d' % b, 'o_wo'], writes=['o_mm%d' % pb])
                P.op('dve', lambda e, b=b, ct=ct, pb=pb: e.tensor_tensor(out=xo[b][:, ct * 512:(ct + 1) * 512], in0=xs[b][:, ct * 512:(ct + 1) * 512],
                                                                         in1=pmm[pb][:], op=ALU.add),
                     reads=['o_mm%d' % pb, 'o_xs%d' % b], writes=['o_xo%d' % b], acc=True)
            P.dma('sp', x_dst[tok0:tok0 + 128, :], xo[b][:], reads=['o_xo%d' % b], writes=[C.xres_key(x_dst)], acc=True)


def stage_conformer(P, nc, C, l):
    with ExitStack() as st:
        SB = lambda n, s, d: st.enter_context(nc.sbuf_tensor("sb_" + str(l) + n, s, d))
        PS = lambda n, s, d: st.enter_context(nc.psum_tensor("bp_" + str(l) + n, s, d))
        cw = SB("cw", [128, 4, 31], F32)
        cb = SB("cb", [128, 4], F32)
        lnw = SB("lnw", [128, 4], F32)
        lnb = SB("lnb", [128, 4], F32)
        pwbias = SB("pwbias", [128, 4], F32)
        pwf = SB("pwf", [128, 4, 512], F32)
        pwb = SB("pwb", [128, 4, 512], BF16)
        dg = SB("dg", [128, 124, 128], BF16)
        W = 542
        at = [SB("at%d" % i, [128, 4, W], BF16) for i in range(2)]
        gt = [SB("gt%d" % i, [128, 4, W], BF16) for i in range(2)]
        zt = [SB("zt%d" % i, [128, 4, 512], BF16) for i in range(2)]
        sig = SB("sig", [128, 4, W], BF16)
        hg = [SB("hg%d" % i, [128, 4, W], BF16) for i in range(2)]
        xc = [SB("xc%d" % i, [128, 4, 512], F32) for i in range(2)]
        xcb = [SB("xcb%d" % i, [128, 4, 512], BF16) for i in range(2)]
        sqb = [SB("sqb%d" % i, [128, 4, 512], BF16) for i in range(2)]
        mean = SB("mean", [128, 512], F32)
        m2 = SB("m2", [128, 512], F32)
        var = SB("var", [128, 512], F32)
        rstd = SB("rstd", [128, 512], F32)
        t1 = SB("t1", [128, 4, 512], F32)
        hs = SB("hs", [128, 4, 512], BF16)
        sz = SB("sz", [128, 4, 512], BF16)
        yb = [SB("yb%d" % i, [128, 4, 512], BF16) for i in range(2)]
        pc = [PS("c%d" % i, [128, 512], F32) for i in range(4)]
        pm = PS("m", [128, 512], F32)
        pq = PS("q", [128, 512], F32)
        po = [PS("o%d" % i, [128, 512], F32) for i in range(2)]
        P.excl.update(['b_pc%d' % i for i in range(4)] + ['b_pm', 'b_pq', 'b_po0', 'b_po1'])

        P.dma('sp', cw[:], C.b_cw[l], writes=['b_cw'])
        P.dma('sp', cb[:], C.b_cb[l], writes=['b_cb'])
        P.dma('sp', lnw[:], C.b_lnw[l], writes=['b_lnw'])
        P.dma('sp', lnb[:], C.b_lnb[l], writes=['b_lnb'])
        P.dma('sp', pwbias[:], C.b_pwb[l], writes=['b_pwbias'])
        P.dma('sp', pwf[:], C.b_pw[l], writes=['b_pwf'])
        P.op('dve', lambda e: e.tensor_copy(out=pwb[:], in_=pwf[:]), reads=['b_pwf'], writes=['b_pwb'])
        for c in range(4):
            for k in range(31):
                eng = 'pool' if (c * 31 + k) % 3 == 2 else 'dve'
                P.op(eng, lambda e, c=c, k=k: e.tensor_scalar(out=dg[:, c * 31 + k, :], in0=C.ident_f[:], scalar1=cw[:, c, k:k + 1], scalar2=None, op0=ALU.mult),
                     reads=['b_cw', 'g_ident_f'], writes=['b_dg'], acc=True)

        tiles = [(s, tt) for s in range(NSEQ) for tt in range(4)]

        def front(i):
            s, tt = tiles[i]
            b = i % 2
            tok0 = s * T + tt * 512
            if tt == 0:
                P.op('pool', lambda e, b=b: e.memset(at[b][:, :, 0:30], 0.0), writes=['b_at%d' % b])
                P.op('pool', lambda e, b=b: e.memset(gt[b][:, :, 0:30], 0.0), writes=['b_gt%d' % b])
                P.dma('sp', at[b][:, :, 30:W], C.pT[UBA0:UBA0 + 512, tok0:tok0 + 512].rearrange("(c p) t -> p c t", p=128), reads=['pT'], writes=['b_at%d' % b])
                P.dma('sp', gt[b][:, :, 30:W], C.pT[UBG0:UBG0 + 512, tok0:tok0 + 512].rearrange("(c p) t -> p c t", p=128), reads=['pT'], writes=['b_gt%d' % b])
            else:
                P.dma('sp', at[b][:], C.pT[UBA0:UBA0 + 512, tok0 - 30:tok0 + 512].rearrange("(c p) t -> p c t", p=128), reads=['pT'], writes=['b_at%d' % b])
                P.dma('sp', gt[b][:], C.pT[UBG0:UBG0 + 512, tok0 - 30:tok0 + 512].rearrange("(c p) t -> p c t", p=128), reads=['pT'], writes=['b_gt%d' % b])
            P.dma('sp', zt[b][:], C.pT[ZB0:ZB0 + 512, tok0:tok0 + 512].rearrange("(c p) t -> p c t", p=128), reads=['pT'], writes=['b_zt%d' % b])
            P.op('act', lambda e, b=b: e.activation(out=sig[:], in_=gt[b][:], func=AF.Sigmoid), reads=['b_gt%d' % b], writes=['b_sig'])
            P.op('dve', lambda e, b=b: e.tensor_tensor(out=hg[b][:], in0=at[b][:], in1=sig[:], op=ALU.mult), reads=['b_at%d' % b, 'b_sig'], writes=['b_hg%d' % b])

        def conv(i):
            b = i % 2
            for c in range(4):
                for k in range(31):
                    P.op('pe', lambda e, c=c, k=k, b=b: e.matmul(pc[c][:], lhsT=dg[:, c * 31 + k, :], rhs=hg[b][:, c, k:k + 512], start=(k == 0), stop=(k == 30)),
                         reads=['b_dg', 'b_hg%d' % b], writes=['b_pc%d' % c])
                P.op('act', lambda e, c=c, b=b: e.activation(out=xc[b][:, c, :], in_=pc[c][:], func=AF.Identity, bias=cb[:, c:c + 1]),
                     reads=['b_pc%d' % c, 'b_cb'], writes=['b_xc%d' % b], acc=True)
                P.op('dve', lambda e, c=c, b=b: e.tensor_scalar(out=xcb[b][:, c, :], in0=pc[c][:], scalar1=cb[:, c:c + 1], scalar2=None, op0=ALU.add),
                     reads=['b_pc%d' % c, 'b_cb'], writes=['b_xcb%d' % b], acc=True)
                P.op('act', lambda e, c=c, b=b: e.activation(out=sqb[b][:, c, :], in_=pc[c][:], func=AF.Square, bias=cb[:, c:c + 1]),
                     reads=['b_pc%d' % c, 'b_cb'], writes=['b_sqb%d' % b], acc=True)

        def back(i):
            s, tt = tiles[i]
            b = i % 2
            tok0 = s * T + tt * 512
            for c in range(4):
                P.op('pe', lambda e, c=c, b=b: e.matmul(pm[:], lhsT=C.ones_b[:], rhs=xcb[b][:, c, :], start=(c == 0), stop=(c == 3)),
                     reads=['b_xcb%d' % b, 'g_ones_b'], writes=['b_pm'])
            for c in range(4):
                P.op('pe', lambda e, c=c, b=b: e.matmul(pq[:], lhsT=C.ones_b[:], rhs=sqb[b][:, c, :], start=(c == 0), stop=(c == 3)),
                     reads=['b_sqb%d' % b, 'g_ones_b'], writes=['b_pq'])
            P.op('act', lambda e: e.activation(out=mean[:], in_=pm[:], func=AF.Copy, scale=1.0 / 512), reads=['b_pm'], writes=['b_mean'])
            P.op('pool', lambda e: e.tensor_tensor(out=m2[:], in0=mean[:], in1=mean[:], op=ALU.mult), reads=['b_mean'], writes=['b_m2'])
            P.op('dve', lambda e: e.scalar_tensor_tensor(out=var[:], in0=pq[:], scalar=1.0 / 512, in1=m2[:], op0=ALU.mult, op1=ALU.subtract),
                 reads=['b_pq', 'b_m2'], writes=['b_var'])
            P.op('act', lambda e: e.activation(out=rstd[:], in_=var[:], func=AF.Ln, bias=C.eps_col[:, 0:1]), reads=['b_var'], writes=['b_rstd'])
            P.op('act', lambda e: e.activation(out=rstd[:], in_=rstd[:], func=AF.Exp, scale=-0.5), reads=['b_rstd'], writes=['b_rstd'])
            P.op('act', lambda e, b=b: e.activation(out=sz[:], in_=zt[b][:], func=AF.Silu), reads=['b_zt%d' % b], writes=['b_sz'])
            for c in range(4):
                P.op('pool', lambda e, c=c, b=b: e.tensor_tensor(out=t1[:, c, :], in0=xc[b][:, c, :], in1=mean[:], op=ALU.subtract),
                     reads=['b_xc%d' % b, 'b_mean'], writes=['b_t1_%d' % c])
                P.op('dve', lambda e, c=c: e.tensor_tensor(out=t1[:, c, :], in0=t1[:, c, :], in1=rstd[:], op=ALU.mult),
                     reads=['b_t1_%d' % c, 'b_rstd'], writes=['b_t1_%d' % c])
                P.op('act', lambda e, c=c: e.activation(out=hs[:, c, :], in_=t1[:, c, :], func=AF.Silu, scale=lnw[:, c:c + 1], bias=lnb[:, c:c + 1]),
                     reads=['b_t1_%d' % c, 'b_lnw', 'b_lnb'], writes=['b_hs'], acc=True)
            for co in range(4):
                o2 = co % 2
                for ci in range(4):
                    P.op('pe', lambda e, co=co, ci=ci, o2=o2: e.matmul(po[o2][:], lhsT=pwb[:, ci, co * 128:(co + 1) * 128], rhs=hs[:, ci, :], start=(ci == 0), stop=(ci == 3)),
                         reads=['b_pwb', 'b_hs'], writes=['b_po%d' % o2])
                P.op('dve', lambda e, co=co, b=b, o2=o2: e.scalar_tensor_tensor(out=yb[b][:, co, :], in0=po[o2][:], scalar=pwbias[:, co:co + 1], in1=sz[:, co, :],
                                                                                op0=ALU.add, op1=ALU.mult),
                     reads=['b_po%d' % o2, 'b_pwbias', 'b_sz'], writes=['b_yb%d' % b], acc=True)
            P.dma('sp', C.yT[YB0:YB0 + 512, tok0:tok0 + 512].rearrange("(c p) t -> p c t", p=128), yb[b][:], reads=['b_yb%d' % b], writes=['yT'], acc=True)

        NT = len(tiles)
        front(0)
        conv(0)
        for i in range(NT):
            if i + 1 < NT:
                front(i + 1)
                conv(i + 1)
            back(i)


def stage_attn(P, nc, C, l):
    TWO_PI = 6.283185307179586
    C1 = 6.28125
    C2 = TWO_PI - C1
    MAGIC = 12582912.0
    with ExitStack() as st:
        SB = lambda n, s, d: st.enter_context(nc.sbuf_tensor("sa_" + str(l) + n, s, d))
        PS = lambda n, s, d: st.enter_context(nc.psum_tensor("ap_" + str(l) + n, s, d))
        invf = SB("invf", [64, 1], F32)
        rtf = SB("rtf", [64, 64], F32)
        rtb = SB("rtb", [64, 64], BF16)
        mkf = SB("mkf", [128, 2, 128], F32)
        mkb = SB("mkb", [128, 2, 128], BF16)
        qw = SB("qw", [64, 1], F32)
        kw = SB("kw", [64, 1], F32)
        snk = SB("snk", [1, 12], F32)
        snkb = SB("snkb", [1, 12, 128], BF16)
        posi = SB("posi", [64, T], I32)
        cosT = SB("cosT", [64, T], F32)
        sinT = SB("sinT", [64, T], F32)
        sq = SB("sq", [64, T], BF16)
        rstd = SB("rstd", [64, T], F32)
        xn = SB("xn", [64, T], BF16)
        t1 = SB("t1", [64, T], F32)
        t2 = SB("t2", [64, T], F32)
        ang, tn, tr = rstd, t1, t2
        kANG, kTN, kTR = 'a_rstd', 'a_t1', 'a_t2'
        kT = [SB("kT%d" % i, [64, T], BF16) for i in range(2)]
        qT = [SB("qT%d" % i, [64, 3, T], BF16) for i in range(2)]
        zaT = [SB("zaT%d" % i, [64, 3, T], BF16) for i in range(2)]
        vt = [SB("vt%d" % i, [128, 16, 64], BF16) for i in range(2)]
        kr = [SB("kr%d" % i, [64, T], BF16) for i in range(2)]
        qr = [SB("qr%d" % i, [64, 3, T], BF16) for i in range(2)]
        ya = [SB("ya%d" % i, [64, 3, T], BF16) for i in range(2)]
        pe_sb = [SB("pe%d" % i, [128, 2, 384], BF16) for i in range(2)]
        pm_sb = [SB("pm%d" % i, [128, 2, 384], BF16) for i in range(2)]
        rden = SB("rden", [64, 384], F32)
        o_sb = SB("o_sb", [64, 384], F32)
        ps = [PS("b%d" % i, [128, 512], F32) for i in range(8)]
        PK = ['a_ps%d' % i for i in range(8)]
        P.excl.update(PK)
        P.dma('sp', invf[:], C.a_invf, writes=['a_invf'])
        P.dma('sp', rtf[:], C.a_rt, writes=['a_rtf'])
        P.dma('sp', mkf[:], C.a_mask, writes=['a_mkf'])
        P.dma('sp', qw[:], C.a_qw[l], writes=['a_qw'])
        P.dma('sp', kw[:], C.a_kw[l], writes=['a_kw'])
        P.dma('sp', snk[:], C.a_snk[l], writes=['a_snk'])
        P.op('dve', lambda e: e.tensor_copy(out=rtb[:], in_=rtf[:]), reads=['a_rtf'], writes=['a_rtb'])
        P.op('dve', lambda e: e.tensor_copy(out=mkb[:], in_=mkf[:]), reads=['a_mkf'], writes=['a_mkb'])
        P.op('act', lambda e: e.activation(out=snk[:], in_=snk[:], func=AF.Exp), reads=['a_snk'], writes=['a_snk'])
        P.op('dve', lambda e: e.tensor_copy(out=snkb[:], in_=snk[:, :].unsqueeze(2).broadcast_to([1, 12, 128])), reads=['a_snk'], writes=['a_snkb'])

        def sincos(dst, shift):
            P.op('dve', lambda e: e.tensor_scalar(out=tn[:], in0=ang[:], scalar1=shift, scalar2=1.0 / TWO_PI, op0=ALU.add, op1=ALU.mult),
                 reads=[kANG], writes=[kTN])
            P.op('dve', lambda e: e.tensor_scalar(out=tn[:], in0=tn[:], scalar1=MAGIC, scalar2=None, op0=ALU.add), reads=[kTN], writes=[kTN])
            P.op('dve', lambda e: e.tensor_scalar(out=tn[:], in0=tn[:], scalar1=-MAGIC, scalar2=None, op0=ALU.add), reads=[kTN], writes=[kTN])
            P.op('dve', lambda e: e.scalar_tensor_tensor(out=tr[:], in0=tn[:], scalar=-C1, in1=ang[:], op0=ALU.mult, op1=ALU.add),
                 reads=[kTN, kANG], writes=[kTR])
            P.op('dve', lambda e: e.tensor_scalar(out=tr[:], in0=tr[:], scalar1=shift, scalar2=None, op0=ALU.add), reads=[kTR], writes=[kTR])
            P.op('dve', lambda e: e.scalar_tensor_tensor(out=tr[:], in0=tn[:], scalar=-C2, in1=tr[:], op0=ALU.mult, op1=ALU.add),
                 reads=[kTN, kTR], writes=[kTR])
            P.op('dve', lambda e: e.tensor_scalar(out=tr[:], in0=tr[:], scalar1=3.1415925, scalar2=-3.1415925, op0=ALU.min, op1=ALU.max),
                 reads=[kTR], writes=[kTR])
            P.op('act', lambda e: e.activation(out=dst[:], in_=tr[:], func=AF.Sin), reads=[kTR], writes=['a_tab'])

        def tables(s):
            P.dma('sp', posi[:], C.pos[s:s + 1, :].broadcast_to([64, T]), writes=['a_posi'])
            P.op('dve', lambda e: e.tensor_copy(out=ang[:], in_=posi[:]), reads=['a_posi'], writes=[kANG])
            P.op('dve', lambda e: e.tensor_scalar(out=ang[:], in0=ang[:], scalar1=invf[:, 0:1], scalar2=None, op0=ALU.mult), reads=[kANG, 'a_invf'], writes=[kANG])
            sincos(sinT, 0.0)
            sincos(cosT, 1.5707963267948966)

        def normrope(src, wcol, wkey, dst, skey, dkey):
            NB = 7
            P.op('act', lambda e: e.activation(out=sq[:], in_=src, func=AF.Square), reads=[skey], writes=['a_sq'])
            for q4 in range(4):
                P.op('pe', lambda e, q4=q4: e.matmul(ps[NB][0:64, :], lhsT=C.ones_b[0:64, 0:64], rhs=sq[:, q4 * 512:(q4 + 1) * 512], start=True, stop=True),
                     reads=['a_sq', 'g_ones_b'], writes=[PK[NB]])
                P.op('act', lambda e, q4=q4: e.activation(out=rstd[:, q4 * 512:(q4 + 1) * 512], in_=ps[NB][0:64, :], func=AF.Ln, scale=1.0 / 64, bias=C.eps_col[0:64, 0:1]),
                     reads=[PK[NB]], writes=['a_rstd'], acc=True)
            yield
            P.op('act', lambda e: e.activation(out=rstd[:], in_=rstd[:], func=AF.Exp, scale=-0.5), reads=['a_rstd'], writes=['a_rstd'])
            P.op('dve', lambda e: e.scalar_tensor_tensor(out=xn[:], in0=src, scalar=wcol[:, 0:1], in1=rstd[:], op0=ALU.mult, op1=ALU.mult),
                 reads=[skey, wkey, 'a_rstd'], writes=['a_xn'])
            yield
            P.op('pool', lambda e: e.tensor_tensor(out=t1[:], in0=xn[:], in1=cosT[:], op=ALU.mult), reads=['a_xn', 'a_tab'], writes=['a_t1'])
            for q4 in range(4):
                P.op('pe', lambda e, q4=q4: e.matmul(ps[NB][0:64, :], lhsT=rtb[:], rhs=xn[:, q4 * 512:(q4 + 1) * 512], start=True, stop=True),
                     reads=['a_xn', 'a_rtb'], writes=[PK[NB]])
                P.op('dve', lambda e, q4=q4: e.tensor_tensor(out=t2[:, q4 * 512:(q4 + 1) * 512], in0=ps[NB][0:64, :], in1=sinT[:, q4 * 512:(q4 + 1) * 512], op=ALU.mult),
                     reads=[PK[NB], 'a_tab'], writes=['a_t2'], acc=True)
                yield
            P.op('pool', lambda e: e.tensor_tensor(out=dst, in0=t1[:], in1=t2[:], op=ALU.add), reads=['a_t1', 'a_t2'], writes=[dkey], acc=True)
            yield

        def prep(s, g, gp):
            c0 = s * T
            G = lambda n: 'a_%s%d' % (n, gp)
            P.dma('sp', kT[gp][:], C.pT[K0 + g * 64:K0 + (g + 1) * 64, c0:c0 + T], reads=['pT'], writes=[G('kT')])
            P.dma('sp', qT[gp][:], C.pT[Q0 + 3 * g * 64:Q0 + 3 * (g + 1) * 64, c0:c0 + T].rearrange("(j d) t -> d j t", d=64), reads=['pT'], writes=[G('qT')])
            P.dma('sp', zaT[gp][:], C.pT[ZA0 + 3 * g * 64:ZA0 + 3 * (g + 1) * 64, c0:c0 + T].rearrange("(j d) t -> d j t", d=64), reads=['pT'], writes=[G('zaT')])
            P.dma('sp', vt[gp][:], C.va[c0:c0 + T, g * 64:(g + 1) * 64].rearrange("(n p) d -> p n d", p=128), reads=['va'], writes=[G('vt')])
            yield
            yield from normrope(kT[gp][:], kw, 'a_kw', kr[gp][:], G('kT'), G('kr'))
            for j in range(3):
                yield from normrope(qT[gp][:, j, :], qw, 'a_qw', qr[gp][:, j, :], G('qT'), G('qr'))
            P.op('act', lambda e: e.activation(out=zaT[gp][:], in_=zaT[gp][:], func=AF.Silu), reads=[G('zaT')], writes=[G('zaT')])
            yield

        def blocks(s, g, gp):
            c0 = s * T
            G = lambda n: 'a_%s%d' % (n, gp)

            def stage1(n):
                par = n % 2
                ipb, icb = 2 * par, 2 * par + 1
                qs = qr[gp][:, :, n * 128:(n + 1) * 128]
                P.op('pe', lambda e: e.matmul(ps[icb][:, 0:384], lhsT=kr[gp][:, n * 128:(n + 1) * 128], rhs=qs, start=True, stop=True),
                     reads=[G('kr'), G('qr')], writes=[PK[icb]])
                if n > 0:
                    P.op('pe', lambda e: e.matmul(ps[ipb][:, 0:384], lhsT=kr[gp][:, (n - 1) * 128:n * 128], rhs=qs, start=True, stop=True),
                         reads=[G('kr'), G('qr')], writes=[PK[ipb]])
                P.op('act', lambda e: e.activation(out=pe_sb[par][:, 1, :], in_=ps[icb][:, 0:384], func=AF.Exp, scale=0.125),
                     reads=[PK[icb]], writes=['a_pe%d' % par], acc=True)
                if n > 0:
                    P.op('act', lambda e: e.activation(out=pe_sb[par][:, 0, :], in_=ps[ipb][:, 0:384], func=AF.Exp, scale=0.125),
                         reads=[PK[ipb]], writes=['a_pe%d' % par], acc=True)
                lo = 0 if n > 0 else 1
                meng = 'dve' if n % 2 == 0 else 'pool'
                P.op(meng, lambda e: e.tensor_tensor(
                    out=pm_sb[par][:, lo:2, :].rearrange("p a (h q) -> p a h q", h=3),
                    in0=pe_sb[par][:, lo:2, :].rearrange("p a (h q) -> p a h q", h=3),
                    in1=mkb[:, lo:2, :].unsqueeze(2).broadcast_to([128, 2 - lo, 3, 128]), op=ALU.mult),
                    reads=['a_pe%d' % par, 'a_mkb'], writes=['a_pm%d' % par])

            def stage2(n):
                par = n % 2
                iob, idb = 4 + par, 6
                if n > 0:
                    P.op('pe', lambda e: e.matmul(ps[iob][0:64, 0:384], lhsT=vt[gp][:, n - 1, :], rhs=pm_sb[par][:, 0, :], start=True, stop=False),
                         reads=[G('vt'), 'a_pm%d' % par], writes=[PK[iob]])
                P.op('pe', lambda e: e.matmul(ps[iob][0:64, 0:384], lhsT=vt[gp][:, n, :], rhs=pm_sb[par][:, 1, :], start=(n == 0), stop=True),
                     reads=[G('vt'), 'a_pm%d' % par], writes=[PK[iob]])
                if n > 0:
                    P.op('pe', lambda e: e.matmul(ps[idb][0:64, 0:384], lhsT=C.ones_b[:, 0:64], rhs=pm_sb[par][:, 0, :], start=True, stop=False),
                         reads=['g_ones_b', 'a_pm%d' % par], writes=[PK[idb]])
                P.op('pe', lambda e: e.matmul(ps[idb][0:64, 0:384], lhsT=C.ones_b[:, 0:64], rhs=pm_sb[par][:, 1, :], start=(n == 0), stop=False),
                     reads=['g_ones_b', 'a_pm%d' % par], writes=[PK[idb]])
                P.op('pe', lambda e: e.matmul(ps[idb][0:64, 0:384], lhsT=C.ones_b[0:1, 0:64], rhs=snkb[0:1, 3 * g:3 * g + 3, :], start=False, stop=True),
                     reads=['g_ones_b', 'a_snkb'], writes=[PK[idb]])
                P.op('act', lambda e: e.activation(out=rden[:], in_=ps[idb][0:64, 0:384], func=AF.Ln), reads=[PK[idb]], writes=['a_rden'])
                P.op('act', lambda e: e.activation(out=rden[:], in_=rden[:], func=AF.Exp, scale=-1.0), reads=['a_rden'], writes=['a_rden'])
                P.op('dve', lambda e: e.tensor_tensor(out=o_sb[:], in0=ps[iob][0:64, 0:384], in1=rden[:], op=ALU.mult), reads=[PK[iob], 'a_rden'], writes=['a_osb'])
                P.op('pool', lambda e: e.tensor_tensor(out=ya[gp][:, :, n * 128:(n + 1) * 128], in0=o_sb[:].rearrange("p (h q) -> p h q", h=3),
                                                       in1=zaT[gp][:, :, n * 128:(n + 1) * 128], op=ALU.mult),
                     reads=['a_osb', G('zaT')], writes=[G('ya')], acc=True)

            stage1(0)
            yield
            for n in range(16):
                if n + 1 < 16:
                    stage1(n + 1)
                    yield
                stage2(n)
                yield
            P.dma('sp', C.yT[YA0 + 3 * g * 64:YA0 + 3 * (g + 1) * 64, c0:c0 + T].rearrange("(j d) t -> d j t", d=64), ya[gp][:], reads=[G('ya')], writes=['yT'], acc=True)

        def merge(gens):
            live = [[g, w] for g, w in gens]
            while live:
                for item in list(live):
                    g, w = item
                    for _ in range(w):
                        try:
                            next(g)
                        except StopIteration:
                            live.remove(item)
                            break

        work = [(s, g) for s in range(NSEQ) for g in range(4)]
        tables(0)
        merge([[prep(work[0][0], work[0][1], 0), 1]])
        for i, (s, g) in enumerate(work):
            gl = [[blocks(s, g, i % 2), 1]]
            if i + 1 < len(work):
                s2, g2 = work[i + 1]
                if s2 != s:
                    tables(s2)
                gl.append([prep(s2, g2, (i + 1) % 2), 1])
            merge(gl)


def stage_gdn(P, nc, C, l):
    NH = 6
    with ExitStack() as st:
        SB = lambda n, s, d: st.enter_context(nc.sbuf_tensor("sc_" + str(l) + n, s, d))
        PS = lambda n, s, d: st.enter_context(nc.psum_tensor("cp_" + str(l) + n, s, d))
        cwc = SB("cwc", [128, 18, 4], F32)
        dgc = SB("dgc", [128, 72, 128], BF16)
        onw = SB("onw", [128, 1], F32)
        alog = SB("alog", [6, 1], F32)
        dtb = SB("dtb", [6, 1], F32)
        lmf = SB("lmf", [128, 7, 128], F32)
        lmb = SB("lmb", [128, 7, 128], BF16)
        ngm = SB("ngm", [128, 128], F32)
        sel = SB("sel", [6, 6, 128], F32)
        onesf = SB("onesf", [6, 128], F32)
        beta1 = SB("beta", [6, T], F32)
        gg1 = SB("gg", [6, T], F32)
        gc1 = SB("gc", [6, T], F32)
        egl1 = SB("egl", [6, 16], F32)
        egd1 = SB("egd", [6, 16, 6], F32)
        S32_1 = SB("S32", [128, NH, 128], F32)
        Sb_1 = SB("Sb", [128, NH, 128], BF16)
        braw = araw = beta = gg = gc = edl = egl = egd = S32 = Sb = None
        yc = [SB("yc%d" % i, [128, NH, 512], BF16) for i in range(2)]
        zc = [SB("zc%d" % i, [128, NH, 512], BF16) for i in range(2)]
        szc = zc
        def two(n, shp, d):
            return [SB("%s%d" % (n, i), shp, d) for i in range(2)]
        xin = two("xin", [128, 18, 131], BF16)
        qkv0_1 = SB("qkv0", [128, 18, 128], F32)
        qkv0 = [qkv0_1, qkv0_1]
        sqt = two("sqt", [128, NH, 128], BF16)
        rn = two("rn", [128, NH, 128], F32)
        KT = two("KT", [128, NH, 128], BF16)
        QT = two("QT", [128, NH, 128], BF16)
        VT = two("VT", [128, NH, 128], BF16)
        KgT = two("KgT", [128, NH, 128], BF16)
        QgT = two("QgT", [128, NH, 128], BF16)
        egcb = two("egcb", [128, NH, 128], F32)
        tmpa = two("tmpa", [128, NH, 128], F32)
        LTb = two("LTb", [128, NH, 128], BF16)
        ET = two("ET", [128, NH, 128], F32)
        cols = two("cols", [128, 24], F32)
        intraT = two("intraT", [128, NH, 128], BF16)
        ktil = two("ktil", [128, NH, 128], BF16)
        BsT = two("BsT", [128, NH, 128], BF16)
        Zsb = two("Zsb", [128, NH, 128], BF16)
        XmA = two("XmA", [128, NH, 128], BF16)
        XmB = two("XmB", [128, NH, 128], BF16)
        XTA = two("XTA", [128, NH, 128], BF16)
        XTB = two("XTB", [128, NH, 128], BF16)
        rsb = two("rsb", [128, NH, 128], BF16)
        vnew = two("vnew", [128, NH, 128], BF16)
        o32 = egcb
        ro = rn
        NSLOT = 4
        slots = [PS("s%d" % i, [128, 8, 128], F32) for i in range(NSLOT)]
        SK = ['c_slot%d' % i for i in range(NSLOT)]
        P.excl.update(SK)
        slot_ctr = [0]

        def nslot():
            i = slot_ctr[0] % NSLOT
            slot_ctr[0] += 1
            return slots[i], SK[i]

        def bc(colap):
            return colap.unsqueeze(2).broadcast_to([128, NH, 128])

        P.dma('sp', cwc[:], C.c_cw[l], writes=['c_cwc'])
        P.dma('sp', onw[:], C.c_onw[l], writes=['c_onw'])
        P.dma('sp', alog[:], C.c_alog[l], writes=['c_alog'])
        P.dma('sp', dtb[:], C.c_dtb[l], writes=['c_dtb'])
        P.dma('sp', lmf[:], C.c_lm, writes=['c_lmf'])
        P.dma('sp', ngm[:], C.c_ngm, writes=['c_ngm'])
        P.dma('sp', sel[:], C.c_sel, writes=['c_sel'])
        P.op('dve', lambda e: e.tensor_copy(out=lmb[:], in_=lmf[:]), reads=['c_lmf'], writes=['c_lmb'])
        P.op('pool', lambda e: e.memset(onesf[:], 1.0), writes=['c_onesf'])
        P.op('act', lambda e: e.activation(out=alog[:], in_=alog[:], func=AF.Exp), reads=['c_alog'], writes=['c_alog'])
        P.op('dve', lambda e: e.tensor_scalar(out=alog[:], in0=alog[:], scalar1=-1.0, scalar2=None, op0=ALU.mult), reads=['c_alog'], writes=['c_alog'])
        for cc in range(18):
            for k in range(4):
                eng = 'pool' if (cc * 4 + k) % 3 == 2 else 'dve'
                P.op(eng, lambda e, cc=cc, k=k: e.tensor_scalar(out=dgc[:, cc * 4 + k, :], in0=C.ident_f[:], scalar1=cwc[:, cc, k:k + 1], scalar2=None, op0=ALU.mult),
                     reads=['c_cwc', 'g_ident_f'], writes=['c_dgc'], acc=True)

        unit = 0
        for s in range(NSEQ):
            c0 = s * T
            beta, gg, gc, edl, egl, egd, S32, Sb = beta1, gg1, gc1, gg1, egl1, egd1, S32_1, Sb_1
            P.dma('sp', beta[:], C.baT[0:6, c0:c0 + T], reads=['baT'], writes=['c_beta'])
            P.dma('sp', gg[:], C.baT[6:12, c0:c0 + T], reads=['baT'], writes=['c_gg'])
            P.op('act', lambda e: e.activation(out=beta[:], in_=beta[:], func=AF.Sigmoid), reads=['c_beta'], writes=['c_beta'])
            P.op('act', lambda e: e.activation(out=gg[:], in_=gg[:], func=AF.Exp, bias=dtb[:, 0:1]), reads=['c_gg', 'c_dtb'], writes=['c_gg'])
            P.op('act', lambda e: e.activation(out=gg[:], in_=gg[:], func=AF.Ln, bias=C.one_col[0:6, 0:1]), reads=['c_gg'], writes=['c_gg'])
            P.op('dve', lambda e: e.tensor_scalar(out=gg[:], in0=gg[:], scalar1=alog[:, 0:1], scalar2=None, op0=ALU.mult), reads=['c_gg', 'c_alog'], writes=['c_gg'])
            for c in range(16):
                P.op('dve', lambda e, c=c: e.tensor_tensor_scan(out=gc[:, c * 128:(c + 1) * 128], data0=onesf[:], data1=gg[:, c * 128:(c + 1) * 128], initial=0.0, op0=ALU.mult, op1=ALU.add),
                     reads=['c_gg', 'c_onesf'], writes=['c_gc'], acc=True)
            gc3 = gc[:].rearrange("p (c t) -> p c t", t=128)
            P.op('dve', lambda e, gc3=gc3: e.tensor_tensor(out=edl[:].rearrange("p (c t) -> p c t", t=128), in0=gc3[:, :, 127:128].broadcast_to([6, 16, 128]), in1=gc3, op=ALU.subtract),
                 reads=['c_gc', 'c_gg'], writes=['c_gg'])
            P.op('act', lambda e: e.activation(out=edl[:], in_=edl[:], func=AF.Exp), reads=['c_gg'], writes=['c_gg'])
            P.op('act', lambda e, gc3=gc3: e.activation(out=egl[:].unsqueeze(2), in_=gc3[:, :, 127:128], func=AF.Exp), reads=['c_gc'], writes=['c_egl'])
            P.op('dve', lambda e: e.tensor_tensor(out=egd[:], in0=C.ident_f[0:6, 0:6].unsqueeze(1).broadcast_to([6, 16, 6]), in1=egl[:].unsqueeze(2).broadcast_to([6, 16, 6]), op=ALU.mult),
                 reads=['c_egl', 'g_ident_f'], writes=['c_egd'])
            P.op('pool', lambda e: e.memset(S32[:], 0.0), writes=['c_S32'])
            P.op('pool', lambda e: e.memset(Sb[:], 0.0), writes=['c_Sb'])
            def pre(c, u):
                U = lambda n, u=u: 'c_%s%d' % (n, u)
                tok0 = s * T + c * 128
                y2 = (c // 4) % 2
                if c % 4 == 0:
                    t512 = s * T + (c // 4) * 512
                    P.dma('sp', zc[y2][:], C.pT[ZC0:ZC0 + 768, t512:t512 + 512].rearrange("(h p) t -> p h t", p=128), reads=['pT'], writes=['c_zc%d' % y2])
                    P.op('act', lambda e, y2=y2: e.activation(out=zc[y2][:], in_=zc[y2][:], func=AF.Silu), reads=['c_zc%d' % y2], writes=['c_zc%d' % y2])
                if c == 0:
                    P.op('pool', lambda e, u=u: e.memset(xin[u][:, :, 0:3], 0.0), writes=[U('xin')])
                    P.dma('sp', xin[u][:, :, 3:131], C.pT[CQ0:CQ0 + 2304, tok0:tok0 + 128].rearrange("(cc p) t -> p cc t", p=128), reads=['pT'], writes=[U('xin')])
                else:
                    P.dma('sp', xin[u][:], C.pT[CQ0:CQ0 + 2304, tok0 - 3:tok0 + 128].rearrange("(cc p) t -> p cc t", p=128), reads=['pT'], writes=[U('xin')])
                for grp in range(3):
                    sl, sk = nslot()
                    for hh in range(6):
                        cc = grp * 6 + hh
                        for k in range(4):
                            P.op('pe', lambda e, u=u, cc=cc, k=k, hh=hh, sl=sl: e.matmul(sl[:, hh, :], lhsT=dgc[:, cc * 4 + k, :], rhs=xin[u][:, cc, k:k + 128], start=(k == 0), stop=(k == 3)),
                                 reads=['c_dgc', U('xin')], writes=[sk])
                    P.op('act', lambda e, grp=grp, sl=sl: e.activation(out=qkv0_1[:, grp * 6:(grp + 1) * 6, :], in_=sl[:, 0:6, :], func=AF.Silu),
                         reads=[sk], writes=['c_qkv0_%d' % grp])
                    yield
                for grp, dst, scl in ((0, QT, 128.0 ** -0.5), (1, KT, 1.0)):
                    src = qkv0_1[:, grp * 6:(grp + 1) * 6, :]
                    P.op('act', lambda e, u=u, src=src: e.activation(out=sqt[u][:], in_=src, func=AF.Square), reads=['c_qkv0_%d' % grp], writes=[U('sqt')])
                    sl, sk = nslot()
                    P.op('pe', lambda e, u=u, sl=sl: e.matmul(sl[:, 0:4, :], lhsT=C.ones_b[:], rhs=sqt[u][:, 0:4, :], start=True, stop=True),
                         reads=['g_ones_b', U('sqt')], writes=[sk])
                    P.op('pe', lambda e, u=u, sl=sl: e.matmul(sl[:, 4:6, :], lhsT=C.ones_b[:], rhs=sqt[u][:, 4:6, :], start=True, stop=True),
                         reads=['g_ones_b', U('sqt')], writes=[sk])
                    P.op('act', lambda e, u=u, sl=sl: e.activation(out=rn[u][:], in_=sl[:, 0:6, :], func=AF.Ln, bias=C.eps_col[:, 0:1]), reads=[sk], writes=[U('rn')])
                    P.op('act', lambda e, u=u: e.activation(out=rn[u][:], in_=rn[u][:], func=AF.Exp, scale=-0.5), reads=[U('rn')], writes=[U('rn')])
                    P.op('dve', lambda e, u=u, src=src, dst=dst, scl=scl: e.scalar_tensor_tensor(out=dst[u][:], in0=src, scalar=scl, in1=rn[u][:], op0=ALU.mult, op1=ALU.mult),
                         reads=['c_qkv0_%d' % grp, U('rn')], writes=[U('QT' if grp == 0 else 'KT')])
                    yield
                P.op('act', lambda e, u=u: e.activation(out=VT[u][:], in_=qkv0_1[:, 12:18, :], func=AF.Copy), reads=['c_qkv0_2'], writes=[U('VT')])
                sl, sk = nslot()
                tsl = slice(c * 128, (c + 1) * 128)
                P.op('pe', lambda e, sl=sl, tsl=tsl: e.matmul(sl[:, 0, 0:6], lhsT=beta[:, tsl], rhs=C.ident_f[0:6, 0:6], start=True, stop=True),
                     reads=['c_beta', 'g_ident_f'], writes=[sk])
                P.op('pe', lambda e, sl=sl, tsl=tsl: e.matmul(sl[:, 0, 6:12], lhsT=gc[:, tsl], rhs=C.ident_f[0:6, 0:6], start=True, stop=True),
                     reads=['c_gc', 'g_ident_f'], writes=[sk])
                P.op('pe', lambda e, sl=sl, tsl=tsl: e.matmul(sl[:, 0, 12:18], lhsT=edl[:, tsl], rhs=C.ident_f[0:6, 0:6], start=True, stop=True),
                     reads=['c_gg', 'g_ident_f'], writes=[sk])
                P.op('pe', lambda e, sl=sl, c=c: e.matmul(sl[:, 0, 18:24], lhsT=onesf[:], rhs=egd[:, c, :], start=True, stop=True),
                     reads=['c_egd', 'c_onesf'], writes=[sk])
                P.op('dve', lambda e, u=u, sl=sl: e.tensor_copy(out=cols[u][:], in_=sl[:, 0, 0:24]), reads=[sk], writes=[U('cols')])
                yield
                sl, sk = nslot()
                for hh in range(6):
                    P.op('pe', lambda e, hh=hh, sl=sl, tsl=tsl: e.matmul(sl[:, hh, :], lhsT=sel[:, hh, :], rhs=gc[:, tsl], start=True, stop=True),
                         reads=['c_sel', 'c_gc'], writes=[sk])
                P.op('act', lambda e, u=u, sl=sl: e.activation(out=egcb[u][:], in_=sl[:, 0:6, :], func=AF.Exp), reads=[sk], writes=[U('egcb')])
                for hh in range(6):
                    P.op('dve', lambda e, u=u, hh=hh, sl=sl: e.scalar_tensor_tensor(out=tmpa[u][:, hh, :], in0=sl[:, hh, :], scalar=cols[u][:, 6 + hh:7 + hh], in1=ngm[:],
                                                                                    op0=ALU.subtract, op1=ALU.add),
                         reads=[sk, U('cols'), 'c_ngm'], writes=[U('tmpa')], acc=True)
                P.op('act', lambda e, u=u: e.activation(out=ET[u][:], in_=tmpa[u][:], func=AF.Exp), reads=[U('tmpa')], writes=[U('ET')])
                yield
                P.op('dve', lambda e, u=u: e.scalar_tensor_tensor(out=KgT[u][:], in0=KT[u][:], scalar=-1.0, in1=egcb[u][:], op0=ALU.mult, op1=ALU.mult),
                     reads=[U('KT'), U('egcb')], writes=[U('KgT')])
                P.op('pool', lambda e, u=u: e.tensor_tensor(out=QgT[u][:], in0=QT[u][:], in1=egcb[u][:], op=ALU.mult), reads=[U('QT'), U('egcb')], writes=[U('QgT')])
                sl, sk = nslot()
                for hh in range(6):
                    P.op('pe', lambda e, u=u, hh=hh, sl=sl: e.matmul(sl[:, hh, :], lhsT=KT[u][:, hh, :], rhs=KT[u][:, hh, :], start=True, stop=True),
                         reads=[U('KT')], writes=[sk])
                for hh in range(6):
                    P.op('dve', lambda e, u=u, hh=hh, sl=sl: e.scalar_tensor_tensor(out=LTb[u][:, hh, :], in0=sl[:, hh, :], scalar=cols[u][:, hh:hh + 1], in1=ET[u][:, hh, :],
                                                                                    op0=ALU.mult, op1=ALU.mult),
                         reads=[sk, U('cols'), U('ET')], writes=[U('LTb')], acc=True)
                yield
                sl, sk = nslot()
                for hh in range(6):
                    P.op('pe', lambda e, u=u, hh=hh, sl=sl: e.matmul(sl[:, hh, :], lhsT=KT[u][:, hh, :], rhs=QT[u][:, hh, :], start=True, stop=True),
                         reads=[U('KT'), U('QT')], writes=[sk])
                P.op('dve', lambda e, u=u, sl=sl: e.tensor_tensor(out=intraT[u][:], in0=sl[:, 0:6, :], in1=ET[u][:], op=ALU.mult), reads=[sk, U('ET')], writes=[U('intraT')])
                yield
                sl, sk = nslot()
                for hh in range(6):
                    P.op('pe', lambda e, u=u, hh=hh, sl=sl: e.matmul(sl[:, hh, :], lhsT=KT[u][:, hh, :], rhs=C.ident_b[:], start=True, stop=True),
                         reads=[U('KT'), 'g_ident_b'], writes=[sk])
                P.op('dve', lambda e, u=u, sl=sl: e.tensor_tensor(out=ktil[u][:], in0=sl[:, 0:6, :], in1=bc(cols[u][:, 12:18]), op=ALU.mult), reads=[sk, U('cols')], writes=[U('ktil')])
                yield
                identb3 = C.ident_b[:].unsqueeze(1).broadcast_to([128, NH, 128])

                def mask_level(lv):
                    eng = 'dve' if lv % 2 == 0 else 'pool'
                    P.op(eng, lambda e, u=u, lv=lv: e.tensor_tensor(out=BsT[u][:], in0=LTb[u][:], in1=lmb[:, lv:lv + 1, :].broadcast_to([128, NH, 128]), op=ALU.mult),
                         reads=[U('LTb'), 'c_lmb'], writes=[U('BsT')])
                mask_level(0)
                sl, sk = nslot()
                for hh in range(6):
                    P.op('pe', lambda e, u=u, hh=hh, sl=sl: e.matmul(sl[:, hh, :], lhsT=BsT[u][:, hh, :], rhs=C.ident_b[:], start=True, stop=True),
                         reads=[U('BsT'), 'g_ident_b'], writes=[sk])
                P.op('dve', lambda e, u=u, sl=sl, identb3=identb3: e.tensor_tensor(out=XmA[u][:], in0=identb3, in1=sl[:, 0:6, :], op=ALU.subtract),
                     reads=[sk, 'g_ident_b'], writes=[U('XmA')])
                P.op('pool', lambda e, u=u, identb3=identb3: e.tensor_tensor(out=XTA[u][:], in0=identb3, in1=BsT[u][:], op=ALU.subtract),
                     reads=[U('BsT'), 'g_ident_b'], writes=[U('XTA')])
                yield
                Xm, XT_, XmN, XTN = XmA, XTA, XmB, XTB
                kXm, kXT, kXmN, kXTN = 'XmA', 'XTA', 'XmB', 'XTB'
                for lv in range(1, 7):
                    last = (lv == 6)
                    mask_level(lv)
                    sl, sk = nslot()
                    for hh in range(6):
                        P.op('pe', lambda e, u=u, hh=hh, sl=sl, Xm=Xm: e.matmul(sl[:, hh, :], lhsT=BsT[u][:, hh, :], rhs=Xm[u][:, hh, :], start=True, stop=True),
                             reads=[U('BsT'), U(kXm)], writes=[sk])
                    P.op('act', lambda e, u=u, sl=sl: e.activation(out=Zsb[u][:], in_=sl[:, 0:6, :], func=AF.Copy), reads=[sk], writes=[U('Zsb')])
                    yield
                    if not last:
                        sl, sk = nslot()
                        for hh in range(6):
                            P.op('pe', lambda e, u=u, hh=hh, sl=sl, XT_=XT_: e.matmul(sl[:, hh, :], lhsT=XT_[u][:, hh, :], rhs=Zsb[u][:, hh, :], start=True, stop=True),
                                 reads=[U('Zsb'), U(kXT)], writes=[sk])
                        P.op('dve', lambda e, u=u, sl=sl, Xm=Xm, XmN=XmN: e.tensor_tensor(out=XmN[u][:], in0=Xm[u][:], in1=sl[:, 0:6, :], op=ALU.subtract),
                             reads=[sk, U(kXm)], writes=[U(kXmN)])
                    sl, sk = nslot()
                    for hh in range(6):
                        P.op('pe', lambda e, u=u, hh=hh, sl=sl, XT_=XT_: e.matmul(sl[:, hh, :], lhsT=Zsb[u][:, hh, :], rhs=XT_[u][:, hh, :], start=True, stop=True),
                             reads=[U('Zsb'), U(kXT)], writes=[sk])
                    P.op('dve', lambda e, u=u, sl=sl, XT_=XT_, XTN=XTN: e.tensor_tensor(out=XTN[u][:], in0=XT_[u][:], in1=sl[:, 0:6, :], op=ALU.subtract),
                         reads=[sk, U(kXT)], writes=[U(kXTN)])
                    yield
                    Xm, XT_, XmN, XTN = XmN, XTN, Xm, XT_
                    kXm, kXT, kXmN, kXTN = kXmN, kXTN, kXm, kXT
                assert kXT == 'XTA'

            def scan(c, u):
                U = lambda n, u=u: 'c_%s%d' % (n, u)
                c4 = c % 4
                y2 = (c // 4) % 2
                ykey = 'c_yc%d' % y2
                TT, kTT = XTA, 'XTA'
                sl, sk = nslot()
                for hh in range(6):
                    P.op('pe', lambda e, u=u, hh=hh, sl=sl: e.matmul(sl[:, hh, :], lhsT=VT[u][:, hh, :], rhs=C.ident_b[:], start=True, stop=False),
                         reads=[U('VT'), 'g_ident_b'], writes=[sk])
                    P.op('pe', lambda e, u=u, hh=hh, sl=sl: e.matmul(sl[:, hh, :], lhsT=KgT[u][:, hh, :], rhs=Sb[:, hh, :], start=False, stop=True),
                         reads=[U('KgT'), 'c_Sb'], writes=[sk])
                P.op('act', lambda e, u=u, sl=sl: e.activation(out=rsb[u][:], in_=sl[:, 0:6, :], func=AF.Copy), reads=[sk], writes=[U('rsb')])
                yield
                sl, sk = nslot()
                for hh in range(6):
                    P.op('pe', lambda e, u=u, hh=hh, sl=sl, TT=TT: e.matmul(sl[:, hh, :], lhsT=TT[u][:, hh, :], rhs=rsb[u][:, hh, :], start=True, stop=True),
                         reads=[U(kTT), U('rsb')], writes=[sk])
                P.op('dve', lambda e, u=u, sl=sl: e.tensor_tensor(out=vnew[u][:], in0=sl[:, 0:6, :], in1=bc(cols[u][:, 0:6]), op=ALU.mult), reads=[sk, U('cols')], writes=[U('vnew')])
                yield
                sl, sk = nslot()
                for hh in range(6):
                    P.op('pe', lambda e, u=u, hh=hh, sl=sl: e.matmul(sl[:, hh, :], lhsT=Sb[:, hh, :], rhs=QgT[u][:, hh, :], start=True, stop=False),
                         reads=[U('QgT'), 'c_Sb'], writes=[sk])
                    P.op('pe', lambda e, u=u, hh=hh, sl=sl: e.matmul(sl[:, hh, :], lhsT=vnew[u][:, hh, :], rhs=intraT[u][:, hh, :], start=False, stop=True),
                         reads=[U('vnew'), U('intraT')], writes=[sk])
                P.op('act', lambda e, u=u, sl=sl: e.activation(out=o32[u][:], in_=sl[:, 0:6, :], func=AF.Copy), reads=[sk], writes=[U('egcb')])
                sl, sk = nslot()
                for hh in range(6):
                    P.op('pe', lambda e, u=u, hh=hh, sl=sl: e.matmul(sl[:, hh, :], lhsT=ktil[u][:, hh, :], rhs=vnew[u][:, hh, :], start=True, stop=True),
                         reads=[U('ktil'), U('vnew')], writes=[sk])
                P.op('pool', lambda e, u=u: e.tensor_tensor(out=S32[:], in0=S32[:], in1=bc(cols[u][:, 18:24]), op=ALU.mult), reads=['c_S32', U('cols')], writes=['c_S32'])
                P.op('dve', lambda e, sl=sl: e.tensor_tensor(out=S32[:], in0=S32[:], in1=sl[:, 0:6, :], op=ALU.add), reads=[sk, 'c_S32'], writes=['c_S32'])
                P.op('act', lambda e: e.activation(out=Sb[:], in_=S32[:], func=AF.Copy), reads=['c_S32'], writes=['c_Sb'])
                yield
                P.op('act', lambda e, u=u: e.activation(out=sqt[u][:], in_=o32[u][:], func=AF.Square), reads=[U('egcb')], writes=[U('sqt')])
                sl, sk = nslot()
                P.op('pe', lambda e, u=u, sl=sl: e.matmul(sl[:, 0:4, :], lhsT=C.ones_b[:], rhs=sqt[u][:, 0:4, :], start=True, stop=True),
                     reads=['g_ones_b', U('sqt')], writes=[sk])
                P.op('pe', lambda e, u=u, sl=sl: e.matmul(sl[:, 4:6, :], lhsT=C.ones_b[:], rhs=sqt[u][:, 4:6, :], start=True, stop=True),
                     reads=['g_ones_b', U('sqt')], writes=[sk])
                P.op('act', lambda e, u=u, sl=sl: e.activation(out=ro[u][:], in_=sl[:, 0:6, :], func=AF.Ln, scale=1.0 / 128, bias=C.eps_col[:, 0:1]), reads=[sk], writes=[U('rn')])
                yield
                P.op('act', lambda e, u=u: e.activation(out=ro[u][:], in_=ro[u][:], func=AF.Exp, scale=-0.5), reads=[U('rn')], writes=[U('rn')])
                P.op('dve', lambda e, u=u: e.scalar_tensor_tensor(out=o32[u][:], in0=o32[u][:], scalar=onw[:, 0:1], in1=ro[u][:], op0=ALU.mult, op1=ALU.mult),
                     reads=[U('egcb'), U('rn'), 'c_onw'], writes=[U('egcb')])
                P.op('pool', lambda e, u=u, c4=c4, y2=y2: e.tensor_tensor(out=yc[y2][:, :, c4 * 128:(c4 + 1) * 128], in0=o32[u][:], in1=zc[y2][:, :, c4 * 128:(c4 + 1) * 128], op=ALU.mult),
                     reads=[U('egcb'), 'c_zc%d' % y2], writes=[ykey], acc=True)
                if c4 == 3:
                    t512 = s * T + (c // 4) * 512
                    P.dma('sp', C.yT[YC0:YC0 + 768, t512:t512 + 512].rearrange("(h p) t -> p h t", p=128), yc[y2][:], reads=[ykey], writes=['yT'], acc=True)
                yield

            def merge(ga, gb, ratio):
                da = db = False
                while not (da and db):
                    if not da:
                        try:
                            next(ga)
                        except StopIteration:
                            da = True
                    for _ in range(ratio):
                        if db:
                            break
                        try:
                            next(gb)
                        except StopIteration:
                            db = True

            for _ in pre(0, 0):
                pass
            for c in range(16):
                nxt = pre(c + 1, (c + 1) % 2) if c + 1 < 16 else iter(())
                merge(scan(c, c % 2), nxt, 5)


def yc_sel(yc, s, y2):
    return yc[s]


def build_program(debug=None):
    nc = bass.Bass("TRN2", target_bir_lowering=False)
    C = Ctx()
    din = lambda n, s, d: nc.dram_tensor(n, s, d, kind="ExternalInput").ap()
    C.x = din("x", [NTOK, D], F32)
    C.pos = din("pos", [NSEQ, T], I32)
    C.norm_w = din("norm_w", [DEPTH, D], F32)
    C.w_fm = din("w_fm", [DEPTH, NCH, 128, 16, 128], F32)
    C.w_ba = din("w_ba", [DEPTH, 128, 16, 12], F32)
    C.w_va = din("w_va", [DEPTH, 128, 16, 256], F32)
    C.w_o = din("w_o", [DEPTH, 128, 16, D], F32)
    C.ident_f_d = din("ident_f", [128, 128], F32)
    C.a_invf = din("a_invf", [64, 1], F32)
    C.a_rt = din("a_rt", [64, 64], F32)
    C.a_mask = din("a_mask", [128, 2, 128], F32)
    C.a_qw = din("a_qw", [DEPTH, 64, 1], F32)
    C.a_kw = din("a_kw", [DEPTH, 64, 1], F32)
    C.a_snk = din("a_snk", [DEPTH, 1, 12], F32)
    C.c_cw = din("c_cw", [DEPTH, 128, 18, 4], F32)
    C.c_onw = din("c_onw", [DEPTH, 128, 1], F32)
    C.c_alog = din("c_alog", [DEPTH, 6, 1], F32)
    C.c_dtb = din("c_dtb", [DEPTH, 6, 1], F32)
    C.c_lm = din("c_lm", [128, 7, 128], F32)
    C.c_ngm = din("c_ngm", [128, 128], F32)
    C.c_rsm = din("c_rsm", [6, T], F32)
    C.c_sel = din("c_sel", [6, 6, 128], F32)
    C.b_cw = din("b_cw", [DEPTH, 128, 4, 31], F32)
    C.b_cb = din("b_cb", [DEPTH, 128, 4], F32)
    C.b_lnw = din("b_lnw", [DEPTH, 128, 4], F32)
    C.b_lnb = din("b_lnb", [DEPTH, 128, 4], F32)
    C.b_pwb = din("b_pwb", [DEPTH, 128, 4], F32)
    C.b_pw = din("b_pw", [DEPTH, 128, 4, 512], F32)
    C.out = nc.dram_tensor("out", [NTOK, D], F32, kind="ExternalOutput").ap()
    dbg_in, dbg_out = debug[1] if debug else ((), ())
    scr = lambda n, s, d: nc.dram_tensor(n, s, d, kind=("ExternalInput" if n in dbg_in else "ExternalOutput" if n in dbg_out else "Internal")).ap()
    stages = debug[0] if debug else 'ALL'
    C.pT = scr("pT", [NFM, NTOK], BF16)
    C.baT = scr("baT", [12, NTOK], F32)
    C.va = scr("va", [NTOK, 256], BF16)
    C.yT = scr("yT", [D, NTOK], BF16)
    C.x1 = scr("x1", [NTOK, D], F32)
    names = {id(C.x.tensor): 'x_in', id(C.x1.tensor): 'x1', id(C.out.tensor): 'out'}
    C.xres_key = lambda ap: names[id(ap.tensor)]

    with ExitStack() as st:
        P = Prog(nc, st)
        SB = lambda n, s, d: st.enter_context(nc.sbuf_tensor("g_" + n, s, d))
        C.ident_f = SB("ident_f", [128, 128], F32)
        C.ident_b = SB("ident_b", [128, 128], BF16)
        C.eps_col = SB("eps_col", [128, 1], F32)
        P.dma('sp', C.ident_f[:], C.ident_f_d, writes=['g_ident_f'])
        P.op('dve', lambda e: e.tensor_copy(out=C.ident_b[:], in_=C.ident_f[:]), reads=['g_ident_f'], writes=['g_ident_b'])
        P.op('dve', lambda e: e.memset(C.eps_col[:], EPS), writes=['g_eps'])
        C.one_col = SB("one_col", [128, 1], F32)
        P.op('dve', lambda e: e.memset(C.one_col[:], 1.0), writes=['g_one'])
        C.ones_b = SB("ones_b", [128, 128], BF16)
        P.op('dve', lambda e: e.memset(C.ones_b[:], 1.0), writes=['g_ones_b'])

        P.barrier()
        if stages == 'ALL':
            srcs = [C.x, C.x1]
            dsts = [C.x1, C.out]
            for l in range(DEPTH):
                stage_inproj(P, nc, C, l, srcs[l])
                P.barrier()
                stage_attn(P, nc, C, l)
                P.barrier()
                stage_conformer(P, nc, C, l)
                P.barrier()
                stage_gdn(P, nc, C, l)
                P.barrier()
                stage_outproj(P, nc, C, l, srcs[l], dsts[l])
                P.barrier()
        elif stages == 'P':
            stage_inproj(P, nc, C, 0, C.x)
        elif stages == 'A':
            stage_attn(P, nc, C, 0)
        elif stages == 'C':
            stage_gdn(P, nc, C, 0)
        elif stages == 'B':
            stage_conformer(P, nc, C, 0)
        elif stages == 'O':
            stage_outproj(P, nc, C, 0, C.x, C.out)
        P.finish()
        P.build()
    return nc


def host_inputs(inputs, debug=None):
    w_in = np.asarray(inputs["w_in"], dtype=np.float32)
    w_out = np.asarray(inputs["w_out"], dtype=np.float32)
    cols = np.concatenate([
        np.arange(0, 768), np.arange(768, 1024), np.arange(1280, 2048),
        np.arange(2048, 2560), np.arange(2560, 3072), np.arange(3072, 3584),
        np.arange(3584, 4352), np.arange(4352, 5120), np.arange(5120, 5888),
        np.arange(5900, 6668)])
    assert cols.size == NFM
    wfm = w_in[:, :, cols].reshape(DEPTH, 16, 128, NCH, 128).transpose(0, 3, 2, 1, 4)
    wba = w_in[:, :, 5888:5900].reshape(DEPTH, 16, 128, 12).transpose(0, 2, 1, 3)
    wva = w_in[:, :, 1024:1280].reshape(DEPTH, 16, 128, 256).transpose(0, 2, 1, 3)
    wo = w_out.reshape(DEPTH, 16, 128, D).transpose(0, 2, 1, 3)
    shared = {
        "norm_w": np.ascontiguousarray(inputs["norm_w"], dtype=np.float32),
        "w_fm": np.ascontiguousarray(wfm), "w_ba": np.ascontiguousarray(wba),
        "w_va": np.ascontiguousarray(wva), "w_o": np.ascontiguousarray(wo),
        "ident_f": np.eye(128, dtype=np.float32),
    }
    f32 = lambda k: np.asarray(inputs[k], dtype=np.float32)
    shared["c_cw"] = np.ascontiguousarray(f32("c_conv_w").transpose(0, 2, 1).reshape(DEPTH, 18, 128, 4).transpose(0, 2, 1, 3))
    shared["c_onw"] = np.ascontiguousarray(f32("c_onorm_w").reshape(DEPTH, 128, 1))
    shared["c_alog"] = np.ascontiguousarray(f32("c_a_log").reshape(DEPTH, 6, 1))
    shared["c_dtb"] = np.ascontiguousarray(f32("c_dt_bias").reshape(DEPTH, 6, 1))
    jj_ = np.arange(128)[:, None]
    ii_ = np.arange(128)[None, :]
    lm = np.zeros((128, 7, 128), np.float32)
    for lv in range(7):
        bsz = 1 << lv
        lm[:, lv, :] = ((ii_ // (2 * bsz) == jj_ // (2 * bsz)) & (jj_ % (2 * bsz) < bsz) & (ii_ % (2 * bsz) >= bsz)).astype(np.float32)
    shared["c_lm"] = lm
    shared["c_ngm"] = np.where(ii_ >= jj_, 0.0, -30000.0).astype(np.float32)
    rsm = np.ones((6, T), np.float32)
    rsm[:, 0::128] = 0.0
    shared["c_rsm"] = rsm
    selm = np.zeros((6, 6, 128), np.float32)
    for hh in range(6):
        selm[hh, hh, :] = 1.0
    shared["c_sel"] = selm
    invf = np.zeros((64, 1), np.float32)
    fr = (np.float32(500000.0) ** (-np.arange(0, 16, 2, dtype=np.float32) / np.float32(16))).astype(np.float32)
    invf[0:8, 0] = fr
    invf[8:16, 0] = fr
    rt = np.zeros((64, 64), np.float32)
    for mm in range(8):
        rt[mm + 8, mm] = -1.0
        rt[mm, mm + 8] = 1.0
    jj = np.arange(128)[:, None]
    ii = np.arange(128)[None, :]
    shared["a_invf"] = invf
    shared["a_rt"] = rt
    shared["a_mask"] = np.ascontiguousarray(np.stack([(jj > ii), (jj <= ii)], axis=1).astype(np.float32))
    shared["a_qw"] = np.ascontiguousarray(f32("q_norm_w").reshape(DEPTH, 64, 1))
    shared["a_kw"] = np.ascontiguousarray(f32("k_norm_w").reshape(DEPTH, 64, 1))
    shared["a_snk"] = np.ascontiguousarray(f32("sinks").reshape(DEPTH, 1, 12))
    col4 = lambda a: np.ascontiguousarray(a.reshape(DEPTH, 4, 128).transpose(0, 2, 1))
    shared["b_cw"] = np.ascontiguousarray(f32("b_conv_w").transpose(0, 2, 1).reshape(DEPTH, 4, 128, 31).transpose(0, 2, 1, 3))
    shared["b_cb"] = col4(f32("b_conv_b"))
    shared["b_lnw"] = col4(f32("b_ln_w"))
    shared["b_lnb"] = col4(f32("b_ln_b"))
    shared["b_pwb"] = col4(f32("b_pw_b"))
    shared["b_pw"] = np.ascontiguousarray(f32("b_pw_w").reshape(DEPTH, 4, 128, 512).transpose(0, 2, 1, 3))
    x = np.asarray(inputs["x"], dtype=np.float32)
    pos = np.asarray(inputs["positions"], dtype=np.int32)
    in_maps = []
    for c in range(NCORES):
        m = dict(shared)
        m["x"] = np.ascontiguousarray(x[c * NSEQ:(c + 1) * NSEQ].reshape(NTOK, D))
        m["pos"] = np.ascontiguousarray(pos[c * NSEQ:(c + 1) * NSEQ])
        in_maps.append(m)
    return in_maps


def kernel(**inputs):
    in_maps = host_inputs(inputs)
    nc = build_program()
    res = run_bass_kernel_spmd(nc, in_maps, core_ids=list(range(NCORES)))
    out = np.stack([r["out"].reshape(NSEQ, T, D) for r in res.results], axis=0)
    return out.reshape(NCORES * NSEQ, T, D).astype(np.float32)
```

```python
import numpy as np
from contextlib import ExitStack
import concourse.bass as bass
import concourse.mybir as mybir
from concourse.bass_utils import run_bass_kernel_spmd

F32 = mybir.dt.float32
BF16 = mybir.dt.bfloat16
I32 = mybir.dt.int32
AF = mybir.ActivationFunctionType
ALU = mybir.AluOpType

ENG = ['pe', 'dve', 'act', 'pool', 'sp']
EPOCH = 30000

NCORES = 8
DEPTH = 2
D = 2048
T = 2048
NSEQ = 2
NTOK = NSEQ * T
EPS = 1e-6
Q0, K0, ZA0, UBA0, UBG0, ZB0, CQ0, CK0, CV0, ZC0 = 0, 768, 1024, 1792, 2304, 2816, 3328, 4096, 4864, 5632
NFM = 6400
NCH = NFM // 128
YA0, YB0, YC0 = 0, 768, 1280


class Prog:
    def __init__(self, nc, stack):
        self.nc = nc
        self.stack = stack
        self.streams = {e: [] for e in ENG}
        self.count = {e: 0 for e in ENG}
        self.sems = {}
        self.seen = {e: {} for e in ENG}
        self.res = {}
        self.dmacnt = {}
        self.excl = set()

    def sem(self, key):
        if key not in self.sems:
            self.sems[key] = self.stack.enter_context(
                self.nc.semaphore("s%d" % len(self.sems)))
        return self.sems[key]

    def _deps(self, reads, writes, acc):
        deps = {}

        def add(d):
            for k, v in d.items():
                if deps.get(k, 0) < v:
                    deps[k] = v
        for r in reads:
            st = self.res.get(r)
            if st:
                add(st[0])
                if r in self.excl:
                    add(st[1])
        for w in writes:
            st = self.res.get(w)
            if st:
                add(st[1])
                if not acc:
                    add(st[0])
        return deps

    def _emit_waits(self, eng, deps):
        for k, v in deps.items():
            if self.seen[eng].get(k, 0) >= v:
                continue
            self.seen[eng][k] = v
            h = self.sem(k)
            self.streams[eng].append(lambda e, h=h, v=v: e.wait_ge(h, v))

    def _commit(self, reads, writes, acc, key, val):
        for r in reads:
            st = self.res.setdefault(r, [{}, {}])
            if st[1].get(key, 0) < val:
                st[1][key] = val
        for w in writes:
            st = self.res.setdefault(w, [{}, {}])
            if acc:
                if st[0].get(key, 0) < val:
                    st[0][key] = val
            else:
                st[0] = {key: val}
                st[1] = {}

    def op(self, eng, fn, reads=(), writes=(), acc=False):
        deps = self._deps(reads, writes, acc)
        if eng == 'pe':
            deps = {k: v for k, v in deps.items() if k[0] != 'pe'}
        n = self.count[eng]
        key = (eng, n // EPOCH)
        val = n % EPOCH + 1
        self._emit_waits(eng, deps)
        h = self.sem(key)
        self.streams[eng].append(lambda e, fn=fn, h=h: fn(e).then_inc(h, 1))
        self.count[eng] = n + 1
        self._commit(reads, writes, acc, key, val)

    def dma(self, q, out, in_, reads=(), writes=(), acc=False, semkey=None, **kw):
        assert len(writes) == 1
        deps = self._deps(reads, writes, acc)
        self._emit_waits(q, deps)
        key = ('dma', semkey if semkey is not None else (reads[0] if acc else writes[0]))
        cnt = self.dmacnt.get(key, 0) + 16
        self.dmacnt[key] = cnt
        h = self.sem(key)
        self.streams[q].append(
            lambda e, out=out, in_=in_, h=h, kw=kw: e.dma_start(out=out, in_=in_, **kw).then_inc(h, 16))
        self._commit(reads, writes, acc, key, cnt)

    def barrier(self):
        deps = {}
        for k, c in self.dmacnt.items():
            deps[k] = c
        for e in ENG:
            n = self.count[e]
            if n > 0:
                deps[(e, (n - 1) // EPOCH)] = (n - 1) % EPOCH + 1
        for e in ENG:
            self._emit_waits(e, dict(deps))

    def finish(self):
        deps = {}
        for k, c in self.dmacnt.items():
            deps[k] = c
        for e in ENG:
            n = self.count[e]
            if n > 0:
                deps[(e, (n - 1) // EPOCH)] = (n - 1) % EPOCH + 1
        self._emit_waits('sp', deps)

    def build(self):
        block = self.stack.enter_context(self.nc.Block())
        streams = self.streams

        @block.tensor
        def _(e):
            for f in streams['pe']:
                f(e)

        @block.vector
        def _(e):
            for f in streams['dve']:
                f(e)

        @block.scalar
        def _(e):
            for f in streams['act']:
                f(e)

        @block.gpsimd
        def _(e):
            for f in streams['pool']:
                f(e)

        @block.sync
        def _(e):
            for f in streams['sp']:
                f(e)


class Ctx:
    pass


def stage_inproj(P, nc, C, l, x_src):
    with ExitStack() as st:
        SB = lambda n, s, d: st.enter_context(nc.sbuf_tensor("p_" + str(l) + n, s, d))
        PS = lambda n, s, d: st.enter_context(nc.psum_tensor("pp_" + str(l) + n, s, d))
        hT = SB("hT", [128, 16, T], BF16)
        nw = SB("nw", [128, D], F32)
        xs = [SB("xs%d" % i, [128, D], F32) for i in range(2)]
        hb = [SB("hb%d" % i, [128, D], BF16) for i in range(2)]
        junk = SB("junk", [128, D], BF16)
        ss = [SB("ss%d" % i, [128, 1], F32) for i in range(2)]
        rs = [SB("rs%d" % i, [128, 1], F32) for i in range(2)]
        wf = [SB("wf%d" % i, [128, 16, 128], F32) for i in range(2)]
        wb = [SB("wb%d" % i, [128, 16, 128], BF16) for i in range(3)]
        wbaf = SB("wbaf", [128, 16, 12], F32)
        wbab = SB("wbab", [128, 16, 12], BF16)
        wvf = SB("wvf", [128, 16, 256], F32)
        wvb = SB("wvb", [128, 16, 256], BF16)
        ob = [SB("ob%d" % i, [128, T], BF16) for i in range(2)]
        obf = SB("obf", [12, T], F32)
        ovb = [SB("ovb%d" % i, [128, 256], BF16) for i in range(2)]
        ptr = [PS("tr%d" % i, [128, 16, 128], BF16) for i in range(2)]
        pmm = [PS("mm%d" % i, [128, 512], F32) for i in range(4)]

        P.dma('sp', nw[:], C.norm_w[l:l + 1, :].broadcast_to([128, D]), writes=['p_nw'])
        def wload(c):
            P.dma('sp', wf[c % 2][:], C.w_fm[l, c], writes=['p_wf%d' % (c % 2)])

        def wcast(c):
            P.op('pool', lambda e, c=c: e.tensor_copy(out=wb[c % 3][:], in_=wf[c % 2][:]),
                 reads=['p_wf%d' % (c % 2)], writes=['p_wb%d' % (c % 3)])

        for s in range(NSEQ):
            wload(0)
            wcast(0)
            wload(1)
            P.dma('sp', wbaf[:], C.w_ba[l], writes=['p_wbaf'])
            P.op('pool', lambda e: e.tensor_copy(out=wbab[:], in_=wbaf[:]), reads=['p_wbaf'], writes=['p_wbab'])
            P.dma('sp', wvf[:], C.w_va[l], writes=['p_wvf'])
            P.op('pool', lambda e: e.tensor_copy(out=wvb[:], in_=wvf[:]), reads=['p_wvf'], writes=['p_wvb'])
            for i in range(16):
                b = i % 2
                tok0 = s * T + i * 128
                P.dma('sp', xs[b][:], x_src[tok0:tok0 + 128, :], writes=['p_xs%d' % b])
                P.op('act', lambda e, b=b: e.activation(out=junk[:], in_=xs[b][:], func=AF.Square, accum_out=ss[b][:]),
                     reads=['p_xs%d' % b], writes=['p_junk', 'p_ss%d' % b])
                P.op('act', lambda e, b=b: e.activation(out=rs[b][:], in_=ss[b][:], func=AF.Sqrt, scale=1.0 / D, bias=C.eps_col[:, 0:1]),
                     reads=['p_ss%d' % b], writes=['p_rs%d' % b])
                P.op('dve', lambda e, b=b: e.reciprocal(out=rs[b][:], in_=rs[b][:]),
                     reads=['p_rs%d' % b], writes=['p_rs%d' % b])
                P.op('dve', lambda e, b=b: e.scalar_tensor_tensor(out=hb[b][:], in0=xs[b][:], scalar=rs[b][:, 0:1], in1=nw[:],
                                                                  op0=ALU.mult, op1=ALU.mult),
                     reads=['p_xs%d' % b, 'p_rs%d' % b, 'p_nw'], writes=['p_hb%d' % b])
                for kc in range(16):
                    P.op('pe', lambda e, b=b, kc=kc: e.transpose(out=ptr[b][:, kc, :], in_=hb[b][:, kc * 128:(kc + 1) * 128], identity=C.ident_b[:]),
                         reads=['p_hb%d' % b], writes=['p_ptr%d' % b])
                ev = 'act' if i % 2 == 0 else 'dve'
                if ev == 'act':
                    P.op('act', lambda e, b=b, i=i: e.activation(out=hT[:, :, i * 128:(i + 1) * 128], in_=ptr[b][:], func=AF.Copy),
                         reads=['p_ptr%d' % b], writes=['p_hT'], acc=True)
                else:
                    P.op('dve', lambda e, b=b, i=i: e.tensor_copy(out=hT[:, :, i * 128:(i + 1) * 128], in_=ptr[b][:]),
                         reads=['p_ptr%d' % b], writes=['p_hT'], acc=True)
            nmm = 0
            for c in range(NCH):
                w3 = c % 3
                if c + 2 < NCH:
                    wload(c + 2)
                if c + 1 < NCH:
                    wcast(c + 1)
                o2 = c % 2
                for tt in range(4):
                    pb = nmm % 4
                    nmm += 1
                    for kc in range(16):
                        P.op('pe', lambda e, w3=w3, kc=kc, tt=tt, pb=pb: e.matmul(pmm[pb][:], lhsT=wb[w3][:, kc, :], rhs=hT[:, kc, tt * 512:(tt + 1) * 512],
                                                                                start=(kc == 0), stop=(kc == 15)),
                             reads=['p_wb%d' % w3, 'p_hT'], writes=['p_mm%d' % pb])
                    if tt % 2 == 0:
                        P.op('act', lambda e, o2=o2, tt=tt, pb=pb: e.activation(out=ob[o2][:, tt * 512:(tt + 1) * 512], in_=pmm[pb][:], func=AF.Copy),
                             reads=['p_mm%d' % pb], writes=['p_ob%d' % o2], acc=True)
                    else:
                        P.op('dve', lambda e, o2=o2, tt=tt, pb=pb: e.tensor_copy(out=ob[o2][:, tt * 512:(tt + 1) * 512], in_=pmm[pb][:]),
                             reads=['p_mm%d' % pb], writes=['p_ob%d' % o2], acc=True)
                P.dma('sp', C.pT[c * 128:(c + 1) * 128, s * T:(s + 1) * T], ob[o2][:], reads=['p_ob%d' % o2], writes=['pT'], acc=True)
            for tt in range(4):
                pb = nmm % 4
                nmm += 1
                for kc in range(16):
                    P.op('pe', lambda e, kc=kc, tt=tt, pb=pb: e.matmul(pmm[pb][0:12, :], lhsT=wbab[:, kc, :], rhs=hT[:, kc, tt * 512:(tt + 1) * 512],
                                                                        start=(kc == 0), stop=(kc == 15)),
                         reads=['p_wbab', 'p_hT'], writes=['p_mm%d' % pb])
                P.op('dve', lambda e, tt=tt, pb=pb: e.tensor_copy(out=obf[:, tt * 512:(tt + 1) * 512], in_=pmm[pb][0:12, :]),
                     reads=['p_mm%d' % pb], writes=['p_obf'], acc=True)
            P.dma('sp', C.baT[:, s * T:(s + 1) * T], obf[:], reads=['p_obf'], writes=['baT'], acc=True)
            for i in range(16):
                pb = nmm % 4
                nmm += 1
                o2 = i % 2
                for kc in range(16):
                    P.op('pe', lambda e, kc=kc, i=i, pb=pb: e.matmul(pmm[pb][:, 0:256], lhsT=hT[:, kc, i * 128:(i + 1) * 128], rhs=wvb[:, kc, :],
                                                                      start=(kc == 0), stop=(kc == 15)),
                         reads=['p_wvb', 'p_hT'], writes=['p_mm%d' % pb])
                P.op('dve', lambda e, o2=o2, pb=pb: e.tensor_copy(out=ovb[o2][:], in_=pmm[pb][:, 0:256]),
                     reads=['p_mm%d' % pb], writes=['p_ovb%d' % o2])
                tok0 = s * T + i * 128
                P.dma('sp', C.va[tok0:tok0 + 128, :], ovb[o2][:], reads=['p_ovb%d' % o2], writes=['va'], acc=True)


def stage_outproj(P, nc, C, l, x_src, x_dst):
    with ExitStack() as st:
        SB = lambda n, s, d: st.enter_context(nc.sbuf_tensor("o_" + str(l) + n, s, d))
        PS = lambda n, s, d: st.enter_context(nc.psum_tensor("op_" + str(l) + n, s, d))
        wo = SB("wo", [128, 16, D], BF16)
        wf = [SB("wf%d" % i, [128, D], F32) for i in range(2)]
        yt = [SB("yt%d" % i, [128, 16, 128], BF16) for i in range(2)]
        xs = [SB("xs%d" % i, [128, D], F32) for i in range(2)]
        xo = [SB("xo%d" % i, [128, D], F32) for i in range(2)]
        pmm = [PS("mm%d" % i, [128, 512], F32) for i in range(8)]
        for kc in range(16):
            f = kc % 2
            P.dma('sp', wf[f][:], C.w_o[l, :, kc, :], writes=['o_wf%d' % f])
            P.op('pool' if kc % 2 == 0 else 'act',
                 (lambda e, f=f, kc=kc: e.tensor_copy(out=wo[:, kc, :], in_=wf[f][:])) if kc % 2 == 0 else
                 (lambda e, f=f, kc=kc: e.activation(out=wo[:, kc, :], in_=wf[f][:], func=AF.Copy)),
                 reads=['o_wf%d' % f], writes=['o_wo'], acc=True)
        def oload(i):
            b = i % 2
            tok0 = i * 128
            P.dma('sp', yt[b][:], C.yT[:, tok0:tok0 + 128].rearrange("(kc p) t -> p kc t", p=128), reads=['yT'], writes=['o_yt%d' % b])
            P.dma('sp', xs[b][:], x_src[tok0:tok0 + 128, :], reads=[C.xres_key(x_src)], writes=['o_xs%d' % b])

        nmm = 0
        NT = NTOK // 128
        oload(0)
        for i in range(NT):
            b = i % 2
            tok0 = i * 128
            if i + 1 < NT:
                oload(i + 1)
            for ct in range(4):
                pb = nmm % 8
                nmm += 1
                for kc in range(16):
                    P.op('pe', lambda e, b=b, kc=kc, ct=ct, pb=pb: e.matmul(pmm[pb][:], lhsT=yt[b][:, kc, :], rhs=wo[:, kc, ct * 512:(ct + 1) * 512],
                                                                            start=(kc == 0), stop=(kc == 15)),
                         reads=['o_yt%d' % b, 'o_wo'], writes=['o_mm%d' % pb])
                P.op('dve', lambda e, b=b, ct=ct, pb=pb: e.tensor_tensor(out=xo[b][:, ct * 512:(ct + 1) * 512], in0=xs[b][:, ct * 512:(ct + 1) * 512],
                                                                         in1=pmm[pb][:], op=ALU.add),
                     reads=['o_mm%d' % pb, 'o_xs%d' % b], writes=['o_xo%d' % b], acc=True)
            P.dma('sp', x_dst[tok0:tok0 + 128, :], xo[b][:], reads=['o_xo%d' % b], writes=[C.xres_key(x_dst)], acc=True)


def stage_conformer(P, nc, C, l):
    with ExitStack() as st:
        SB = lambda n, s, d: st.enter_context(nc.sbuf_tensor("sb_" + str(l) + n, s, d))
        PS = lambda n, s, d: st.enter_context(nc.psum_tensor("bp_" + str(l) + n, s, d))
        cw = SB("cw", [128, 4, 31], F32)
        cb = SB("cb", [128, 4], F32)
        lnw = SB("lnw", [128, 4], F32)
        lnb = SB("lnb", [128, 4], F32)
        pwbias = SB("pwbias", [128, 4], F32)
        pwf = SB("pwf", [128, 4, 512], F32)
        pwb = SB("pwb", [128, 4, 512], BF16)
        dg = SB("dg", [128, 124, 128], BF16)
        W = 542
        at = [SB("at%d" % i, [128, 4, W], BF16) for i in range(2)]
        gt = [SB("gt%d" % i, [128, 4, W], BF16) for i in range(2)]
        zt = [SB("zt%d" % i, [128, 4, 512], BF16) for i in range(2)]
        sig = SB("sig", [128, 4, W], BF16)
        hg = [SB("hg%d" % i, [128, 4, W], BF16) for i in range(2)]
        xc = [SB("xc%d" % i, [128, 4, 512], F32) for i in range(2)]
        xcb = [SB("xcb%d" % i, [128, 4, 512], BF16) for i in range(2)]
        sqb = [SB("sqb%d" % i, [128, 4, 512], BF16) for i in range(2)]
        mean = SB("mean", [128, 512], F32)
        m2 = SB("m2", [128, 512], F32)
        var = SB("var", [128, 512], F32)
        rstd = SB("rstd", [128, 512], F32)
        t1 = SB("t1", [128, 4, 512], F32)
        hs = SB("hs", [128, 4, 512], BF16)
        sz = SB("sz", [128, 4, 512], BF16)
        yb = [SB("yb%d" % i, [128, 4, 512], BF16) for i in range(2)]
        pc = [PS("c%d" % i, [128, 512], F32) for i in range(4)]
        pm = PS("m", [128, 512], F32)
        pq = PS("q", [128, 512], F32)
        po = [PS("o%d" % i, [128, 512], F32) for i in range(2)]
        P.excl.update(['b_pc%d' % i for i in range(4)] + ['b_pm', 'b_pq', 'b_po0', 'b_po1'])

        P.dma('sp', cw[:], C.b_cw[l], writes=['b_cw'])
        P.dma('sp', cb[:], C.b_cb[l], writes=['b_cb'])
        P.dma('sp', lnw[:], C.b_lnw[l], writes=['b_lnw'])
        P.dma('sp', lnb[:], C.b_lnb[l], writes=['b_lnb'])
        P.dma('sp', pwbias[:], C.b_pwb[l], writes=['b_pwbias'])
        P.dma('sp', pwf[:], C.b_pw[l], writes=['b_pwf'])
        P.op('dve', lambda e: e.tensor_copy(out=pwb[:], in_=pwf[:]), reads=['b_pwf'], writes=['b_pwb'])
        for c in range(4):
            for k in range(31):
                eng = 'pool' if (c * 31 + k) % 3 == 2 else 'dve'
                P.op(eng, lambda e, c=c, k=k: e.tensor_scalar(out=dg[:, c * 31 + k, :], in0=C.ident_f[:], scalar1=cw[:, c, k:k + 1], scalar2=None, op0=ALU.mult),
                     reads=['b_cw', 'g_ident_f'], writes=['b_dg'], acc=True)

        tiles = [(s, tt) for s in range(NSEQ) for tt in range(4)]

        def front(i):
            s, tt = tiles[i]
            b = i % 2
            tok0 = s * T + tt * 512
            if tt == 0:
                P.op('pool', lambda e, b=b: e.memset(at[b][:, :, 0:30], 0.0), writes=['b_at%d' % b])
                P.op('pool', lambda e, b=b: e.memset(gt[b][:, :, 0:30], 0.0), writes=['b_gt%d' % b])
                P.dma('sp', at[b][:, :, 30:W], C.pT[UBA0:UBA0 + 512, tok0:tok0 + 512].rearrange("(c p) t -> p c t", p=128), reads=['pT'], writes=['b_at%d' % b])
                P.dma('sp', gt[b][:, :, 30:W], C.pT[UBG0:UBG0 + 512, tok0:tok0 + 512].rearrange("(c p) t -> p c t", p=128), reads=['pT'], writes=['b_gt%d' % b])
            else:
                P.dma('sp', at[b][:], C.pT[UBA0:UBA0 + 512, tok0 - 30:tok0 + 512].rearrange("(c p) t -> p c t", p=128), reads=['pT'], writes=['b_at%d' % b])
                P.dma('sp', gt[b][:], C.pT[UBG0:UBG0 + 512, tok0 - 30:tok0 + 512].rearrange("(c p) t -> p c t", p=128), reads=['pT'], writes=['b_gt%d' % b])
            P.dma('sp', zt[b][:], C.pT[ZB0:ZB0 + 512, tok0:tok0 + 512].rearrange("(c p) t -> p c t", p=128), reads=['pT'], writes=['b_zt%d' % b])
            P.op('act', lambda e, b=b: e.activation(out=sig[:], in_=gt[b][:], func=AF.Sigmoid), reads=['b_gt%d' % b], writes=['b_sig'])
            P.op('dve', lambda e, b=b: e.tensor_tensor(out=hg[b][:], in0=at[b][:], in1=sig[:], op=ALU.mult), reads=['b_at%d' % b, 'b_sig'], writes=['b_hg%d' % b])

        def conv(i):
            b = i % 2
            for c in range(4):
                for k in range(31):
                    P.op('pe', lambda e, c=c, k=k, b=b: e.matmul(pc[c][:], lhsT=dg[:, c * 31 + k, :], rhs=hg[b][:, c, k:k + 512], start=(k == 0), stop=(k == 30)),
                         reads=['b_dg', 'b_hg%d' % b], writes=['b_pc%d' % c])
                P.op('act', lambda e, c=c, b=b: e.activation(out=xc[b][:, c, :], in_=pc[c][:], func=AF.Identity, bias=cb[:, c:c + 1]),
                     reads=['b_pc%d' % c, 'b_cb'], writes=['b_xc%d' % b], acc=True)
                P.op('dve', lambda e, c=c, b=b: e.tensor_scalar(out=xcb[b][:, c, :], in0=pc[c][:], scalar1=cb[:, c:c + 1], scalar2=None, op0=ALU.add),
                     reads=['b_pc%d' % c, 'b_cb'], writes=['b_xcb%d' % b], acc=True)
                P.op('act', lambda e, c=c, b=b: e.activation(out=sqb[b][:, c, :], in_=pc[c][:], func=AF.Square, bias=cb[:, c:c + 1]),
                     reads=['b_pc%d' % c, 'b_cb'], writes=['b_sqb%d' % b], acc=True)

        def back(i):
            s, tt = tiles[i]
            b = i % 2
            tok0 = s * T + tt * 512
            for c in range(4):
                P.op('pe', lambda e, c=c, b=b: e.matmul(pm[:], lhsT=C.ones_b[:], rhs=xcb[b][:, c, :], start=(c == 0), stop=(c == 3)),
                     reads=['b_xcb%d' % b, 'g_ones_b'], writes=['b_pm'])
            for c in range(4):
                P.op('pe', lambda e, c=c, b=b: e.matmul(pq[:], lhsT=C.ones_b[:], rhs=sqb[b][:, c, :], start=(c == 0), stop=(c == 3)),
                     reads=['b_sqb%d' % b, 'g_ones_b'], writes=['b_pq'])
            P.op('act', lambda e: e.activation(out=mean[:], in_=pm[:], func=AF.Copy, scale=1.0 / 512), reads=['b_pm'], writes=['b_mean'])
            P.op('pool', lambda e: e.tensor_tensor(out=m2[:], in0=mean[:], in1=mean[:], op=ALU.mult), reads=['b_mean'], writes=['b_m2'])
            P.op('dve', lambda e: e.scalar_tensor_tensor(out=var[:], in0=pq[:], scalar=1.0 / 512, in1=m2[:], op0=ALU.mult, op1=ALU.subtract),
                 reads=['b_pq', 'b_m2'], writes=['b_var'])
            P.op('act', lambda e: e.activation(out=rstd[:], in_=var[:], func=AF.Ln, bias=C.eps_col[:, 0:1]), reads=['b_var'], writes=['b_rstd'])
            P.op('act', lambda e: e.activation(out=rstd[:], in_=rstd[:], func=AF.Exp, scale=-0.5), reads=['b_rstd'], writes=['b_rstd'])
            P.op('act', lambda e, b=b: e.activation(out=sz[:], in_=zt[b][:], func=AF.Silu), reads=['b_zt%d' % b], writes=['b_sz'])
            for c in range(4):
                P.op('pool', lambda e, c=c, b=b: e.tensor_tensor(out=t1[:, c, :], in0=xc[b][:, c, :], in1=mean[:], op=ALU.subtract),
                     reads=['b_xc%d' % b, 'b_mean'], writes=['b_t1_%d' % c])
                P.op('dve', lambda e, c=c: e.tensor_tensor(out=t1[:, c, :], in0=t1[:, c, :], in1=rstd[:], op=ALU.mult),
                     reads=['b_t1_%d' % c, 'b_rstd'], writes=['b_t1_%d' % c])
                P.op('act', lambda e, c=c: e.activation(out=hs[:, c, :], in_=t1[:, c, :], func=AF.Silu, scale=lnw[:, c:c + 1], bias=lnb[:, c:c + 1]),
                     reads=['b_t1_%d' % c, 'b_lnw', 'b_lnb'], writes=['b_hs'], acc=True)
            for co in range(4):
                o2 = co % 2
                for ci in range(4):
                    P.op('pe', lambda e, co=co, ci=ci, o2=o2: e.matmul(po[o2][:], lhsT=pwb[:, ci, co * 128:(co + 1) * 128], rhs=hs[:, ci, :], start=(ci == 0), stop=(ci == 3)),
                         reads=['b_pwb', 'b_hs'], writes=['b_po%d' % o2])
                P.op('dve', lambda e, co=co, b=b, o2=o2: e.scalar_tensor_tensor(out=yb[b][:, co, :], in0=po[o2][:], scalar=pwbias[:, co:co + 1], in1=sz[:, co, :],
                                                                                op0=ALU.add, op1=ALU.mult),
                     reads=['b_po%d' % o2, 'b_pwbias', 'b_sz'], writes=['b_yb%d' % b], acc=True)
            P.dma('sp', C.yT[YB0:YB0 + 512, tok0:tok0 + 512].rearrange("(c p) t -> p c t", p=128), yb[b][:], reads=['b_yb%d' % b], writes=['yT'], acc=True)

        NT = len(tiles)
        front(0)
        conv(0)
        for i in range(NT):
            if i + 1 < NT:
                front(i + 1)
                conv(i + 1)
            back(i)


def stage_attn(P, nc, C, l):
    TWO_PI = 6.283185307179586
    C1 = 6.28125
    C2 = TWO_PI - C1
    MAGIC = 12582912.0
    with ExitStack() as st:
        SB = lambda n, s, d: st.enter_context(nc.sbuf_tensor("sa_" + str(l) + n, s, d))
        PS = lambda n, s, d: st.enter_context(nc.psum_tensor("ap_" + str(l) + n, s, d))
        invf = SB("invf", [64, 1], F32)
        rtf = SB("rtf", [64, 64], F32)
        rtb = SB("rtb", [64, 64], BF16)
        mkf = SB("mkf", [128, 2, 128], F32)
        mkb = SB("mkb", [128, 2, 128], BF16)
        qw = SB("qw", [64, 1], F32)
        kw = SB("kw", [64, 1], F32)
        snk = SB("snk", [1, 12], F32)
        snkb = SB("snkb", [1, 12, 128], BF16)
        posi = SB("posi", [64, T], I32)
        cosT = SB("cosT", [64, T], F32)
        sinT = SB("sinT", [64, T], F32)
        sq = SB("sq", [64, T], BF16)
        rstd = SB("rstd", [64, T], F32)
        xn = SB("xn", [64, T], BF16)
        t1 = SB("t1", [64, T], F32)
        t2 = SB("t2", [64, T], F32)
        ang, tn, tr = rstd, t1, t2
        kANG, kTN, kTR = 'a_rstd', 'a_t1', 'a_t2'
        kT = [SB("kT%d" % i, [64, T], BF16) for i in range(2)]
        qT = [SB("qT%d" % i, [64, 3, T], BF16) for i in range(2)]
        zaT = [SB("zaT%d" % i, [64, 3, T], BF16) for i in range(2)]
        vt = [SB("vt%d" % i, [128, 16, 64], BF16) for i in range(2)]
        kr = [SB("kr%d" % i, [64, T], BF16) for i in range(2)]
        qr = [SB("qr%d" % i, [64, 3, T], BF16) for i in range(2)]
        ya = [SB("ya%d" % i, [64, 3, T], BF16) for i in range(2)]
        pe_sb = [SB("pe%d" % i, [128, 2, 384], BF16) for i in range(2)]
        pm_sb = [SB("pm%d" % i, [128, 2, 384], BF16) for i in range(2)]
        rden = SB("rden", [64, 384], F32)
        o_sb = SB("o_sb", [64, 384], F32)
        ps = [PS("b%d" % i, [128, 512], F32) for i in range(8)]
        PK = ['a_ps%d' % i for i in range(8)]
        P.excl.update(PK)
        P.dma('sp', invf[:], C.a_invf, writes=['a_invf'])
        P.dma('sp', rtf[:], C.a_rt, writes=['a_rtf'])
        P.dma('sp', mkf[:], C.a_mask, writes=['a_mkf'])
        P.dma('sp', qw[:], C.a_qw[l], writes=['a_qw'])
        P.dma('sp', kw[:], C.a_kw[l], writes=['a_kw'])
        P.dma('sp', snk[:], C.a_snk[l], writes=['a_snk'])
        P.op('dve', lambda e: e.tensor_copy(out=rtb[:], in_=rtf[:]), reads=['a_rtf'], writes=['a_rtb'])
        P.op('dve', lambda e: e.tensor_copy(out=mkb[:], in_=mkf[:]), reads=['a_mkf'], writes=['a_mkb'])
        P.op('act', lambda e: e.activation(out=snk[:], in_=snk[:], func=AF.Exp), reads=['a_snk'], writes=['a_snk'])
        P.op('dve', lambda e: e.tensor_copy(out=snkb[:], in_=snk[:, :].unsqueeze(2).broadcast_to([1, 12, 128])), reads=['a_snk'], writes=['a_snkb'])

        def sincos(dst, shift):
            P.op('dve', lambda e: e.tensor_scalar(out=tn[:], in0=ang[:], scalar1=shift, scalar2=1.0 / TWO_PI, op0=ALU.add, op1=ALU.mult),
                 reads=[kANG], writes=[kTN])
            P.op('dve', lambda e: e.tensor_scalar(out=tn[:], in0=tn[:], scalar1=MAGIC, scalar2=None, op0=ALU.add), reads=[kTN], writes=[kTN])
            P.op('dve', lambda e: e.tensor_scalar(out=tn[:], in0=tn[:], scalar1=-MAGIC, scalar2=None, op0=ALU.add), reads=[kTN], writes=[kTN])
            P.op('dve', lambda e: e.scalar_tensor_tensor(out=tr[:], in0=tn[:], scalar=-C1, in1=ang[:], op0=ALU.mult, op1=ALU.add),
                 reads=[kTN, kANG], writes=[kTR])
            P.op('dve', lambda e: e.tensor_scalar(out=tr[:], in0=tr[:], scalar1=shift, scalar2=None, op0=ALU.add), reads=[kTR], writes=[kTR])
            P.op('dve', lambda e: e.scalar_tensor_tensor(out=tr[:], in0=tn[:], scalar=-C2, in1=tr[:], op0=ALU.mult, op1=ALU.add),
                 reads=[kTN, kTR], writes=[kTR])
            P.op('dve', lambda e: e.tensor_scalar(out=tr[:], in0=tr[:], scalar1=3.1415925, scalar2=-3.1415925, op0=ALU.min, op1=ALU.max),
                 reads=[kTR], writes=[kTR])
            P.op('act', lambda e: e.activation(out=dst[:], in_=tr[:], func=AF.Sin), reads=[kTR], writes=['a_tab'])

        def tables(s):
            P.dma('sp', posi[:], C.pos[s:s + 1, :].broadcast_to([64, T]), writes=['a_posi'])
            P.op('dve', lambda e: e.tensor_copy(out=ang[:], in_=posi[:]), reads=['a_posi'], writes=[kANG])
            P.op('dve', lambda e: e.tensor_scalar(out=ang[:], in0=ang[:], scalar1=invf[:, 0:1], scalar2=None, op0=ALU.mult), reads=[kANG, 'a_invf'], writes=[kANG])
            sincos(sinT, 0.0)
            sincos(cosT, 1.5707963267948966)

        def normrope(src, wcol, wkey, dst, skey, dkey):
            NB = 7
            P.op('act', lambda e: e.activation(out=sq[:], in_=src, func=AF.Square), reads=[skey], writes=['a_sq'])
            for q4 in range(4):
                P.op('pe', lambda e, q4=q4: e.matmul(ps[NB][0:64, :], lhsT=C.ones_b[0:64, 0:64], rhs=sq[:, q4 * 512:(q4 + 1) * 512], start=True, stop=True),
                     reads=['a_sq', 'g_ones_b'], writes=[PK[NB]])
                P.op('act', lambda e, q4=q4: e.activation(out=rstd[:, q4 * 512:(q4 + 1) * 512], in_=ps[NB][0:64, :], func=AF.Ln, scale=1.0 / 64, bias=C.eps_col[0:64, 0:1]),
                     reads=[PK[NB]], writes=['a_rstd'], acc=True)
            yield
            P.op('act', lambda e: e.activation(out=rstd[:], in_=rstd[:], func=AF.Exp, scale=-0.5), reads=['a_rstd'], writes=['a_rstd'])
            P.op('dve', lambda e: e.scalar_tensor_tensor(out=xn[:], in0=src, scalar=wcol[:, 0:1], in1=rstd[:], op0=ALU.mult, op1=ALU.mult),
                 reads=[skey, wkey, 'a_rstd'], writes=['a_xn'])
            yield
            P.op('pool', lambda e: e.tensor_tensor(out=t1[:], in0=xn[:], in1=cosT[:], op=ALU.mult), reads=['a_xn', 'a_tab'], writes=['a_t1'])
            for q4 in range(4):
                P.op('pe', lambda e, q4=q4: e.matmul(ps[NB][0:64, :], lhsT=rtb[:], rhs=xn[:, q4 * 512:(q4 + 1) * 512], start=True, stop=True),
                     reads=['a_xn', 'a_rtb'], writes=[PK[NB]])
                P.op('dve', lambda e, q4=q4: e.tensor_tensor(out=t2[:, q4 * 512:(q4 + 1) * 512], in0=ps[NB][0:64, :], in1=sinT[:, q4 * 512:(q4 + 1) * 512], op=ALU.mult),
                     reads=[PK[NB], 'a_tab'], writes=['a_t2'], acc=True)
                yield
            P.op('pool', lambda e: e.tensor_tensor(out=dst, in0=t1[:], in1=t2[:], op=ALU.add), reads=['a_t1', 'a_t2'], writes=[dkey], acc=True)
            yield

        def prep(s, g, gp):
            c0 = s * T
            G = lambda n: 'a_%s%d' % (n, gp)
            P.dma('sp', kT[gp][:], C.pT[K0 + g * 64:K0 + (g + 1) * 64, c0:c0 + T], reads=['pT'], writes=[G('kT')])
            P.dma('sp', qT[gp][:], C.pT[Q0 + 3 * g * 64:Q0 + 3 * (g + 1) * 64, c0:c0 + T].rearrange("(j d) t -> d j t", d=64), reads=['pT'], writes=[G('qT')])
            P.dma('sp', zaT[gp][:], C.pT[ZA0 + 3 * g * 64:ZA0 + 3 * (g + 1) * 64, c0:c0 + T].rearrange("(j d) t -> d j t", d=64), reads=['pT'], writes=[G('zaT')])
            P.dma('sp', vt[gp][:], C.va[c0:c0 + T, g * 64:(g + 1) * 64].rearrange("(n p) d -> p n d", p=128), reads=['va'], writes=[G('vt')])
            yield
            yield from normrope(kT[gp][:], kw, 'a_kw', kr[gp][:], G('kT'), G('kr'))
            for j in range(3):
                yield from normrope(qT[gp][:, j, :], qw, 'a_qw', qr[gp][:, j, :], G('qT'), G('qr'))
            P.op('act', lambda e: e.activation(out=zaT[gp][:], in_=zaT[gp][:], func=AF.Silu), reads=[G('zaT')], writes=[G('zaT')])
            yield

        def blocks(s, g, gp):
            c0 = s * T
            G = lambda n: 'a_%s%d' % (n, gp)

            def stage1(n):
                par = n % 2
                ipb, icb = 2 * par, 2 * par + 1
                qs = qr[gp][:, :, n * 128:(n + 1) * 128]
                P.op('pe', lambda e: e.matmul(ps[icb][:, 0:384], lhsT=kr[gp][:, n * 128:(n + 1) * 128], rhs=qs, start=True, stop=True),
                     reads=[G('kr'), G('qr')], writes=[PK[icb]])
                if n > 0:
                    P.op('pe', lambda e: e.matmul(ps[ipb][:, 0:384], lhsT=kr[gp][:, (n - 1) * 128:n * 128], rhs=qs, start=True, stop=True),
                         reads=[G('kr'), G('qr')], writes=[PK[ipb]])
                P.op('act', lambda e: e.activation(out=pe_sb[par][:, 1, :], in_=ps[icb][:, 0:384], func=AF.Exp, scale=0.125),
                     reads=[PK[icb]], writes=['a_pe%d' % par], acc=True)
                if n > 0:
                    P.op('act', lambda e: e.activation(out=pe_sb[par][:, 0, :], in_=ps[ipb][:, 0:384], func=AF.Exp, scale=0.125),
                         reads=[PK[ipb]], writes=['a_pe%d' % par], acc=True)
                lo = 0 if n > 0 else 1
                meng = 'dve' if n % 2 == 0 else 'pool'
                P.op(meng, lambda e: e.tensor_tensor(
                    out=pm_sb[par][:, lo:2, :].rearrange("p a (h q) -> p a h q", h=3),
                    in0=pe_sb[par][:, lo:2, :].rearrange("p a (h q) -> p a h q", h=3),
                    in1=mkb[:, lo:2, :].unsqueeze(2).broadcast_to([128, 2 - lo, 3, 128]), op=ALU.mult),
                    reads=['a_pe%d' % par, 'a_mkb'], writes=['a_pm%d' % par])

            def stage2(n):
                par = n % 2
                iob, idb = 4 + par, 6
                if n > 0:
                    P.op('pe', lambda e: e.matmul(ps[iob][0:64, 0:384], lhsT=vt[gp][:, n - 1, :], rhs=pm_sb[par][:, 0, :], start=True, stop=False),
                         reads=[G('vt'), 'a_pm%d' % par], writes=[PK[iob]])
                P.op('pe', lambda e: e.matmul(ps[iob][0:64, 0:384], lhsT=vt[gp][:, n, :], rhs=pm_sb[par][:, 1, :], start=(n == 0), stop=True),
                     reads=[G('vt'), 'a_pm%d' % par], writes=[PK[iob]])
                if n > 0:
                    P.op('pe', lambda e: e.matmul(ps[idb][0:64, 0:384], lhsT=C.ones_b[:, 0:64], rhs=pm_sb[par][:, 0, :], start=True, stop=False),
                         reads=['g_ones_b', 'a_pm%d' % par], writes=[PK[idb]])
                P.op('pe', lambda e: e.matmul(ps[idb][0:64, 0:384], lhsT=C.ones_b[:, 0:64], rhs=pm_sb[par][:, 1, :], start=(n == 0), stop=False),
                     reads=['g_ones_b', 'a_pm%d' % par], writes=[PK[idb]])
                P.op('pe', lambda e: e.matmul(ps[idb][0:64, 0:384], lhsT=C.ones_b[0:1, 0:64], rhs=snkb[0:1, 3 * g:3 * g + 3, :], start=False, stop=True),
                     reads=['g_ones_b', 'a_snkb'], writes=[PK[idb]])
                P.op('act', lambda e: e.activation(out=rden[:], in_=ps[idb][0:64, 0:384], func=AF.Ln), reads=[PK[idb]], writes=['a_rden'])
                P.op('act', lambda e: e.activation(out=rden[:], in_=rden[:], func=AF.Exp, scale=-1.0), reads=['a_rden'], writes=['a_rden'])
                P.op('dve', lambda e: e.tensor_tensor(out=o_sb[:], in0=ps[iob][0:64, 0:384], in1=rden[:], op=ALU.mult), reads=[PK[iob], 'a_rden'], writes=['a_osb'])
                P.op('pool', lambda e: e.tensor_tensor(out=ya[gp][:, :, n * 128:(n + 1) * 128], in0=o_sb[:].rearrange("p (h q) -> p h q", h=3),
                                                       in1=zaT[gp][:, :, n * 128:(n + 1) * 128], op=ALU.mult),
                     reads=['a_osb', G('zaT')], writes=[G('ya')], acc=True)

            stage1(0)
            yield
            for n in range(16):
                if n + 1 < 16:
                    stage1(n + 1)
                    yield
                stage2(n)
                yield
            P.dma('sp', C.yT[YA0 + 3 * g * 64:YA0 + 3 * (g + 1) * 64, c0:c0 + T].rearrange("(j d) t -> d j t", d=64), ya[gp][:], reads=[G('ya')], writes=['yT'], acc=True)

        def merge(gens):
            live = [[g, w] for g, w in gens]
            while live:
                for item in list(live):
                    g, w = item
                    for _ in range(w):
                        try:
                            next(g)
                        except StopIteration:
                            live.remove(item)
                            break

        work = [(s, g) for s in range(NSEQ) for g in range(4)]
        tables(0)
        merge([[prep(work[0][0], work[0][1], 0), 1]])
        for i, (s, g) in enumerate(work):
            gl = [[blocks(s, g, i % 2), 2]]
            if i + 1 < len(work):
                s2, g2 = work[i + 1]
                if s2 != s:
                    tables(s2)
                gl.append([prep(s2, g2, (i + 1) % 2), 1])
            merge(gl)


def stage_gdn(P, nc, C, l):
    NH = 6
    with ExitStack() as st:
        SB = lambda n, s, d: st.enter_context(nc.sbuf_tensor("sc_" + str(l) + n, s, d))
        PS = lambda n, s, d: st.enter_context(nc.psum_tensor("cp_" + str(l) + n, s, d))
        cwc = SB("cwc", [128, 18, 4], F32)
        dgc = SB("dgc", [128, 72, 128], BF16)
        onw = SB("onw", [128, 1], F32)
        alog = SB("alog", [6, 1], F32)
        dtb = SB("dtb", [6, 1], F32)
        lmf = SB("lmf", [128, 7, 128], F32)
        lmb = SB("lmb", [128, 7, 128], BF16)
        ngm = SB("ngm", [128, 128], F32)
        sel = SB("sel", [6, 6, 128], F32)
        onesf = SB("onesf", [6, 128], F32)
        beta1 = SB("beta", [6, T], F32)
        gg1 = SB("gg", [6, T], F32)
        gc1 = SB("gc", [6, T], F32)
        egl1 = SB("egl", [6, 16], F32)
        egd1 = SB("egd", [6, 16, 6], F32)
        S32_1 = SB("S32", [128, NH, 128], F32)
        Sb_1 = SB("Sb", [128, NH, 128], BF16)
        braw = araw = beta = gg = gc = edl = egl = egd = S32 = Sb = None
        yc = [SB("yc%d" % i, [128, NH, 512], BF16) for i in range(2)]
        zc = [SB("zc%d" % i, [128, NH, 512], BF16) for i in range(2)]
        szc = zc
        def two(n, shp, d):
            return [SB("%s%d" % (n, i), shp, d) for i in range(2)]
        xin = two("xin", [128, 18, 131], BF16)
        qkv0_1 = SB("qkv0", [128, 18, 128], F32)
        qkv0 = [qkv0_1, qkv0_1]
        sqt = two("sqt", [128, NH, 128], BF16)
        rn = two("rn", [128, NH, 128], F32)
        KT = two("KT", [128, NH, 128], BF16)
        QT = two("QT", [128, NH, 128], BF16)
        VT = two("VT", [128, NH, 128], BF16)
        KgT = two("KgT", [128, NH, 128], BF16)
        QgT = two("QgT", [128, NH, 128], BF16)
        egcb = two("egcb", [128, NH, 128], F32)
        tmpa = two("tmpa", [128, NH, 128], F32)
        LTb = two("LTb", [128, NH, 128], BF16)
        ET = two("ET", [128, NH, 128], F32)
        cols = two("cols", [128, 24], F32)
        intraT = two("intraT", [128, NH, 128], BF16)
        ktil = two("ktil", [128, NH, 128], BF16)
        BsT = two("BsT", [128, NH, 128], BF16)
        Zsb = two("Zsb", [128, NH, 128], BF16)
        XmA = two("XmA", [128, NH, 128], BF16)
        XmB = two("XmB", [128, NH, 128], BF16)
        XTA = two("XTA", [128, NH, 128], BF16)
        XTB = two("XTB", [128, NH, 128], BF16)
        rsb = two("rsb", [128, NH, 128], BF16)
        vnew = two("vnew", [128, NH, 128], BF16)
        o32 = egcb
        ro = rn
        NSLOT = 4
        slots = [PS("s%d" % i, [128, 8, 128], F32) for i in range(NSLOT)]
        SK = ['c_slot%d' % i for i in range(NSLOT)]
        P.excl.update(SK)
        slot_ctr = [0]

        def nslot():
            i = slot_ctr[0] % NSLOT
            slot_ctr[0] += 1
            return slots[i], SK[i]

        def bc(colap):
            return colap.unsqueeze(2).broadcast_to([128, NH, 128])

        P.dma('sp', cwc[:], C.c_cw[l], writes=['c_cwc'])
        P.dma('sp', onw[:], C.c_onw[l], writes=['c_onw'])
        P.dma('sp', alog[:], C.c_alog[l], writes=['c_alog'])
        P.dma('sp', dtb[:], C.c_dtb[l], writes=['c_dtb'])
        P.dma('sp', lmf[:], C.c_lm, writes=['c_lmf'])
        P.dma('sp', ngm[:], C.c_ngm, writes=['c_ngm'])
        P.dma('sp', sel[:], C.c_sel, writes=['c_sel'])
        P.op('dve', lambda e: e.tensor_copy(out=lmb[:], in_=lmf[:]), reads=['c_lmf'], writes=['c_lmb'])
        P.op('pool', lambda e: e.memset(onesf[:], 1.0), writes=['c_onesf'])
        P.op('act', lambda e: e.activation(out=alog[:], in_=alog[:], func=AF.Exp), reads=['c_alog'], writes=['c_alog'])
        P.op('dve', lambda e: e.tensor_scalar(out=alog[:], in0=alog[:], scalar1=-1.0, scalar2=None, op0=ALU.mult), reads=['c_alog'], writes=['c_alog'])
        for cc in range(18):
            for k in range(4):
                eng = 'pool' if (cc * 4 + k) % 3 == 2 else 'dve'
                P.op(eng, lambda e, cc=cc, k=k: e.tensor_scalar(out=dgc[:, cc * 4 + k, :], in0=C.ident_f[:], scalar1=cwc[:, cc, k:k + 1], scalar2=None, op0=ALU.mult),
                     reads=['c_cwc', 'g_ident_f'], writes=['c_dgc'], acc=True)

        unit = 0
        for s in range(NSEQ):
            c0 = s * T
            beta, gg, gc, edl, egl, egd, S32, Sb = beta1, gg1, gc1, gg1, egl1, egd1, S32_1, Sb_1
            P.dma('sp', beta[:], C.baT[0:6, c0:c0 + T], reads=['baT'], writes=['c_beta'])
            P.dma('sp', gg[:], C.baT[6:12, c0:c0 + T], reads=['baT'], writes=['c_gg'])
            P.op('act', lambda e: e.activation(out=beta[:], in_=beta[:], func=AF.Sigmoid), reads=['c_beta'], writes=['c_beta'])
            P.op('act', lambda e: e.activation(out=gg[:], in_=gg[:], func=AF.Exp, bias=dtb[:, 0:1]), reads=['c_gg', 'c_dtb'], writes=['c_gg'])
            P.op('act', lambda e: e.activation(out=gg[:], in_=gg[:], func=AF.Ln, bias=C.one_col[0:6, 0:1]), reads=['c_gg'], writes=['c_gg'])
            P.op('dve', lambda e: e.tensor_scalar(out=gg[:], in0=gg[:], scalar1=alog[:, 0:1], scalar2=None, op0=ALU.mult), reads=['c_gg', 'c_alog'], writes=['c_gg'])
            for c in range(16):
                P.op('dve', lambda e, c=c: e.tensor_tensor_scan(out=gc[:, c * 128:(c + 1) * 128], data0=onesf[:], data1=gg[:, c * 128:(c + 1) * 128], initial=0.0, op0=ALU.mult, op1=ALU.add),
                     reads=['c_gg', 'c_onesf'], writes=['c_gc'], acc=True)
            gc3 = gc[:].rearrange("p (c t) -> p c t", t=128)
            P.op('dve', lambda e, gc3=gc3: e.tensor_tensor(out=edl[:].rearrange("p (c t) -> p c t", t=128), in0=gc3[:, :, 127:128].broadcast_to([6, 16, 128]), in1=gc3, op=ALU.subtract),
                 reads=['c_gc', 'c_gg'], writes=['c_gg'])
            P.op('act', lambda e: e.activation(out=edl[:], in_=edl[:], func=AF.Exp), reads=['c_gg'], writes=['c_gg'])
            P.op('act', lambda e, gc3=gc3: e.activation(out=egl[:].unsqueeze(2), in_=gc3[:, :, 127:128], func=AF.Exp), reads=['c_gc'], writes=['c_egl'])
            P.op('dve', lambda e: e.tensor_tensor(out=egd[:], in0=C.ident_f[0:6, 0:6].unsqueeze(1).broadcast_to([6, 16, 6]), in1=egl[:].unsqueeze(2).broadcast_to([6, 16, 6]), op=ALU.mult),
                 reads=['c_egl', 'g_ident_f'], writes=['c_egd'])
            P.op('pool', lambda e: e.memset(S32[:], 0.0), writes=['c_S32'])
            P.op('pool', lambda e: e.memset(Sb[:], 0.0), writes=['c_Sb'])
            def pre(c, u):
                U = lambda n, u=u: 'c_%s%d' % (n, u)
                tok0 = s * T + c * 128
                y2 = (c // 4) % 2
                if c % 4 == 0:
                    t512 = s * T + (c // 4) * 512
                    P.dma('sp', zc[y2][:], C.pT[ZC0:ZC0 + 768, t512:t512 + 512].rearrange("(h p) t -> p h t", p=128), reads=['pT'], writes=['c_zc%d' % y2])
                    P.op('act', lambda e, y2=y2: e.activation(out=zc[y2][:], in_=zc[y2][:], func=AF.Silu), reads=['c_zc%d' % y2], writes=['c_zc%d' % y2])
                if c == 0:
                    P.op('pool', lambda e, u=u: e.memset(xin[u][:, :, 0:3], 0.0), writes=[U('xin')])
                    P.dma('sp', xin[u][:, :, 3:131], C.pT[CQ0:CQ0 + 2304, tok0:tok0 + 128].rearrange("(cc p) t -> p cc t", p=128), reads=['pT'], writes=[U('xin')])
                else:
                    P.dma('sp', xin[u][:], C.pT[CQ0:CQ0 + 2304, tok0 - 3:tok0 + 128].rearrange("(cc p) t -> p cc t", p=128), reads=['pT'], writes=[U('xin')])
                for grp in range(3):
                    sl, sk = nslot()
                    for hh in range(6):
                        cc = grp * 6 + hh
                        for k in range(4):
                            P.op('pe', lambda e, u=u, cc=cc, k=k, hh=hh, sl=sl: e.matmul(sl[:, hh, :], lhsT=dgc[:, cc * 4 + k, :], rhs=xin[u][:, cc, k:k + 128], start=(k == 0), stop=(k == 3)),
                                 reads=['c_dgc', U('xin')], writes=[sk])
                    P.op('act', lambda e, grp=grp, sl=sl: e.activation(out=qkv0_1[:, grp * 6:(grp + 1) * 6, :], in_=sl[:, 0:6, :], func=AF.Silu),
                         reads=[sk], writes=['c_qkv0_%d' % grp])
                    yield
                for grp, dst, scl in ((0, QT, 128.0 ** -0.5), (1, KT, 1.0)):
                    src = qkv0_1[:, grp * 6:(grp + 1) * 6, :]
                    P.op('act', lambda e, u=u, src=src: e.activation(out=sqt[u][:], in_=src, func=AF.Square), reads=['c_qkv0_%d' % grp], writes=[U('sqt')])
                    sl, sk = nslot()
                    P.op('pe', lambda e, u=u, sl=sl: e.matmul(sl[:, 0:4, :], lhsT=C.ones_b[:], rhs=sqt[u][:, 0:4, :], start=True, stop=True),
                         reads=['g_ones_b', U('sqt')], writes=[sk])
                    P.op('pe', lambda e, u=u, sl=sl: e.matmul(sl[:, 4:6, :], lhsT=C.ones_b[:], rhs=sqt[u][:, 4:6, :], start=True, stop=True),
                         reads=['g_ones_b', U('sqt')], writes=[sk])
                    P.op('act', lambda e, u=u, sl=sl: e.activation(out=rn[u][:], in_=sl[:, 0:6, :], func=AF.Ln, bias=C.eps_col[:, 0:1]), reads=[sk], writes=[U('rn')])
                    P.op('act', lambda e, u=u: e.activation(out=rn[u][:], in_=rn[u][:], func=AF.Exp, scale=-0.5), reads=[U('rn')], writes=[U('rn')])
                    P.op('dve', lambda e, u=u, src=src, dst=dst, scl=scl: e.scalar_tensor_tensor(out=dst[u][:], in0=src, scalar=scl, in1=rn[u][:], op0=ALU.mult, op1=ALU.mult),
                         reads=['c_qkv0_%d' % grp, U('rn')], writes=[U('QT' if grp == 0 else 'KT')])
                    yield
                P.op('act', lambda e, u=u: e.activation(out=VT[u][:], in_=qkv0_1[:, 12:18, :], func=AF.Copy), reads=['c_qkv0_2'], writes=[U('VT')])
                sl, sk = nslot()
                tsl = slice(c * 128, (c + 1) * 128)
                P.op('pe', lambda e, sl=sl, tsl=tsl: e.matmul(sl[:, 0, 0:6], lhsT=beta[:, tsl], rhs=C.ident_f[0:6, 0:6], start=True, stop=True),
                     reads=['c_beta', 'g_ident_f'], writes=[sk])
                P.op('pe', lambda e, sl=sl, tsl=tsl: e.matmul(sl[:, 0, 6:12], lhsT=gc[:, tsl], rhs=C.ident_f[0:6, 0:6], start=True, stop=True),
                     reads=['c_gc', 'g_ident_f'], writes=[sk])
                P.op('pe', lambda e, sl=sl, tsl=tsl: e.matmul(sl[:, 0, 12:18], lhsT=edl[:, tsl], rhs=C.ident_f[0:6, 0:6], start=True, stop=True),
                     reads=['c_gg', 'g_ident_f'], writes=[sk])
                P.op('pe', lambda e, sl=sl, c=c: e.matmul(sl[:, 0, 18:24], lhsT=onesf[:], rhs=egd[:, c, :], start=True, stop=True),
                     reads=['c_egd', 'c_onesf'], writes=[sk])
                P.op('dve', lambda e, u=u, sl=sl: e.tensor_copy(out=cols[u][:], in_=sl[:, 0, 0:24]), reads=[sk], writes=[U('cols')])
                yield
                sl, sk = nslot()
                for hh in range(6):
                    P.op('pe', lambda e, hh=hh, sl=sl, tsl=tsl: e.matmul(sl[:, hh, :], lhsT=sel[:, hh, :], rhs=gc[:, tsl], start=True, stop=True),
                         reads=['c_sel', 'c_gc'], writes=[sk])
                P.op('act', lambda e, u=u, sl=sl: e.activation(out=egcb[u][:], in_=sl[:, 0:6, :], func=AF.Exp), reads=[sk], writes=[U('egcb')])
                for hh in range(6):
                    P.op('dve', lambda e, u=u, hh=hh, sl=sl: e.scalar_tensor_tensor(out=tmpa[u][:, hh, :], in0=sl[:, hh, :], scalar=cols[u][:, 6 + hh:7 + hh], in1=ngm[:],
                                                                                    op0=ALU.subtract, op1=ALU.add),
                         reads=[sk, U('cols'), 'c_ngm'], writes=[U('tmpa')], acc=True)
                P.op('act', lambda e, u=u: e.activation(out=ET[u][:], in_=tmpa[u][:], func=AF.Exp), reads=[U('tmpa')], writes=[U('ET')])
                yield
                P.op('dve', lambda e, u=u: e.scalar_tensor_tensor(out=KgT[u][:], in0=KT[u][:], scalar=-1.0, in1=egcb[u][:], op0=ALU.mult, op1=ALU.mult),
                     reads=[U('KT'), U('egcb')], writes=[U('KgT')])
                P.op('pool', lambda e, u=u: e.tensor_tensor(out=QgT[u][:], in0=QT[u][:], in1=egcb[u][:], op=ALU.mult), reads=[U('QT'), U('egcb')], writes=[U('QgT')])
                sl, sk = nslot()
                for hh in range(6):
                    P.op('pe', lambda e, u=u, hh=hh, sl=sl: e.matmul(sl[:, hh, :], lhsT=KT[u][:, hh, :], rhs=KT[u][:, hh, :], start=True, stop=True),
                         reads=[U('KT')], writes=[sk])
                for hh in range(6):
                    P.op('dve', lambda e, u=u, hh=hh, sl=sl: e.scalar_tensor_tensor(out=LTb[u][:, hh, :], in0=sl[:, hh, :], scalar=cols[u][:, hh:hh + 1], in1=ET[u][:, hh, :],
                                                                                    op0=ALU.mult, op1=ALU.mult),
                         reads=[sk, U('cols'), U('ET')], writes=[U('LTb')], acc=True)
                yield
                sl, sk = nslot()
                for hh in range(6):
                    P.op('pe', lambda e, u=u, hh=hh, sl=sl: e.matmul(sl[:, hh, :], lhsT=KT[u][:, hh, :], rhs=QT[u][:, hh, :], start=True, stop=True),
                         reads=[U('KT'), U('QT')], writes=[sk])
                P.op('dve', lambda e, u=u, sl=sl: e.tensor_tensor(out=intraT[u][:], in0=sl[:, 0:6, :], in1=ET[u][:], op=ALU.mult), reads=[sk, U('ET')], writes=[U('intraT')])
                yield
                sl, sk = nslot()
                for hh in range(6):
                    P.op('pe', lambda e, u=u, hh=hh, sl=sl: e.matmul(sl[:, hh, :], lhsT=KT[u][:, hh, :], rhs=C.ident_b[:], start=True, stop=True),
                         reads=[U('KT'), 'g_ident_b'], writes=[sk])
                P.op('dve', lambda e, u=u, sl=sl: e.tensor_tensor(out=ktil[u][:], in0=sl[:, 0:6, :], in1=bc(cols[u][:, 12:18]), op=ALU.mult), reads=[sk, U('cols')], writes=[U('ktil')])
                yield
                identb3 = C.ident_b[:].unsqueeze(1).broadcast_to([128, NH, 128])

                def mask_level(lv):
                    eng = 'dve' if lv % 2 == 0 else 'pool'
                    P.op(eng, lambda e, u=u, lv=lv: e.tensor_tensor(out=BsT[u][:], in0=LTb[u][:], in1=lmb[:, lv:lv + 1, :].broadcast_to([128, NH, 128]), op=ALU.mult),
                         reads=[U('LTb'), 'c_lmb'], writes=[U('BsT')])
                mask_level(0)
                sl, sk = nslot()
                for hh in range(6):
                    P.op('pe', lambda e, u=u, hh=hh, sl=sl: e.matmul(sl[:, hh, :], lhsT=BsT[u][:, hh, :], rhs=C.ident_b[:], start=True, stop=True),
                         reads=[U('BsT'), 'g_ident_b'], writes=[sk])
                P.op('dve', lambda e, u=u, sl=sl, identb3=identb3: e.tensor_tensor(out=XmA[u][:], in0=identb3, in1=sl[:, 0:6, :], op=ALU.subtract),
                     reads=[sk, 'g_ident_b'], writes=[U('XmA')])
                P.op('pool', lambda e, u=u, identb3=identb3: e.tensor_tensor(out=XTA[u][:], in0=identb3, in1=BsT[u][:], op=ALU.subtract),
                     reads=[U('BsT'), 'g_ident_b'], writes=[U('XTA')])
                yield
                Xm, XT_, XmN, XTN = XmA, XTA, XmB, XTB
                kXm, kXT, kXmN, kXTN = 'XmA', 'XTA', 'XmB', 'XTB'
                for lv in range(1, 7):
                    last = (lv == 6)
                    mask_level(lv)
                    sl, sk = nslot()
                    for hh in range(6):
                        P.op('pe', lambda e, u=u, hh=hh, sl=sl, Xm=Xm: e.matmul(sl[:, hh, :], lhsT=BsT[u][:, hh, :], rhs=Xm[u][:, hh, :], start=True, stop=True),
                             reads=[U('BsT'), U(kXm)], writes=[sk])
                    P.op('act', lambda e, u=u, sl=sl: e.activation(out=Zsb[u][:], in_=sl[:, 0:6, :], func=AF.Copy), reads=[sk], writes=[U('Zsb')])
                    yield
                    if not last:
                        sl, sk = nslot()
                        for hh in range(6):
                            P.op('pe', lambda e, u=u, hh=hh, sl=sl, XT_=XT_: e.matmul(sl[:, hh, :], lhsT=XT_[u][:, hh, :], rhs=Zsb[u][:, hh, :], start=True, stop=True),
                                 reads=[U('Zsb'), U(kXT)], writes=[sk])
                        P.op('dve', lambda e, u=u, sl=sl, Xm=Xm, XmN=XmN: e.tensor_tensor(out=XmN[u][:], in0=Xm[u][:], in1=sl[:, 0:6, :], op=ALU.subtract),
                             reads=[sk, U(kXm)], writes=[U(kXmN)])
                    sl, sk = nslot()
                    for hh in range(6):
                        P.op('pe', lambda e, u=u, hh=hh, sl=sl, XT_=XT_: e.matmul(sl[:, hh, :], lhsT=Zsb[u][:, hh, :], rhs=XT_[u][:, hh, :], start=True, stop=True),
                             reads=[U('Zsb'), U(kXT)], writes=[sk])
                    P.op('dve', lambda e, u=u, sl=sl, XT_=XT_, XTN=XTN: e.tensor_tensor(out=XTN[u][:], in0=XT_[u][:], in1=sl[:, 0:6, :], op=ALU.subtract),
                         reads=[sk, U(kXT)], writes=[U(kXTN)])
                    yield
                    Xm, XT_, XmN, XTN = XmN, XTN, Xm, XT_
                    kXm, kXT, kXmN, kXTN = kXmN, kXTN, kXm, kXT
                assert kXT == 'XTA'

            def scan(c, u):
                U = lambda n, u=u: 'c_%s%d' % (n, u)
                c4 = c % 4
                y2 = (c // 4) % 2
                ykey = 'c_yc%d' % y2
                TT, kTT = XTA, 'XTA'
                sl, sk = nslot()
                for hh in range(6):
                    P.op('pe', lambda e, u=u, hh=hh, sl=sl: e.matmul(sl[:, hh, :], lhsT=VT[u][:, hh, :], rhs=C.ident_b[:], start=True, stop=False),
                         reads=[U('VT'), 'g_ident_b'], writes=[sk])
                    P.op('pe', lambda e, u=u, hh=hh, sl=sl: e.matmul(sl[:, hh, :], lhsT=KgT[u][:, hh, :], rhs=Sb[:, hh, :], start=False, stop=True),
                         reads=[U('KgT'), 'c_Sb'], writes=[sk])
                P.op('act', lambda e, u=u, sl=sl: e.activation(out=rsb[u][:], in_=sl[:, 0:6, :], func=AF.Copy), reads=[sk], writes=[U('rsb')])
                yield
                sl, sk = nslot()
                for hh in range(6):
                    P.op('pe', lambda e, u=u, hh=hh, sl=sl, TT=TT: e.matmul(sl[:, hh, :], lhsT=TT[u][:, hh, :], rhs=rsb[u][:, hh, :], start=True, stop=True),
                         reads=[U(kTT), U('rsb')], writes=[sk])
                P.op('dve', lambda e, u=u, sl=sl: e.tensor_tensor(out=vnew[u][:], in0=sl[:, 0:6, :], in1=bc(cols[u][:, 0:6]), op=ALU.mult), reads=[sk, U('cols')], writes=[U('vnew')])
                yield
                sl, sk = nslot()
                for hh in range(6):
                    P.op('pe', lambda e, u=u, hh=hh, sl=sl: e.matmul(sl[:, hh, :], lhsT=Sb[:, hh, :], rhs=QgT[u][:, hh, :], start=True, stop=False),
                         reads=[U('QgT'), 'c_Sb'], writes=[sk])
                    P.op('pe', lambda e, u=u, hh=hh, sl=sl: e.matmul(sl[:, hh, :], lhsT=vnew[u][:, hh, :], rhs=intraT[u][:, hh, :], start=False, stop=True),
                         reads=[U('vnew'), U('intraT')], writes=[sk])
                P.op('act', lambda e, u=u, sl=sl: e.activation(out=o32[u][:], in_=sl[:, 0:6, :], func=AF.Copy), reads=[sk], writes=[U('egcb')])
                sl, sk = nslot()
                for hh in range(6):
                    P.op('pe', lambda e, u=u, hh=hh, sl=sl: e.matmul(sl[:, hh, :], lhsT=ktil[u][:, hh, :], rhs=vnew[u][:, hh, :], start=True, stop=True),
                         reads=[U('ktil'), U('vnew')], writes=[sk])
                P.op('pool', lambda e, u=u: e.tensor_tensor(out=S32[:], in0=S32[:], in1=bc(cols[u][:, 18:24]), op=ALU.mult), reads=['c_S32', U('cols')], writes=['c_S32'])
                P.op('dve', lambda e, sl=sl: e.tensor_tensor(out=S32[:], in0=S32[:], in1=sl[:, 0:6, :], op=ALU.add), reads=[sk, 'c_S32'], writes=['c_S32'])
                P.op('act', lambda e: e.activation(out=Sb[:], in_=S32[:], func=AF.Copy), reads=['c_S32'], writes=['c_Sb'])
                yield
                P.op('act', lambda e, u=u: e.activation(out=sqt[u][:], in_=o32[u][:], func=AF.Square), reads=[U('egcb')], writes=[U('sqt')])
                sl, sk = nslot()
                P.op('pe', lambda e, u=u, sl=sl: e.matmul(sl[:, 0:4, :], lhsT=C.ones_b[:], rhs=sqt[u][:, 0:4, :], start=True, stop=True),
                     reads=['g_ones_b', U('sqt')], writes=[sk])
                P.op('pe', lambda e, u=u, sl=sl: e.matmul(sl[:, 4:6, :], lhsT=C.ones_b[:], rhs=sqt[u][:, 4:6, :], start=True, stop=True),
                     reads=['g_ones_b', U('sqt')], writes=[sk])
                P.op('act', lambda e, u=u, sl=sl: e.activation(out=ro[u][:], in_=sl[:, 0:6, :], func=AF.Ln, scale=1.0 / 128, bias=C.eps_col[:, 0:1]), reads=[sk], writes=[U('rn')])
                yield
                P.op('act', lambda e, u=u: e.activation(out=ro[u][:], in_=ro[u][:], func=AF.Exp, scale=-0.5), reads=[U('rn')], writes=[U('rn')])
                P.op('dve', lambda e, u=u: e.scalar_tensor_tensor(out=o32[u][:], in0=o32[u][:], scalar=onw[:, 0:1], in1=ro[u][:], op0=ALU.mult, op1=ALU.mult),
                     reads=[U('egcb'), U('rn'), 'c_onw'], writes=[U('egcb')])
                P.op('pool', lambda e, u=u, c4=c4, y2=y2: e.tensor_tensor(out=yc[y2][:, :, c4 * 128:(c4 + 1) * 128], in0=o32[u][:], in1=zc[y2][:, :, c4 * 128:(c4 + 1) * 128], op=ALU.mult),
                     reads=[U('egcb'), 'c_zc%d' % y2], writes=[ykey], acc=True)
                if c4 == 3:
                    t512 = s * T + (c // 4) * 512
                    P.dma('sp', C.yT[YC0:YC0 + 768, t512:t512 + 512].rearrange("(h p) t -> p h t", p=128), yc[y2][:], reads=[ykey], writes=['yT'], acc=True)
                yield

            def merge(ga, gb, ratio):
                da = db = False
                while not (da and db):
                    if not da:
                        try:
                            next(ga)
                        except StopIteration:
                            da = True
                    for _ in range(ratio):
                        if db:
                            break
                        try:
                            next(gb)
                        except StopIteration:
                            db = True

            for _ in pre(0, 0):
                pass
            for c in range(16):
                nxt = pre(c + 1, (c + 1) % 2) if c + 1 < 16 else iter(())
                merge(scan(c, c % 2), nxt, 5)


def yc_sel(yc, s, y2):
    return yc[s]


def build_program(debug=None):
    nc = bass.Bass("TRN2", target_bir_lowering=False)
    C = Ctx()
    din = lambda n, s, d: nc.dram_tensor(n, s, d, kind="ExternalInput").ap()
    C.x = din("x", [NTOK, D], F32)
    C.pos = din("pos", [NSEQ, T], I32)
    C.norm_w = din("norm_w", [DEPTH, D], F32)
    C.w_fm = din("w_fm", [DEPTH, NCH, 128, 16, 128], F32)
    C.w_ba = din("w_ba", [DEPTH, 128, 16, 12], F32)
    C.w_va = din("w_va", [DEPTH, 128, 16, 256], F32)
    C.w_o = din("w_o", [DEPTH, 128, 16, D], F32)
    C.ident_f_d = din("ident_f", [128, 128], F32)
    C.a_invf = din("a_invf", [64, 1], F32)
    C.a_rt = din("a_rt", [64, 64], F32)
    C.a_mask = din("a_mask", [128, 2, 128], F32)
    C.a_qw = din("a_qw", [DEPTH, 64, 1], F32)
    C.a_kw = din("a_kw", [DEPTH, 64, 1], F32)
    C.a_snk = din("a_snk", [DEPTH, 1, 12], F32)
    C.c_cw = din("c_cw", [DEPTH, 128, 18, 4], F32)
    C.c_onw = din("c_onw", [DEPTH, 128, 1], F32)
    C.c_alog = din("c_alog", [DEPTH, 6, 1], F32)
    C.c_dtb = din("c_dtb", [DEPTH, 6, 1], F32)
    C.c_lm = din("c_lm", [128, 7, 128], F32)
    C.c_ngm = din("c_ngm", [128, 128], F32)
    C.c_rsm = din("c_rsm", [6, T], F32)
    C.c_sel = din("c_sel", [6, 6, 128], F32)
    C.b_cw = din("b_cw", [DEPTH, 128, 4, 31], F32)
    C.b_cb = din("b_cb", [DEPTH, 128, 4], F32)
    C.b_lnw = din("b_lnw", [DEPTH, 128, 4], F32)
    C.b_lnb = din("b_lnb", [DEPTH, 128, 4], F32)
    C.b_pwb = din("b_pwb", [DEPTH, 128, 4], F32)
    C.b_pw = din("b_pw", [DEPTH, 128, 4, 512], F32)
    C.out = nc.dram_tensor("out", [NTOK, D], F32, kind="ExternalOutput").ap()
    dbg_in, dbg_out = debug[1] if debug else ((), ())
    scr = lambda n, s, d: nc.dram_tensor(n, s, d, kind=("ExternalInput" if n in dbg_in else "ExternalOutput" if n in dbg_out else "Internal")).ap()
    stages = debug[0] if debug else 'ALL'
    C.pT = scr("pT", [NFM, NTOK], BF16)
    C.baT = scr("baT", [12, NTOK], F32)
    C.va = scr("va", [NTOK, 256], BF16)
    C.yT = scr("yT", [D, NTOK], BF16)
    C.x1 = scr("x1", [NTOK, D], F32)
    names = {id(C.x.tensor): 'x_in', id(C.x1.tensor): 'x1', id(C.out.tensor): 'out'}
    C.xres_key = lambda ap: names[id(ap.tensor)]

    with ExitStack() as st:
        P = Prog(nc, st)
        SB = lambda n, s, d: st.enter_context(nc.sbuf_tensor("g_" + n, s, d))
        C.ident_f = SB("ident_f", [128, 128], F32)
        C.ident_b = SB("ident_b", [128, 128], BF16)
        C.eps_col = SB("eps_col", [128, 1], F32)
        P.dma('sp', C.ident_f[:], C.ident_f_d, writes=['g_ident_f'])
        P.op('dve', lambda e: e.tensor_copy(out=C.ident_b[:], in_=C.ident_f[:]), reads=['g_ident_f'], writes=['g_ident_b'])
        P.op('dve', lambda e: e.memset(C.eps_col[:], EPS), writes=['g_eps'])
        C.one_col = SB("one_col", [128, 1], F32)
        P.op('dve', lambda e: e.memset(C.one_col[:], 1.0), writes=['g_one'])
        C.ones_b = SB("ones_b", [128, 128], BF16)
        P.op('dve', lambda e: e.memset(C.ones_b[:], 1.0), writes=['g_ones_b'])

        P.barrier()
        if stages == 'ALL':
            srcs = [C.x, C.x1]
            dsts = [C.x1, C.out]
            for l in range(DEPTH):
                stage_inproj(P, nc, C, l, srcs[l])
                P.barrier()
                stage_attn(P, nc, C, l)
                P.barrier()
                stage_conformer(P, nc, C, l)
                P.barrier()
                stage_gdn(P, nc, C, l)
                P.barrier()
                stage_outproj(P, nc, C, l, srcs[l], dsts[l])
                P.barrier()
        elif stages == 'P':
            stage_inproj(P, nc, C, 0, C.x)
        elif stages == 'A':
            stage_attn(P, nc, C, 0)
        elif stages == 'C':
            stage_gdn(P, nc, C, 0)
        elif stages == 'B':
            stage_conformer(P, nc, C, 0)
        elif stages == 'O':
            stage_outproj(P, nc, C, 0, C.x, C.out)
        P.finish()
        P.build()
    return nc


def host_inputs(inputs, debug=None):
    w_in = np.asarray(inputs["w_in"], dtype=np.float32)
    w_out = np.asarray(inputs["w_out"], dtype=np.float32)
    cols = np.concatenate([
        np.arange(0, 768), np.arange(768, 1024), np.arange(1280, 2048),
        np.arange(2048, 2560), np.arange(2560, 3072), np.arange(3072, 3584),
        np.arange(3584, 4352), np.arange(4352, 5120), np.arange(5120, 5888),
        np.arange(5900, 6668)])
    assert cols.size == NFM
    wfm = w_in[:, :, cols].reshape(DEPTH, 16, 128, NCH, 128).transpose(0, 3, 2, 1, 4)
    wba = w_in[:, :, 5888:5900].reshape(DEPTH, 16, 128, 12).transpose(0, 2, 1, 3)
    wva = w_in[:, :, 1024:1280].reshape(DEPTH, 16, 128, 256).transpose(0, 2, 1, 3)
    wo = w_out.reshape(DEPTH, 16, 128, D).transpose(0, 2, 1, 3)
    shared = {
        "norm_w": np.ascontiguousarray(inputs["norm_w"], dtype=np.float32),
        "w_fm": np.ascontiguousarray(wfm), "w_ba": np.ascontiguousarray(wba),
        "w_va": np.ascontiguousarray(wva), "w_o": np.ascontiguousarray(wo),
        "ident_f": np.eye(128, dtype=np.float32),
    }
    f32 = lambda k: np.asarray(inputs[k], dtype=np.float32)
    shared["c_cw"] = np.ascontiguousarray(f32("c_conv_w").transpose(0, 2, 1).reshape(DEPTH, 18, 128, 4).transpose(0, 2, 1, 3))
    shared["c_onw"] = np.ascontiguousarray(f32("c_onorm_w").reshape(DEPTH, 128, 1))
    shared["c_alog"] = np.ascontiguousarray(f32("c_a_log").reshape(DEPTH, 6, 1))
    shared["c_dtb"] = np.ascontiguousarray(f32("c_dt_bias").reshape(DEPTH, 6, 1))
    jj_ = np.arange(128)[:, None]
    ii_ = np.arange(128)[None, :]
    lm = np.zeros((128, 7, 128), np.float32)
    for lv in range(7):
        bsz = 1 << lv
        lm[:, lv, :] = ((ii_ // (2 * bsz) == jj_ // (2 * bsz)) & (jj_ % (2 * bsz) < bsz) & (ii_ % (2 * bsz) >= bsz)).astype(np.float32)
    shared["c_lm"] = lm
    shared["c_ngm"] = np.where(ii_ >= jj_, 0.0, -30000.0).astype(np.float32)
    rsm = np.ones((6, T), np.float32)
    rsm[:, 0::128] = 0.0
    shared["c_rsm"] = rsm
    selm = np.zeros((6, 6, 128), np.float32)
    for hh in range(6):
        selm[hh, hh, :] = 1.0
    shared["c_sel"] = selm
    invf = np.zeros((64, 1), np.float32)
    fr = (np.float32(500000.0) ** (-np.arange(0, 16, 2, dtype=np.float32) / np.float32(16))).astype(np.float32)
    invf[0:8, 0] = fr
    invf[8:16, 0] = fr
    rt = np.zeros((64, 64), np.float32)
    for mm in range(8):
        rt[mm + 8, mm] = -1.0
        rt[mm, mm + 8] = 1.0
    jj = np.arange(128)[:, None]
    ii = np.arange(128)[None, :]
    shared["a_invf"] = invf
    shared["a_rt"] = rt
    shared["a_mask"] = np.ascontiguousarray(np.stack([(jj > ii), (jj <= ii)], axis=1).astype(np.float32))
    shared["a_qw"] = np.ascontiguousarray(f32("q_norm_w").reshape(DEPTH, 64, 1))
    shared["a_kw"] = np.ascontiguousarray(f32("k_norm_w").reshape(DEPTH, 64, 1))
    shared["a_snk"] = np.ascontiguousarray(f32("sinks").reshape(DEPTH, 1, 12))
    col4 = lambda a: np.ascontiguousarray(a.reshape(DEPTH, 4, 128).transpose(0, 2, 1))
    shared["b_cw"] = np.ascontiguousarray(f32("b_conv_w").transpose(0, 2, 1).reshape(DEPTH, 4, 128, 31).transpose(0, 2, 1, 3))
    shared["b_cb"] = col4(f32("b_conv_b"))
    shared["b_lnw"] = col4(f32("b_ln_w"))
    shared["b_lnb"] = col4(f32("b_ln_b"))
    shared["b_pwb"] = col4(f32("b_pw_b"))
    shared["b_pw"] = np.ascontiguousarray(f32("b_pw_w").reshape(DEPTH, 4, 128, 512).transpose(0, 2, 1, 3))
    x = np.asarray(inputs["x"], dtype=np.float32)
    pos = np.asarray(inputs["positions"], dtype=np.int32)
    in_maps = []
    for c in range(NCORES):
        m = dict(shared)
        m["x"] = np.ascontiguousarray(x[c * NSEQ:(c + 1) * NSEQ].reshape(NTOK, D))
        m["pos"] = np.ascontiguousarray(pos[c * NSEQ:(c + 1) * NSEQ])
        in_maps.append(m)
    return in_maps


def kernel(**inputs):
    in_maps = host_inputs(inputs)
    nc = build_program()
    res = run_bass_kernel_spmd(nc, in_maps, core_ids=list(range(NCORES)))
    out = np.stack([r["out"].reshape(NSEQ, T, D) for r in res.results], axis=0)
    return out.reshape(NCORES * NSEQ, T, D).astype(np.float32)
```

```python
import numpy as np
from contextlib import ExitStack
import concourse.bass as bass
import concourse.mybir as mybir
from concourse.bass_utils import run_bass_kernel_spmd

F32 = mybir.dt.float32
BF16 = mybir.dt.bfloat16
I32 = mybir.dt.int32
AF = mybir.ActivationFunctionType
ALU = mybir.AluOpType

ENG = ['pe', 'dve', 'act', 'pool', 'sp']
EPOCH = 30000

NCORES = 8
DEPTH = 2
D = 2048
T = 2048
NSEQ = 2
NTOK = NSEQ * T
EPS = 1e-6
Q0, K0, ZA0, UBA0, UBG0, ZB0, CQ0, CK0, CV0, ZC0 = 0, 768, 1024, 1792, 2304, 2816, 3328, 4096, 4864, 5632
NFM = 6400
NCH = NFM // 128
YA0, YB0, YC0 = 0, 768, 1280


class Prog:
    def __init__(self, nc, stack):
        self.nc = nc
        self.stack = stack
        self.streams = {e: [] for e in ENG}
        self.count = {e: 0 for e in ENG}
        self.sems = {}
        self.seen = {e: {} for e in ENG}
        self.res = {}
        self.dmacnt = {}
        self.excl = set()

    def sem(self, key):
        if key not in self.sems:
            self.sems[key] = self.stack.enter_context(
                self.nc.semaphore("s%d" % len(self.sems)))
        return self.sems[key]

    def _deps(self, reads, writes, acc):
        deps = {}

        def add(d):
            for k, v in d.items():
                if deps.get(k, 0) < v:
                    deps[k] = v
        for r in reads:
            st = self.res.get(r)
            if st:
                add(st[0])
                if r in self.excl:
                    add(st[1])
        for w in writes:
            st = self.res.get(w)
            if st:
                add(st[1])
                if not acc:
                    add(st[0])
        return deps

    def _emit_waits(self, eng, deps):
        for k, v in deps.items():
            if self.seen[eng].get(k, 0) >= v:
                continue
            self.seen[eng][k] = v
            h = self.sem(k)
            self.streams[eng].append(lambda e, h=h, v=v: e.wait_ge(h, v))

    def _commit(self, reads, writes, acc, key, val):
        for r in reads:
            st = self.res.setdefault(r, [{}, {}])
            if st[1].get(key, 0) < val:
                st[1][key] = val
        for w in writes:
            st = self.res.setdefault(w, [{}, {}])
            if acc:
                if st[0].get(key, 0) < val:
                    st[0][key] = val
            else:
                st[0] = {key: val}
                st[1] = {}

    def op(self, eng, fn, reads=(), writes=(), acc=False):
        deps = self._deps(reads, writes, acc)
        if eng == 'pe':
            deps = {k: v for k, v in deps.items() if k[0] != 'pe'}
        n = self.count[eng]
        key = (eng, n // EPOCH)
        val = n % EPOCH + 1
        self._emit_waits(eng, deps)
        h = self.sem(key)
        self.streams[eng].append(lambda e, fn=fn, h=h: fn(e).then_inc(h, 1))
        self.count[eng] = n + 1
        self._commit(reads, writes, acc, key, val)

    def dma(self, q, out, in_, reads=(), writes=(), acc=False, semkey=None, **kw):
        assert len(writes) == 1
        deps = self._deps(reads, writes, acc)
        self._emit_waits(q, deps)
        key = ('dma', semkey if semkey is not None else (reads[0] if acc else writes[0]))
        cnt = self.dmacnt.get(key, 0) + 16
        self.dmacnt[key] = cnt
        h = self.sem(key)
        self.streams[q].append(
            lambda e, out=out, in_=in_, h=h, kw=kw: e.dma_start(out=out, in_=in_, **kw).then_inc(h, 16))
        self._commit(reads, writes, acc, key, cnt)

    def barrier(self):
        deps = {}
        for k, c in self.dmacnt.items():
            deps[k] = c
        for e in ENG:
            n = self.count[e]
            if n > 0:
                deps[(e, (n - 1) // EPOCH)] = (n - 1) % EPOCH + 1
        for e in ENG:
            self._emit_waits(e, dict(deps))

    def finish(self):
        deps = {}
        for k, c in self.dmacnt.items():
            deps[k] = c
        for e in ENG:
            n = self.count[e]
            if n > 0:
                deps[(e, (n - 1) // EPOCH)] = (n - 1) % EPOCH + 1
        self._emit_waits('sp', deps)

    def build(self):
        block = self.stack.enter_context(self.nc.Block())
        streams = self.streams

        @block.tensor
        def _(e):
            for f in streams['pe']:
                f(e)

        @block.vector
        def _(e):
            for f in streams['dve']:
                f(e)

        @block.scalar
        def _(e):
            for f in streams['act']:
                f(e)

        @block.gpsimd
        def _(e):
            for f in streams['pool']:
                f(e)

        @block.sync
        def _(e):
            for f in streams['sp']:
                f(e)


class Ctx:
    pass


def stage_inproj(P, nc, C, l, x_src):
    with ExitStack() as st:
        SB = lambda n, s, d: st.enter_context(nc.sbuf_tensor("p_" + str(l) + n, s, d))
        PS = lambda n, s, d: st.enter_context(nc.psum_tensor("pp_" + str(l) + n, s, d))
        hT = SB("hT", [128, 16, T], BF16)
        nw = SB("nw", [128, D], F32)
        xs = [SB("xs%d" % i, [128, D], F32) for i in range(2)]
        hb = [SB("hb%d" % i, [128, D], BF16) for i in range(2)]
        junk = SB("junk", [128, D], BF16)
        ss = [SB("ss%d" % i, [128, 1], F32) for i in range(2)]
        rs = [SB("rs%d" % i, [128, 1], F32) for i in range(2)]
        wf = [SB("wf%d" % i, [128, 16, 128], F32) for i in range(2)]
        wb = [SB("wb%d" % i, [128, 16, 128], BF16) for i in range(3)]
        wbaf = SB("wbaf", [128, 16, 12], F32)
        wbab = SB("wbab", [128, 16, 12], BF16)
        wvf = SB("wvf", [128, 16, 256], F32)
        wvb = SB("wvb", [128, 16, 256], BF16)
        ob = [SB("ob%d" % i, [128, T], BF16) for i in range(2)]
        obf = SB("obf", [12, T], F32)
        ovb = [SB("ovb%d" % i, [128, 256], BF16) for i in range(2)]
        ptr = [PS("tr%d" % i, [128, 16, 128], BF16) for i in range(2)]
        pmm = [PS("mm%d" % i, [128, 512], F32) for i in range(4)]

        P.dma('sp', nw[:], C.norm_w[l:l + 1, :].broadcast_to([128, D]), writes=['p_nw'])
        def wload(c):
            P.dma('sp', wf[c % 2][:], C.w_fm[l, c], writes=['p_wf%d' % (c % 2)])

        def wcast(c):
            P.op('pool', lambda e, c=c: e.tensor_copy(out=wb[c % 3][:], in_=wf[c % 2][:]),
                 reads=['p_wf%d' % (c % 2)], writes=['p_wb%d' % (c % 3)])

        for s in range(NSEQ):
            wload(0)
            wcast(0)
            wload(1)
            P.dma('sp', wbaf[:], C.w_ba[l], writes=['p_wbaf'])
            P.op('pool', lambda e: e.tensor_copy(out=wbab[:], in_=wbaf[:]), reads=['p_wbaf'], writes=['p_wbab'])
            P.dma('sp', wvf[:], C.w_va[l], writes=['p_wvf'])
            P.op('pool', lambda e: e.tensor_copy(out=wvb[:], in_=wvf[:]), reads=['p_wvf'], writes=['p_wvb'])
            for i in range(16):
                b = i % 2
                tok0 = s * T + i * 128
                P.dma('sp', xs[b][:], x_src[tok0:tok0 + 128, :], writes=['p_xs%d' % b])
                P.op('act', lambda e, b=b: e.activation(out=junk[:], in_=xs[b][:], func=AF.Square, accum_out=ss[b][:]),
                     reads=['p_xs%d' % b], writes=['p_junk', 'p_ss%d' % b])
                P.op('act', lambda e, b=b: e.activation(out=rs[b][:], in_=ss[b][:], func=AF.Sqrt, scale=1.0 / D, bias=C.eps_col[:, 0:1]),
                     reads=['p_ss%d' % b], writes=['p_rs%d' % b])
                P.op('dve', lambda e, b=b: e.reciprocal(out=rs[b][:], in_=rs[b][:]),
                     reads=['p_rs%d' % b], writes=['p_rs%d' % b])
                P.op('dve', lambda e, b=b: e.scalar_tensor_tensor(out=hb[b][:], in0=xs[b][:], scalar=rs[b][:, 0:1], in1=nw[:],
                                                                  op0=ALU.mult, op1=ALU.mult),
                     reads=['p_xs%d' % b, 'p_rs%d' % b, 'p_nw'], writes=['p_hb%d' % b])
                for kc in range(16):
                    P.op('pe', lambda e, b=b, kc=kc: e.transpose(out=ptr[b][:, kc, :], in_=hb[b][:, kc * 128:(kc + 1) * 128], identity=C.ident_b[:]),
                         reads=['p_hb%d' % b], writes=['p_ptr%d' % b])
                ev = 'act' if i % 2 == 0 else 'dve'
                if ev == 'act':
                    P.op('act', lambda e, b=b, i=i: e.activation(out=hT[:, :, i * 128:(i + 1) * 128], in_=ptr[b][:], func=AF.Copy),
                         reads=['p_ptr%d' % b], writes=['p_hT'], acc=True)
                else:
                    P.op('dve', lambda e, b=b, i=i: e.tensor_copy(out=hT[:, :, i * 128:(i + 1) * 128], in_=ptr[b][:]),
                         reads=['p_ptr%d' % b], writes=['p_hT'], acc=True)
            nmm = 0
            for c in range(NCH):
                w3 = c % 3
                if c + 2 < NCH:
                    wload(c + 2)
                if c + 1 < NCH:
                    wcast(c + 1)
                o2 = c % 2
                for tt in range(4):
                    pb = nmm % 4
                    nmm += 1
                    for kc in range(16):
                        P.op('pe', lambda e, w3=w3, kc=kc, tt=tt, pb=pb: e.matmul(pmm[pb][:], lhsT=wb[w3][:, kc, :], rhs=hT[:, kc, tt * 512:(tt + 1) * 512],
                                                                                start=(kc == 0), stop=(kc == 15)),
                             reads=['p_wb%d' % w3, 'p_hT'], writes=['p_mm%d' % pb])
                    if tt % 2 == 0:
                        P.op('act', lambda e, o2=o2, tt=tt, pb=pb: e.activation(out=ob[o2][:, tt * 512:(tt + 1) * 512], in_=pmm[pb][:], func=AF.Copy),
                             reads=['p_mm%d' % pb], writes=['p_ob%d' % o2], acc=True)
                    else:
                        P.op('dve', lambda e, o2=o2, tt=tt, pb=pb: e.tensor_copy(out=ob[o2][:, tt * 512:(tt + 1) * 512], in_=pmm[pb][:]),
                             reads=['p_mm%d' % pb], writes=['p_ob%d' % o2], acc=True)
                P.dma('sp', C.pT[c * 128:(c + 1) * 128, s * T:(s + 1) * T], ob[o2][:], reads=['p_ob%d' % o2], writes=['pT'], acc=True)
            for tt in range(4):
                pb = nmm % 4
                nmm += 1
                for kc in range(16):
                    P.op('pe', lambda e, kc=kc, tt=tt, pb=pb: e.matmul(pmm[pb][0:12, :], lhsT=wbab[:, kc, :], rhs=hT[:, kc, tt * 512:(tt + 1) * 512],
                                                                        start=(kc == 0), stop=(kc == 15)),
                         reads=['p_wbab', 'p_hT'], writes=['p_mm%d' % pb])
                P.op('dve', lambda e, tt=tt, pb=pb: e.tensor_copy(out=obf[:, tt * 512:(tt + 1) * 512], in_=pmm[pb][0:12, :]),
                     reads=['p_mm%d' % pb], writes=['p_obf'], acc=True)
            P.dma('sp', C.baT[:, s * T:(s + 1) * T], obf[:], reads=['p_obf'], writes=['baT'], acc=True)
            for i in range(16):
                pb = nmm % 4
                nmm += 1
                o2 = i % 2
                for kc in range(16):
                    P.op('pe', lambda e, kc=kc, i=i, pb=pb: e.matmul(pmm[pb][:, 0:256], lhsT=hT[:, kc, i * 128:(i + 1) * 128], rhs=wvb[:, kc, :],
                                                                      start=(kc == 0), stop=(kc == 15)),
                         reads=['p_wvb', 'p_hT'], writes=['p_mm%d' % pb])
                P.op('dve', lambda e, o2=o2, pb=pb: e.tensor_copy(out=ovb[o2][:], in_=pmm[pb][:, 0:256]),
                     reads=['p_mm%d' % pb], writes=['p_ovb%d' % o2])
                tok0 = s * T + i * 128
                P.dma('sp', C.va[tok0:tok0 + 128, :], ovb[o2][:], reads=['p_ovb%d' % o2], writes=['va'], acc=True)


def stage_outproj(P, nc, C, l, x_src, x_dst):
    with ExitStack() as st:
        SB = lambda n, s, d: st.enter_context(nc.sbuf_tensor("o_" + str(l) + n, s, d))
        PS = lambda n, s, d: st.enter_context(nc.psum_tensor("op_" + str(l) + n, s, d))
        wo = SB("wo", [128, 16, D], BF16)
        wf = [SB("wf%d" % i, [128, D], F32) for i in range(2)]
        yt = [SB("yt%d" % i, [128, 16, 128], BF16) for i in range(2)]
        xs = [SB("xs%d" % i, [128, D], F32) for i in range(2)]
        xo = [SB("xo%d" % i, [128, D], F32) for i in range(2)]
        pmm = [PS("mm%d" % i, [128, 512], F32) for i in range(8)]
        for kc in range(16):
            f = kc % 2
            P.dma('sp', wf[f][:], C.w_o[l, :, kc, :], writes=['o_wf%d' % f])
            P.op('pool' if kc % 2 == 0 else 'act',
                 (lambda e, f=f, kc=kc: e.tensor_copy(out=wo[:, kc, :], in_=wf[f][:])) if kc % 2 == 0 else
                 (lambda e, f=f, kc=kc: e.activation(out=wo[:, kc, :], in_=wf[f][:], func=AF.Copy)),
                 reads=['o_wf%d' % f], writes=['o_wo'], acc=True)
        def oload(i):
            b = i % 2
            tok0 = i * 128
            P.dma('sp', yt[b][:], C.yT[:, tok0:tok0 + 128].rearrange("(kc p) t -> p kc t", p=128), reads=['yT'], writes=['o_yt%d' % b])
            P.dma('sp', xs[b][:], x_src[tok0:tok0 + 128, :], reads=[C.xres_key(x_src)], writes=['o_xs%d' % b])

        nmm = 0
        NT = NTOK // 128
        oload(0)
        for i in range(NT):
            b = i % 2
            tok0 = i * 128
            if i + 1 < NT:
                oload(i + 1)
            for ct in range(4):
                pb = nmm % 8
                nmm += 1
                for kc in range(16):
                    P.op('pe', lambda e, b=b, kc=kc, ct=ct, pb=pb: e.matmul(pmm[pb][:], lhsT=yt[b][:, kc, :], rhs=wo[:, kc, ct * 512:(ct + 1) * 512],
                                                                            start=(kc == 0), stop=(kc == 15)),
                         reads=['o_yt%d' % b, 'o_wo'], writes=['o_mm%d' % pb])
                P.op('dve', lambda e, b=b, ct=ct, pb=pb: e.tensor_tensor(out=xo[b][:, ct * 512:(ct + 1) * 512], in0=xs[b][:, ct * 512:(ct + 1) * 512],
                                                                         in1=pmm[pb][:], op=ALU.add),
                     reads=['o_mm%d' % pb, 'o_xs%d' % b], writes=['o_xo%d' % b], acc=True)
            P.dma('sp', x_dst[tok0:tok0 + 128, :], xo[b][:], reads=['o_xo%d' % b], writes=[C.xres_key(x_dst)], acc=True)


def stage_conformer(P, nc, C, l):
    with ExitStack() as st:
        SB = lambda n, s, d: st.enter_context(nc.sbuf_tensor("sb_" + str(l) + n, s, d))
        PS = lambda n, s, d: st.enter_context(nc.psum_tensor("bp_" + str(l) + n, s, d))
        cw = SB("cw", [128, 4, 31], F32)
        cb = SB("cb", [128, 4], F32)
        lnw = SB("lnw", [128, 4], F32)
        lnb = SB("lnb", [128, 4], F32)
        pwbias = SB("pwbias", [128, 4], F32)
        pwf = SB("pwf", [128, 4, 512], F32)
        pwb = SB("pwb", [128, 4, 512], BF16)
        dg = SB("dg", [128, 124, 128], BF16)
        W = 542
        at = [SB("at%d" % i, [128, 4, W], BF16) for i in range(2)]
        gt = [SB("gt%d" % i, [128, 4, W], BF16) for i in range(2)]
        zt = [SB("zt%d" % i, [128, 4, 512], BF16) for i in range(2)]
        sig = SB("sig", [128, 4, W], BF16)
        hg = [SB("hg%d" % i, [128, 4, W], BF16) for i in range(2)]
        xc = [SB("xc%d" % i, [128, 4, 512], F32) for i in range(2)]
        xcb = [SB("xcb%d" % i, [128, 4, 512], BF16) for i in range(2)]
        sqb = [SB("sqb%d" % i, [128, 4, 512], BF16) for i in range(2)]
        mean = SB("mean", [128, 512], F32)
        m2 = SB("m2", [128, 512], F32)
        var = SB("var", [128, 512], F32)
        rstd = SB("rstd", [128, 512], F32)
        t1 = SB("t1", [128, 4, 512], F32)
        hs = SB("hs", [128, 4, 512], BF16)
        sz = SB("sz", [128, 4, 512], BF16)
        yb = [SB("yb%d" % i, [128, 4, 512], BF16) for i in range(2)]
        pc = [PS("c%d" % i, [128, 512], F32) for i in range(4)]
        pm = PS("m", [128, 512], F32)
        pq = PS("q", [128, 512], F32)
        po = [PS("o%d" % i, [128, 512], F32) for i in range(2)]
        P.excl.update(['b_pc%d' % i for i in range(4)] + ['b_pm', 'b_pq', 'b_po0', 'b_po1'])

        P.dma('sp', cw[:], C.b_cw[l], writes=['b_cw'])
        P.dma('sp', cb[:], C.b_cb[l], writes=['b_cb'])
        P.dma('sp', lnw[:], C.b_lnw[l], writes=['b_lnw'])
        P.dma('sp', lnb[:], C.b_lnb[l], writes=['b_lnb'])
        P.dma('sp', pwbias[:], C.b_pwb[l], writes=['b_pwbias'])
        P.dma('sp', pwf[:], C.b_pw[l], writes=['b_pwf'])
        P.op('dve', lambda e: e.tensor_copy(out=pwb[:], in_=pwf[:]), reads=['b_pwf'], writes=['b_pwb'])
        cwflat = cw[:].rearrange("p c k -> p (c k)")
        for eng, lo, hi in (('dve', 0, 84), ('pool', 84, 124)):
            P.op(eng, lambda e, lo=lo, hi=hi: e.tensor_tensor(out=dg[:, lo:hi, :], in0=C.ident_f[:].unsqueeze(1).broadcast_to([128, hi - lo, 128]),
                                                              in1=cwflat[:, lo:hi].unsqueeze(2).broadcast_to([128, hi - lo, 128]), op=ALU.mult),
                 reads=['b_cw', 'g_ident_f'], writes=['b_dg'], acc=True)

        tiles = [(s, tt) for s in range(NSEQ) for tt in range(4)]

        def front(i):
            s, tt = tiles[i]
            b = i % 2
            tok0 = s * T + tt * 512
            if tt == 0:
                P.op('pool', lambda e, b=b: e.memset(at[b][:, :, 0:30], 0.0), writes=['b_at%d' % b])
                P.op('pool', lambda e, b=b: e.memset(gt[b][:, :, 0:30], 0.0), writes=['b_gt%d' % b])
                P.dma('sp', at[b][:, :, 30:W], C.pT[UBA0:UBA0 + 512, tok0:tok0 + 512].rearrange("(c p) t -> p c t", p=128), reads=['pT'], writes=['b_at%d' % b])
                P.dma('sp', gt[b][:, :, 30:W], C.pT[UBG0:UBG0 + 512, tok0:tok0 + 512].rearrange("(c p) t -> p c t", p=128), reads=['pT'], writes=['b_gt%d' % b])
            else:
                P.dma('sp', at[b][:], C.pT[UBA0:UBA0 + 512, tok0 - 30:tok0 + 512].rearrange("(c p) t -> p c t", p=128), reads=['pT'], writes=['b_at%d' % b])
                P.dma('sp', gt[b][:], C.pT[UBG0:UBG0 + 512, tok0 - 30:tok0 + 512].rearrange("(c p) t -> p c t", p=128), reads=['pT'], writes=['b_gt%d' % b])
            P.dma('sp', zt[b][:], C.pT[ZB0:ZB0 + 512, tok0:tok0 + 512].rearrange("(c p) t -> p c t", p=128), reads=['pT'], writes=['b_zt%d' % b])
            P.op('act', lambda e, b=b: e.activation(out=sig[:], in_=gt[b][:], func=AF.Sigmoid), reads=['b_gt%d' % b], writes=['b_sig'])
            P.op('dve', lambda e, b=b: e.tensor_tensor(out=hg[b][:], in0=at[b][:], in1=sig[:], op=ALU.mult), reads=['b_at%d' % b, 'b_sig'], writes=['b_hg%d' % b])

        def conv(i):
            b = i % 2
            for c in range(4):
                for k in range(31):
                    P.op('pe', lambda e, c=c, k=k, b=b: e.matmul(pc[c][:], lhsT=dg[:, c * 31 + k, :], rhs=hg[b][:, c, k:k + 512], start=(k == 0), stop=(k == 30)),
                         reads=['b_dg', 'b_hg%d' % b], writes=['b_pc%d' % c])
                P.op('act', lambda e, c=c, b=b: e.activation(out=xc[b][:, c, :], in_=pc[c][:], func=AF.Identity, bias=cb[:, c:c + 1]),
                     reads=['b_pc%d' % c, 'b_cb'], writes=['b_xc%d' % b], acc=True)
                P.op('dve', lambda e, c=c, b=b: e.tensor_scalar(out=xcb[b][:, c, :], in0=pc[c][:], scalar1=cb[:, c:c + 1], scalar2=None, op0=ALU.add),
                     reads=['b_pc%d' % c, 'b_cb'], writes=['b_xcb%d' % b], acc=True)
                P.op('act', lambda e, c=c, b=b: e.activation(out=sqb[b][:, c, :], in_=pc[c][:], func=AF.Square, bias=cb[:, c:c + 1]),
                     reads=['b_pc%d' % c, 'b_cb'], writes=['b_sqb%d' % b], acc=True)

        def back(i):
            s, tt = tiles[i]
            b = i % 2
            tok0 = s * T + tt * 512
            for c in range(4):
                P.op('pe', lambda e, c=c, b=b: e.matmul(pm[:], lhsT=C.ones_b[:], rhs=xcb[b][:, c, :], start=(c == 0), stop=(c == 3)),
                     reads=['b_xcb%d' % b, 'g_ones_b'], writes=['b_pm'])
            for c in range(4):
                P.op('pe', lambda e, c=c, b=b: e.matmul(pq[:], lhsT=C.ones_b[:], rhs=sqb[b][:, c, :], start=(c == 0), stop=(c == 3)),
                     reads=['b_sqb%d' % b, 'g_ones_b'], writes=['b_pq'])
            P.op('act', lambda e: e.activation(out=mean[:], in_=pm[:], func=AF.Copy, scale=1.0 / 512), reads=['b_pm'], writes=['b_mean'])
            P.op('pool', lambda e: e.tensor_tensor(out=m2[:], in0=mean[:], in1=mean[:], op=ALU.mult), reads=['b_mean'], writes=['b_m2'])
            P.op('dve', lambda e: e.scalar_tensor_tensor(out=var[:], in0=pq[:], scalar=1.0 / 512, in1=m2[:], op0=ALU.mult, op1=ALU.subtract),
                 reads=['b_pq', 'b_m2'], writes=['b_var'])
            P.op('act', lambda e: e.activation(out=rstd[:], in_=var[:], func=AF.Ln, bias=C.eps_col[:, 0:1]), reads=['b_var'], writes=['b_rstd'])
            P.op('act', lambda e: e.activation(out=rstd[:], in_=rstd[:], func=AF.Exp, scale=-0.5), reads=['b_rstd'], writes=['b_rstd'])
            P.op('act', lambda e, b=b: e.activation(out=sz[:], in_=zt[b][:], func=AF.Silu), reads=['b_zt%d' % b], writes=['b_sz'])
            for c in range(4):
                P.op('pool', lambda e, c=c, b=b: e.tensor_tensor(out=t1[:, c, :], in0=xc[b][:, c, :], in1=mean[:], op=ALU.subtract),
                     reads=['b_xc%d' % b, 'b_mean'], writes=['b_t1_%d' % c])
                P.op('dve', lambda e, c=c: e.tensor_tensor(out=t1[:, c, :], in0=t1[:, c, :], in1=rstd[:], op=ALU.mult),
                     reads=['b_t1_%d' % c, 'b_rstd'], writes=['b_t1_%d' % c])
                P.op('act', lambda e, c=c: e.activation(out=hs[:, c, :], in_=t1[:, c, :], func=AF.Silu, scale=lnw[:, c:c + 1], bias=lnb[:, c:c + 1]),
                     reads=['b_t1_%d' % c, 'b_lnw', 'b_lnb'], writes=['b_hs'], acc=True)
            for co in range(4):
                o2 = co % 2
                for ci in range(4):
                    P.op('pe', lambda e, co=co, ci=ci, o2=o2: e.matmul(po[o2][:], lhsT=pwb[:, ci, co * 128:(co + 1) * 128], rhs=hs[:, ci, :], start=(ci == 0), stop=(ci == 3)),
                         reads=['b_pwb', 'b_hs'], writes=['b_po%d' % o2])
                P.op('dve', lambda e, co=co, b=b, o2=o2: e.scalar_tensor_tensor(out=yb[b][:, co, :], in0=po[o2][:], scalar=pwbias[:, co:co + 1], in1=sz[:, co, :],
                                                                                op0=ALU.add, op1=ALU.mult),
                     reads=['b_po%d' % o2, 'b_pwbias', 'b_sz'], writes=['b_yb%d' % b], acc=True)
            P.dma('sp', C.yT[YB0:YB0 + 512, tok0:tok0 + 512].rearrange("(c p) t -> p c t", p=128), yb[b][:], reads=['b_yb%d' % b], writes=['yT'], acc=True)

        NT = len(tiles)
        front(0)
        conv(0)
        for i in range(NT):
            if i + 1 < NT:
                front(i + 1)
                conv(i + 1)
            back(i)


def stage_attn(P, nc, C, l):
    TWO_PI = 6.283185307179586
    C1 = 6.28125
    C2 = TWO_PI - C1
    MAGIC = 12582912.0
    with ExitStack() as st:
        SB = lambda n, s, d: st.enter_context(nc.sbuf_tensor("sa_" + str(l) + n, s, d))
        PS = lambda n, s, d: st.enter_context(nc.psum_tensor("ap_" + str(l) + n, s, d))
        invf = SB("invf", [64, 1], F32)
        rtf = SB("rtf", [64, 64], F32)
        rtb = SB("rtb", [64, 64], BF16)
        mkf = SB("mkf", [128, 2, 128], F32)
        mkb = SB("mkb", [128, 2, 128], BF16)
        qw = SB("qw", [64, 1], F32)
        kw = SB("kw", [64, 1], F32)
        snk = SB("snk", [1, 12], F32)
        snkb = SB("snkb", [1, 12, 128], BF16)
        posi = SB("posi", [64, T], I32)
        cosT = SB("cosT", [64, T], F32)
        sinT = SB("sinT", [64, T], F32)
        sq = SB("sq", [64, T], BF16)
        rstd = SB("rstd", [64, T], F32)
        xn = SB("xn", [64, T], BF16)
        t1 = SB("t1", [64, T], F32)
        t2 = SB("t2", [64, T], F32)
        ang, tn, tr = rstd, t1, t2
        kANG, kTN, kTR = 'a_rstd', 'a_t1', 'a_t2'
        kT = [SB("kT%d" % i, [64, T], BF16) for i in range(2)]
        qT = [SB("qT%d" % i, [64, 3, T], BF16) for i in range(2)]
        zaT = [SB("zaT%d" % i, [64, 3, T], BF16) for i in range(2)]
        vt = [SB("vt%d" % i, [128, 16, 64], BF16) for i in range(2)]
        kr = [SB("kr%d" % i, [64, T], BF16) for i in range(2)]
        qr = [SB("qr%d" % i, [64, 3, T], BF16) for i in range(2)]
        ya = [SB("ya%d" % i, [64, 3, T], BF16) for i in range(2)]
        pe_sb = [SB("pe%d" % i, [128, 2, 384], BF16) for i in range(2)]
        pm_sb = [SB("pm%d" % i, [128, 2, 384], BF16) for i in range(2)]
        rden = SB("rden", [64, 384], F32)
        o_sb = SB("o_sb", [64, 384], F32)
        ps = [PS("b%d" % i, [128, 512], F32) for i in range(8)]
        PK = ['a_ps%d' % i for i in range(8)]
        P.excl.update(PK)
        P.dma('sp', invf[:], C.a_invf, writes=['a_invf'])
        P.dma('sp', rtf[:], C.a_rt, writes=['a_rtf'])
        P.dma('sp', mkf[:], C.a_mask, writes=['a_mkf'])
        P.dma('sp', qw[:], C.a_qw[l], writes=['a_qw'])
        P.dma('sp', kw[:], C.a_kw[l], writes=['a_kw'])
        P.dma('sp', snk[:], C.a_snk[l], writes=['a_snk'])
        P.op('dve', lambda e: e.tensor_copy(out=rtb[:], in_=rtf[:]), reads=['a_rtf'], writes=['a_rtb'])
        P.op('dve', lambda e: e.tensor_copy(out=mkb[:], in_=mkf[:]), reads=['a_mkf'], writes=['a_mkb'])
        P.op('act', lambda e: e.activation(out=snk[:], in_=snk[:], func=AF.Exp), reads=['a_snk'], writes=['a_snk'])
        P.op('dve', lambda e: e.tensor_copy(out=snkb[:], in_=snk[:, :].unsqueeze(2).broadcast_to([1, 12, 128])), reads=['a_snk'], writes=['a_snkb'])

        def sincos(dst, shift):
            P.op('dve', lambda e: e.tensor_scalar(out=tn[:], in0=ang[:], scalar1=shift, scalar2=1.0 / TWO_PI, op0=ALU.add, op1=ALU.mult),
                 reads=[kANG], writes=[kTN])
            P.op('dve', lambda e: e.tensor_scalar(out=tn[:], in0=tn[:], scalar1=MAGIC, scalar2=None, op0=ALU.add), reads=[kTN], writes=[kTN])
            P.op('dve', lambda e: e.tensor_scalar(out=tn[:], in0=tn[:], scalar1=-MAGIC, scalar2=None, op0=ALU.add), reads=[kTN], writes=[kTN])
            P.op('dve', lambda e: e.scalar_tensor_tensor(out=tr[:], in0=tn[:], scalar=-C1, in1=ang[:], op0=ALU.mult, op1=ALU.add),
                 reads=[kTN, kANG], writes=[kTR])
            P.op('dve', lambda e: e.tensor_scalar(out=tr[:], in0=tr[:], scalar1=shift, scalar2=None, op0=ALU.add), reads=[kTR], writes=[kTR])
            P.op('dve', lambda e: e.scalar_tensor_tensor(out=tr[:], in0=tn[:], scalar=-C2, in1=tr[:], op0=ALU.mult, op1=ALU.add),
                 reads=[kTN, kTR], writes=[kTR])
            P.op('dve', lambda e: e.tensor_scalar(out=tr[:], in0=tr[:], scalar1=3.1415925, scalar2=-3.1415925, op0=ALU.min, op1=ALU.max),
                 reads=[kTR], writes=[kTR])
            P.op('act', lambda e: e.activation(out=dst[:], in_=tr[:], func=AF.Sin), reads=[kTR], writes=['a_tab'])

        def tables(s):
            P.dma('sp', posi[:], C.pos[s:s + 1, :].broadcast_to([64, T]), writes=['a_posi'])
            P.op('dve', lambda e: e.tensor_copy(out=ang[:], in_=posi[:]), reads=['a_posi'], writes=[kANG])
            P.op('dve', lambda e: e.tensor_scalar(out=ang[:], in0=ang[:], scalar1=invf[:, 0:1], scalar2=None, op0=ALU.mult), reads=[kANG, 'a_invf'], writes=[kANG])
            sincos(sinT, 0.0)
            sincos(cosT, 1.5707963267948966)

        def normrope(src, wcol, wkey, dst, skey, dkey):
            NB = 7
            P.op('act', lambda e: e.activation(out=sq[:], in_=src, func=AF.Square), reads=[skey], writes=['a_sq'])
            for q4 in range(4):
                P.op('pe', lambda e, q4=q4: e.matmul(ps[NB][0:64, :], lhsT=C.ones_b[0:64, 0:64], rhs=sq[:, q4 * 512:(q4 + 1) * 512], start=True, stop=True),
                     reads=['a_sq', 'g_ones_b'], writes=[PK[NB]])
                P.op('act', lambda e, q4=q4: e.activation(out=rstd[:, q4 * 512:(q4 + 1) * 512], in_=ps[NB][0:64, :], func=AF.Ln, scale=1.0 / 64, bias=C.eps_col[0:64, 0:1]),
                     reads=[PK[NB]], writes=['a_rstd'], acc=True)
            yield
            P.op('act', lambda e: e.activation(out=rstd[:], in_=rstd[:], func=AF.Exp, scale=-0.5), reads=['a_rstd'], writes=['a_rstd'])
            P.op('dve', lambda e: e.scalar_tensor_tensor(out=xn[:], in0=src, scalar=wcol[:, 0:1], in1=rstd[:], op0=ALU.mult, op1=ALU.mult),
                 reads=[skey, wkey, 'a_rstd'], writes=['a_xn'])
            yield
            P.op('pool', lambda e: e.tensor_tensor(out=t1[:], in0=xn[:], in1=cosT[:], op=ALU.mult), reads=['a_xn', 'a_tab'], writes=['a_t1'])
            for q4 in range(4):
                P.op('pe', lambda e, q4=q4: e.matmul(ps[NB][0:64, :], lhsT=rtb[:], rhs=xn[:, q4 * 512:(q4 + 1) * 512], start=True, stop=True),
                     reads=['a_xn', 'a_rtb'], writes=[PK[NB]])
                P.op('dve', lambda e, q4=q4: e.tensor_tensor(out=t2[:, q4 * 512:(q4 + 1) * 512], in0=ps[NB][0:64, :], in1=sinT[:, q4 * 512:(q4 + 1) * 512], op=ALU.mult),
                     reads=[PK[NB], 'a_tab'], writes=['a_t2'], acc=True)
                yield
            P.op('pool', lambda e: e.tensor_tensor(out=dst, in0=t1[:], in1=t2[:], op=ALU.add), reads=['a_t1', 'a_t2'], writes=[dkey], acc=True)
            yield

        def prep(s, g, gp):
            c0 = s * T
            G = lambda n: 'a_%s%d' % (n, gp)
            P.dma('sp', kT[gp][:], C.pT[K0 + g * 64:K0 + (g + 1) * 64, c0:c0 + T], reads=['pT'], writes=[G('kT')])
            P.dma('sp', qT[gp][:], C.pT[Q0 + 3 * g * 64:Q0 + 3 * (g + 1) * 64, c0:c0 + T].rearrange("(j d) t -> d j t", d=64), reads=['pT'], writes=[G('qT')])
            P.dma('sp', zaT[gp][:], C.pT[ZA0 + 3 * g * 64:ZA0 + 3 * (g + 1) * 64, c0:c0 + T].rearrange("(j d) t -> d j t", d=64), reads=['pT'], writes=[G('zaT')])
            P.dma('sp', vt[gp][:], C.va[c0:c0 + T, g * 64:(g + 1) * 64].rearrange("(n p) d -> p n d", p=128), reads=['va'], writes=[G('vt')])
            yield
            yield from normrope(kT[gp][:], kw, 'a_kw', kr[gp][:], G('kT'), G('kr'))
            for j in range(3):
                yield from normrope(qT[gp][:, j, :], qw, 'a_qw', qr[gp][:, j, :], G('qT'), G('qr'))
            P.op('act', lambda e: e.activation(out=zaT[gp][:], in_=zaT[gp][:], func=AF.Silu), reads=[G('zaT')], writes=[G('zaT')])
            yield

        def blocks(s, g, gp):
            c0 = s * T
            G = lambda n: 'a_%s%d' % (n, gp)

            def stage1(n):
                par = n % 2
                ipb, icb = 2 * par, 2 * par + 1
                qs = qr[gp][:, :, n * 128:(n + 1) * 128]
                P.op('pe', lambda e: e.matmul(ps[icb][:, 0:384], lhsT=kr[gp][:, n * 128:(n + 1) * 128], rhs=qs, start=True, stop=True),
                     reads=[G('kr'), G('qr')], writes=[PK[icb]])
                if n > 0:
                    P.op('pe', lambda e: e.matmul(ps[ipb][:, 0:384], lhsT=kr[gp][:, (n - 1) * 128:n * 128], rhs=qs, start=True, stop=True),
                         reads=[G('kr'), G('qr')], writes=[PK[ipb]])
                P.op('act', lambda e: e.activation(out=pe_sb[par][:, 1, :], in_=ps[icb][:, 0:384], func=AF.Exp, scale=0.125),
                     reads=[PK[icb]], writes=['a_pe%d' % par], acc=True)
                if n > 0:
                    P.op('act', lambda e: e.activation(out=pe_sb[par][:, 0, :], in_=ps[ipb][:, 0:384], func=AF.Exp, scale=0.125),
                         reads=[PK[ipb]], writes=['a_pe%d' % par], acc=True)
                lo = 0 if n > 0 else 1
                meng = 'dve'
                P.op(meng, lambda e: e.tensor_tensor(
                    out=pm_sb[par][:, lo:2, :].rearrange("p a (h q) -> p a h q", h=3),
                    in0=pe_sb[par][:, lo:2, :].rearrange("p a (h q) -> p a h q", h=3),
                    in1=mkb[:, lo:2, :].unsqueeze(2).broadcast_to([128, 2 - lo, 3, 128]), op=ALU.mult),
                    reads=['a_pe%d' % par, 'a_mkb'], writes=['a_pm%d' % par])

            def stage2(n):
                par = n % 2
                iob, idb = 4 + par, 6
                if n > 0:
                    P.op('pe', lambda e: e.matmul(ps[iob][0:64, 0:384], lhsT=vt[gp][:, n - 1, :], rhs=pm_sb[par][:, 0, :], start=True, stop=False),
                         reads=[G('vt'), 'a_pm%d' % par], writes=[PK[iob]])
                P.op('pe', lambda e: e.matmul(ps[iob][0:64, 0:384], lhsT=vt[gp][:, n, :], rhs=pm_sb[par][:, 1, :], start=(n == 0), stop=True),
                     reads=[G('vt'), 'a_pm%d' % par], writes=[PK[iob]])
                if n > 0:
                    P.op('pe', lambda e: e.matmul(ps[idb][0:64, 0:384], lhsT=C.ones_b[:, 0:64], rhs=pm_sb[par][:, 0, :], start=True, stop=False),
                         reads=['g_ones_b', 'a_pm%d' % par], writes=[PK[idb]])
                P.op('pe', lambda e: e.matmul(ps[idb][0:64, 0:384], lhsT=C.ones_b[:, 0:64], rhs=pm_sb[par][:, 1, :], start=(n == 0), stop=False),
                     reads=['g_ones_b', 'a_pm%d' % par], writes=[PK[idb]])
                P.op('pe', lambda e: e.matmul(ps[idb][0:64, 0:384], lhsT=C.ones_b[0:1, 0:64], rhs=snkb[0:1, 3 * g:3 * g + 3, :], start=False, stop=True),
                     reads=['g_ones_b', 'a_snkb'], writes=[PK[idb]])
                P.op('act', lambda e: e.activation(out=rden[:], in_=ps[idb][0:64, 0:384], func=AF.Ln), reads=[PK[idb]], writes=['a_rden'])
                P.op('act', lambda e: e.activation(out=rden[:], in_=rden[:], func=AF.Exp, scale=-1.0), reads=['a_rden'], writes=['a_rden'])
                P.op('dve', lambda e: e.tensor_tensor(out=o_sb[:], in0=ps[iob][0:64, 0:384], in1=rden[:], op=ALU.mult), reads=[PK[iob], 'a_rden'], writes=['a_osb'])
                P.op('pool', lambda e: e.tensor_tensor(out=ya[gp][:, :, n * 128:(n + 1) * 128], in0=o_sb[:].rearrange("p (h q) -> p h q", h=3),
                                                       in1=zaT[gp][:, :, n * 128:(n + 1) * 128], op=ALU.mult),
                     reads=['a_osb', G('zaT')], writes=[G('ya')], acc=True)

            stage1(0)
            yield
            for n in range(16):
                if n + 1 < 16:
                    stage1(n + 1)
                    yield
                stage2(n)
                yield
            P.dma('sp', C.yT[YA0 + 3 * g * 64:YA0 + 3 * (g + 1) * 64, c0:c0 + T].rearrange("(j d) t -> d j t", d=64), ya[gp][:], reads=[G('ya')], writes=['yT'], acc=True)

        def merge(gens):
            live = [[g, w] for g, w in gens]
            while live:
                for item in list(live):
                    g, w = item
                    for _ in range(w):
                        try:
                            next(g)
                        except StopIteration:
                            live.remove(item)
                            break

        work = [(s, g) for s in range(NSEQ) for g in range(4)]
        tables(0)
        merge([[prep(work[0][0], work[0][1], 0), 1]])
        for i, (s, g) in enumerate(work):
            gl = [[blocks(s, g, i % 2), 2]]
            if i + 1 < len(work):
                s2, g2 = work[i + 1]
                if s2 != s:
                    tables(s2)
                gl.append([prep(s2, g2, (i + 1) % 2), 1])
            merge(gl)


def stage_gdn(P, nc, C, l):
    NH = 6
    with ExitStack() as st:
        SB = lambda n, s, d: st.enter_context(nc.sbuf_tensor("sc_" + str(l) + n, s, d))
        PS = lambda n, s, d: st.enter_context(nc.psum_tensor("cp_" + str(l) + n, s, d))
        cwc = SB("cwc", [128, 18, 4], F32)
        dgc = SB("dgc", [128, 72, 128], BF16)
        onw = SB("onw", [128, 1], F32)
        alog = SB("alog", [6, 1], F32)
        dtb = SB("dtb", [6, 1], F32)
        lmf = SB("lmf", [128, 7, 128], F32)
        lmb = SB("lmb", [128, 7, 128], BF16)
        ngm = SB("ngm", [128, 128], F32)
        sel = SB("sel", [6, 6, 128], F32)
        onesf = SB("onesf", [6, 128], F32)
        beta1 = SB("beta", [6, T], F32)
        gg1 = SB("gg", [6, T], F32)
        gc1 = SB("gc", [6, T], F32)
        egl1 = SB("egl", [6, 16], F32)
        egd1 = SB("egd", [6, 16, 6], F32)
        S32_1 = SB("S32", [128, NH, 128], F32)
        Sb_1 = SB("Sb", [128, NH, 128], BF16)
        braw = araw = beta = gg = gc = edl = egl = egd = S32 = Sb = None
        yc = [SB("yc%d" % i, [128, NH, 512], BF16) for i in range(2)]
        zc = [SB("zc%d" % i, [128, NH, 512], BF16) for i in range(2)]
        szc = zc
        def two(n, shp, d):
            return [SB("%s%d" % (n, i), shp, d) for i in range(2)]
        xin = two("xin", [128, 18, 131], BF16)
        qkv0_1 = SB("qkv0", [128, 18, 128], F32)
        qkv0 = [qkv0_1, qkv0_1]
        sqt = two("sqt", [128, NH, 128], BF16)
        rn = two("rn", [128, NH, 128], F32)
        KT = two("KT", [128, NH, 128], BF16)
        QT = two("QT", [128, NH, 128], BF16)
        VT = two("VT", [128, NH, 128], BF16)
        KgT = two("KgT", [128, NH, 128], BF16)
        QgT = two("QgT", [128, NH, 128], BF16)
        egcb = two("egcb", [128, NH, 128], F32)
        tmpa = two("tmpa", [128, NH, 128], F32)
        LTb = two("LTb", [128, NH, 128], BF16)
        ET = two("ET", [128, NH, 128], F32)
        cols = two("cols", [128, 24], F32)
        intraT = two("intraT", [128, NH, 128], BF16)
        ktil = two("ktil", [128, NH, 128], BF16)
        BsT = two("BsT", [128, NH, 128], BF16)
        Zsb = two("Zsb", [128, NH, 128], BF16)
        XmA = two("XmA", [128, NH, 128], BF16)
        XmB = two("XmB", [128, NH, 128], BF16)
        XTA = two("XTA", [128, NH, 128], BF16)
        XTB = two("XTB", [128, NH, 128], BF16)
        rsb = two("rsb", [128, NH, 128], BF16)
        vnew = two("vnew", [128, NH, 128], BF16)
        o32 = egcb
        ro = rn
        NSLOT = 4
        slots = [PS("s%d" % i, [128, 8, 128], F32) for i in range(NSLOT)]
        SK = ['c_slot%d' % i for i in range(NSLOT)]
        P.excl.update(SK)
        slot_ctr = [0]

        def nslot():
            i = slot_ctr[0] % NSLOT
            slot_ctr[0] += 1
            return slots[i], SK[i]

        def bc(colap):
            return colap.unsqueeze(2).broadcast_to([128, NH, 128])

        P.dma('sp', cwc[:], C.c_cw[l], writes=['c_cwc'])
        P.dma('sp', onw[:], C.c_onw[l], writes=['c_onw'])
        P.dma('sp', alog[:], C.c_alog[l], writes=['c_alog'])
        P.dma('sp', dtb[:], C.c_dtb[l], writes=['c_dtb'])
        P.dma('sp', lmf[:], C.c_lm, writes=['c_lmf'])
        P.dma('sp', ngm[:], C.c_ngm, writes=['c_ngm'])
        P.dma('sp', sel[:], C.c_sel, writes=['c_sel'])
        P.op('dve', lambda e: e.tensor_copy(out=lmb[:], in_=lmf[:]), reads=['c_lmf'], writes=['c_lmb'])
        P.op('pool', lambda e: e.memset(onesf[:], 1.0), writes=['c_onesf'])
        P.op('act', lambda e: e.activation(out=alog[:], in_=alog[:], func=AF.Exp), reads=['c_alog'], writes=['c_alog'])
        P.op('dve', lambda e: e.tensor_scalar(out=alog[:], in0=alog[:], scalar1=-1.0, scalar2=None, op0=ALU.mult), reads=['c_alog'], writes=['c_alog'])
        cwcflat = cwc[:].rearrange("p c k -> p (c k)")
        for eng, lo, hi in (('dve', 0, 48), ('pool', 48, 72)):
            P.op(eng, lambda e, lo=lo, hi=hi: e.tensor_tensor(out=dgc[:, lo:hi, :], in0=C.ident_f[:].unsqueeze(1).broadcast_to([128, hi - lo, 128]),
                                                              in1=cwcflat[:, lo:hi].unsqueeze(2).broadcast_to([128, hi - lo, 128]), op=ALU.mult),
                 reads=['c_cwc', 'g_ident_f'], writes=['c_dgc'], acc=True)

        unit = 0
        for s in range(NSEQ):
            c0 = s * T
            beta, gg, gc, edl, egl, egd, S32, Sb = beta1, gg1, gc1, gg1, egl1, egd1, S32_1, Sb_1
            P.dma('sp', beta[:], C.baT[0:6, c0:c0 + T], reads=['baT'], writes=['c_beta'])
            P.dma('sp', gg[:], C.baT[6:12, c0:c0 + T], reads=['baT'], writes=['c_gg'])
            P.op('act', lambda e: e.activation(out=beta[:], in_=beta[:], func=AF.Sigmoid), reads=['c_beta'], writes=['c_beta'])
            P.op('act', lambda e: e.activation(out=gg[:], in_=gg[:], func=AF.Exp, bias=dtb[:, 0:1]), reads=['c_gg', 'c_dtb'], writes=['c_gg'])
            P.op('act', lambda e: e.activation(out=gg[:], in_=gg[:], func=AF.Ln, bias=C.one_col[0:6, 0:1]), reads=['c_gg'], writes=['c_gg'])
            P.op('dve', lambda e: e.tensor_scalar(out=gg[:], in0=gg[:], scalar1=alog[:, 0:1], scalar2=None, op0=ALU.mult), reads=['c_gg', 'c_alog'], writes=['c_gg'])
            for c in range(16):
                P.op('dve', lambda e, c=c: e.tensor_tensor_scan(out=gc[:, c * 128:(c + 1) * 128], data0=onesf[:], data1=gg[:, c * 128:(c + 1) * 128], initial=0.0, op0=ALU.mult, op1=ALU.add),
                     reads=['c_gg', 'c_onesf'], writes=['c_gc'], acc=True)
            gc3 = gc[:].rearrange("p (c t) -> p c t", t=128)
            P.op('dve', lambda e, gc3=gc3: e.tensor_tensor(out=edl[:].rearrange("p (c t) -> p c t", t=128), in0=gc3[:, :, 127:128].broadcast_to([6, 16, 128]), in1=gc3, op=ALU.subtract),
                 reads=['c_gc', 'c_gg'], writes=['c_gg'])
            P.op('act', lambda e: e.activation(out=edl[:], in_=edl[:], func=AF.Exp), reads=['c_gg'], writes=['c_gg'])
            P.op('act', lambda e, gc3=gc3: e.activation(out=egl[:].unsqueeze(2), in_=gc3[:, :, 127:128], func=AF.Exp), reads=['c_gc'], writes=['c_egl'])
            P.op('dve', lambda e: e.tensor_tensor(out=egd[:], in0=C.ident_f[0:6, 0:6].unsqueeze(1).broadcast_to([6, 16, 6]), in1=egl[:].unsqueeze(2).broadcast_to([6, 16, 6]), op=ALU.mult),
                 reads=['c_egl', 'g_ident_f'], writes=['c_egd'])
            P.op('pool', lambda e: e.memset(S32[:], 0.0), writes=['c_S32'])
            P.op('pool', lambda e: e.memset(Sb[:], 0.0), writes=['c_Sb'])
            def pre(c, u):
                U = lambda n, u=u: 'c_%s%d' % (n, u)
                tok0 = s * T + c * 128
                y2 = (c // 4) % 2
                if c % 4 == 0:
                    t512 = s * T + (c // 4) * 512
                    P.dma('sp', zc[y2][:], C.pT[ZC0:ZC0 + 768, t512:t512 + 512].rearrange("(h p) t -> p h t", p=128), reads=['pT'], writes=['c_zc%d' % y2])
                    P.op('act', lambda e, y2=y2: e.activation(out=zc[y2][:], in_=zc[y2][:], func=AF.Silu), reads=['c_zc%d' % y2], writes=['c_zc%d' % y2])
                if c == 0:
                    P.op('pool', lambda e, u=u: e.memset(xin[u][:, :, 0:3], 0.0), writes=[U('xin')])
                    P.dma('sp', xin[u][:, :, 3:131], C.pT[CQ0:CQ0 + 2304, tok0:tok0 + 128].rearrange("(cc p) t -> p cc t", p=128), reads=['pT'], writes=[U('xin')])
                else:
                    P.dma('sp', xin[u][:], C.pT[CQ0:CQ0 + 2304, tok0 - 3:tok0 + 128].rearrange("(cc p) t -> p cc t", p=128), reads=['pT'], writes=[U('xin')])
                for grp in range(3):
                    sl, sk = nslot()
                    for hh in range(6):
                        cc = grp * 6 + hh
                        for k in range(4):
                            P.op('pe', lambda e, u=u, cc=cc, k=k, hh=hh, sl=sl: e.matmul(sl[:, hh, :], lhsT=dgc[:, cc * 4 + k, :], rhs=xin[u][:, cc, k:k + 128], start=(k == 0), stop=(k == 3)),
                                 reads=['c_dgc', U('xin')], writes=[sk])
                    P.op('act', lambda e, grp=grp, sl=sl: e.activation(out=qkv0_1[:, grp * 6:(grp + 1) * 6, :], in_=sl[:, 0:6, :], func=AF.Silu),
                         reads=[sk], writes=['c_qkv0_%d' % grp])
                    yield
                for grp, dst, scl in ((0, QT, 128.0 ** -0.5), (1, KT, 1.0)):
                    src = qkv0_1[:, grp * 6:(grp + 1) * 6, :]
                    P.op('act', lambda e, u=u, src=src: e.activation(out=sqt[u][:], in_=src, func=AF.Square), reads=['c_qkv0_%d' % grp], writes=[U('sqt')])
                    sl, sk = nslot()
                    P.op('pe', lambda e, u=u, sl=sl: e.matmul(sl[:, 0:4, :], lhsT=C.ones_b[:], rhs=sqt[u][:, 0:4, :], start=True, stop=True),
                         reads=['g_ones_b', U('sqt')], writes=[sk])
                    P.op('pe', lambda e, u=u, sl=sl: e.matmul(sl[:, 4:6, :], lhsT=C.ones_b[:], rhs=sqt[u][:, 4:6, :], start=True, stop=True),
                         reads=['g_ones_b', U('sqt')], writes=[sk])
                    P.op('act', lambda e, u=u, sl=sl: e.activation(out=rn[u][:], in_=sl[:, 0:6, :], func=AF.Ln, bias=C.eps_col[:, 0:1]), reads=[sk], writes=[U('rn')])
                    P.op('act', lambda e, u=u: e.activation(out=rn[u][:], in_=rn[u][:], func=AF.Exp, scale=-0.5), reads=[U('rn')], writes=[U('rn')])
                    P.op('dve', lambda e, u=u, src=src, dst=dst, scl=scl: e.scalar_tensor_tensor(out=dst[u][:], in0=src, scalar=scl, in1=rn[u][:], op0=ALU.mult, op1=ALU.mult),
                         reads=['c_qkv0_%d' % grp, U('rn')], writes=[U('QT' if grp == 0 else 'KT')])
                    yield
                P.op('act', lambda e, u=u: e.activation(out=VT[u][:], in_=qkv0_1[:, 12:18, :], func=AF.Copy), reads=['c_qkv0_2'], writes=[U('VT')])
                sl, sk = nslot()
                tsl = slice(c * 128, (c + 1) * 128)
                P.op('pe', lambda e, sl=sl, tsl=tsl: e.matmul(sl[:, 0, 0:6], lhsT=beta[:, tsl], rhs=C.ident_f[0:6, 0:6], start=True, stop=True),
                     reads=['c_beta', 'g_ident_f'], writes=[sk])
                P.op('pe', lambda e, sl=sl, tsl=tsl: e.matmul(sl[:, 0, 6:12], lhsT=gc[:, tsl], rhs=C.ident_f[0:6, 0:6], start=True, stop=True),
                     reads=['c_gc', 'g_ident_f'], writes=[sk])
                P.op('pe', lambda e, sl=sl, tsl=tsl: e.matmul(sl[:, 0, 12:18], lhsT=edl[:, tsl], rhs=C.ident_f[0:6, 0:6], start=True, stop=True),
                     reads=['c_gg', 'g_ident_f'], writes=[sk])
                P.op('pe', lambda e, sl=sl, c=c: e.matmul(sl[:, 0, 18:24], lhsT=onesf[:], rhs=egd[:, c, :], start=True, stop=True),
                     reads=['c_egd', 'c_onesf'], writes=[sk])
                P.op('dve', lambda e, u=u, sl=sl: e.tensor_copy(out=cols[u][:], in_=sl[:, 0, 0:24]), reads=[sk], writes=[U('cols')])
                yield
                sl, sk = nslot()
                for hh in range(6):
                    P.op('pe', lambda e, hh=hh, sl=sl, tsl=tsl: e.matmul(sl[:, hh, :], lhsT=sel[:, hh, :], rhs=gc[:, tsl], start=True, stop=True),
                         reads=['c_sel', 'c_gc'], writes=[sk])
                P.op('act', lambda e, u=u, sl=sl: e.activation(out=egcb[u][:], in_=sl[:, 0:6, :], func=AF.Exp), reads=[sk], writes=[U('egcb')])
                for hh in range(6):
                    P.op('dve', lambda e, u=u, hh=hh, sl=sl: e.scalar_tensor_tensor(out=tmpa[u][:, hh, :], in0=sl[:, hh, :], scalar=cols[u][:, 6 + hh:7 + hh], in1=ngm[:],
                                                                                    op0=ALU.subtract, op1=ALU.add),
                         reads=[sk, U('cols'), 'c_ngm'], writes=[U('tmpa')], acc=True)
                P.op('act', lambda e, u=u: e.activation(out=ET[u][:], in_=tmpa[u][:], func=AF.Exp), reads=[U('tmpa')], writes=[U('ET')])
                yield
                P.op('dve', lambda e, u=u: e.scalar_tensor_tensor(out=KgT[u][:], in0=KT[u][:], scalar=-1.0, in1=egcb[u][:], op0=ALU.mult, op1=ALU.mult),
                     reads=[U('KT'), U('egcb')], writes=[U('KgT')])
                P.op('pool', lambda e, u=u: e.tensor_tensor(out=QgT[u][:], in0=QT[u][:], in1=egcb[u][:], op=ALU.mult), reads=[U('QT'), U('egcb')], writes=[U('QgT')])
                sl, sk = nslot()
                for hh in range(6):
                    P.op('pe', lambda e, u=u, hh=hh, sl=sl: e.matmul(sl[:, hh, :], lhsT=KT[u][:, hh, :], rhs=KT[u][:, hh, :], start=True, stop=True),
                         reads=[U('KT')], writes=[sk])
                for hh in range(6):
                    P.op('dve', lambda e, u=u, hh=hh, sl=sl: e.scalar_tensor_tensor(out=LTb[u][:, hh, :], in0=sl[:, hh, :], scalar=cols[u][:, hh:hh + 1], in1=ET[u][:, hh, :],
                                                                                    op0=ALU.mult, op1=ALU.mult),
                         reads=[sk, U('cols'), U('ET')], writes=[U('LTb')], acc=True)
                yield
                sl, sk = nslot()
                for hh in range(6):
                    P.op('pe', lambda e, u=u, hh=hh, sl=sl: e.matmul(sl[:, hh, :], lhsT=KT[u][:, hh, :], rhs=QT[u][:, hh, :], start=True, stop=True),
                         reads=[U('KT'), U('QT')], writes=[sk])
                P.op('dve', lambda e, u=u, sl=sl: e.tensor_tensor(out=intraT[u][:], in0=sl[:, 0:6, :], in1=ET[u][:], op=ALU.mult), reads=[sk, U('ET')], writes=[U('intraT')])
                yield
                sl, sk = nslot()
                for hh in range(6):
                    P.op('pe', lambda e, u=u, hh=hh, sl=sl: e.matmul(sl[:, hh, :], lhsT=KT[u][:, hh, :], rhs=C.ident_b[:], start=True, stop=True),
                         reads=[U('KT'), 'g_ident_b'], writes=[sk])
                P.op('dve', lambda e, u=u, sl=sl: e.tensor_tensor(out=ktil[u][:], in0=sl[:, 0:6, :], in1=bc(cols[u][:, 12:18]), op=ALU.mult), reads=[sk, U('cols')], writes=[U('ktil')])
                yield
                identb3 = C.ident_b[:].unsqueeze(1).broadcast_to([128, NH, 128])

                def mask_level(lv):
                    eng = 'dve' if lv % 2 == 0 else 'pool'
                    P.op(eng, lambda e, u=u, lv=lv: e.tensor_tensor(out=BsT[u][:], in0=LTb[u][:], in1=lmb[:, lv:lv + 1, :].broadcast_to([128, NH, 128]), op=ALU.mult),
                         reads=[U('LTb'), 'c_lmb'], writes=[U('BsT')])
                mask_level(0)
                sl, sk = nslot()
                for hh in range(6):
                    P.op('pe', lambda e, u=u, hh=hh, sl=sl: e.matmul(sl[:, hh, :], lhsT=BsT[u][:, hh, :], rhs=C.ident_b[:], start=True, stop=True),
                         reads=[U('BsT'), 'g_ident_b'], writes=[sk])
                P.op('dve', lambda e, u=u, sl=sl, identb3=identb3: e.tensor_tensor(out=XmA[u][:], in0=identb3, in1=sl[:, 0:6, :], op=ALU.subtract),
                     reads=[sk, 'g_ident_b'], writes=[U('XmA')])
                P.op('pool', lambda e, u=u, identb3=identb3: e.tensor_tensor(out=XTA[u][:], in0=identb3, in1=BsT[u][:], op=ALU.subtract),
                     reads=[U('BsT'), 'g_ident_b'], writes=[U('XTA')])
                yield
                Xm, XT_, XmN, XTN = XmA, XTA, XmB, XTB
                kXm, kXT, kXmN, kXTN = 'XmA', 'XTA', 'XmB', 'XTB'
                for lv in range(1, 7):
                    last = (lv == 6)
                    mask_level(lv)
                    sl, sk = nslot()
                    for hh in range(6):
                        P.op('pe', lambda e, u=u, hh=hh, sl=sl, Xm=Xm: e.matmul(sl[:, hh, :], lhsT=BsT[u][:, hh, :], rhs=Xm[u][:, hh, :], start=True, stop=True),
                             reads=[U('BsT'), U(kXm)], writes=[sk])
                    P.op('act', lambda e, u=u, sl=sl: e.activation(out=Zsb[u][:], in_=sl[:, 0:6, :], func=AF.Copy), reads=[sk], writes=[U('Zsb')])
                    yield
                    if not last:
                        sl, sk = nslot()
                        for hh in range(6):
                            P.op('pe', lambda e, u=u, hh=hh, sl=sl, XT_=XT_: e.matmul(sl[:, hh, :], lhsT=XT_[u][:, hh, :], rhs=Zsb[u][:, hh, :], start=True, stop=True),
                                 reads=[U('Zsb'), U(kXT)], writes=[sk])
                        P.op('dve', lambda e, u=u, sl=sl, Xm=Xm, XmN=XmN: e.tensor_tensor(out=XmN[u][:], in0=Xm[u][:], in1=sl[:, 0:6, :], op=ALU.subtract),
                             reads=[sk, U(kXm)], writes=[U(kXmN)])
                    sl, sk = nslot()
                    for hh in range(6):
                        P.op('pe', lambda e, u=u, hh=hh, sl=sl, XT_=XT_: e.matmul(sl[:, hh, :], lhsT=Zsb[u][:, hh, :], rhs=XT_[u][:, hh, :], start=True, stop=True),
                             reads=[U('Zsb'), U(kXT)], writes=[sk])
                    P.op('dve', lambda e, u=u, sl=sl, XT_=XT_, XTN=XTN: e.tensor_tensor(out=XTN[u][:], in0=XT_[u][:], in1=sl[:, 0:6, :], op=ALU.subtract),
                         reads=[sk, U(kXT)], writes=[U(kXTN)])
                    yield
                    Xm, XT_, XmN, XTN = XmN, XTN, Xm, XT_
                    kXm, kXT, kXmN, kXTN = kXmN, kXTN, kXm, kXT
                assert kXT == 'XTA'

            def scan(c, u):
                U = lambda n, u=u: 'c_%s%d' % (n, u)
                c4 = c % 4
                y2 = (c // 4) % 2
                ykey = 'c_yc%d' % y2
                TT, kTT = XTA, 'XTA'
                sl, sk = nslot()
                for hh in range(6):
                    P.op('pe', lambda e, u=u, hh=hh, sl=sl: e.matmul(sl[:, hh, :], lhsT=VT[u][:, hh, :], rhs=C.ident_b[:], start=True, stop=False),
                         reads=[U('VT'), 'g_ident_b'], writes=[sk])
                    P.op('pe', lambda e, u=u, hh=hh, sl=sl: e.matmul(sl[:, hh, :], lhsT=KgT[u][:, hh, :], rhs=Sb[:, hh, :], start=False, stop=True),
                         reads=[U('KgT'), 'c_Sb'], writes=[sk])
                P.op('act', lambda e, u=u, sl=sl: e.activation(out=rsb[u][:], in_=sl[:, 0:6, :], func=AF.Copy), reads=[sk], writes=[U('rsb')])
                yield
                sl, sk = nslot()
                for hh in range(6):
                    P.op('pe', lambda e, u=u, hh=hh, sl=sl, TT=TT: e.matmul(sl[:, hh, :], lhsT=TT[u][:, hh, :], rhs=rsb[u][:, hh, :], start=True, stop=True),
                         reads=[U(kTT), U('rsb')], writes=[sk])
                P.op('dve', lambda e, u=u, sl=sl: e.tensor_tensor(out=vnew[u][:], in0=sl[:, 0:6, :], in1=bc(cols[u][:, 0:6]), op=ALU.mult), reads=[sk, U('cols')], writes=[U('vnew')])
                yield
                sl, sk = nslot()
                for hh in range(6):
                    P.op('pe', lambda e, u=u, hh=hh, sl=sl: e.matmul(sl[:, hh, :], lhsT=Sb[:, hh, :], rhs=QgT[u][:, hh, :], start=True, stop=False),
                         reads=[U('QgT'), 'c_Sb'], writes=[sk])
                    P.op('pe', lambda e, u=u, hh=hh, sl=sl: e.matmul(sl[:, hh, :], lhsT=vnew[u][:, hh, :], rhs=intraT[u][:, hh, :], start=False, stop=True),
                         reads=[U('vnew'), U('intraT')], writes=[sk])
                P.op('act', lambda e, u=u, sl=sl: e.activation(out=o32[u][:], in_=sl[:, 0:6, :], func=AF.Copy), reads=[sk], writes=[U('egcb')])
                sl, sk = nslot()
                for hh in range(6):
                    P.op('pe', lambda e, u=u, hh=hh, sl=sl: e.matmul(sl[:, hh, :], lhsT=ktil[u][:, hh, :], rhs=vnew[u][:, hh, :], start=True, stop=True),
                         reads=[U('ktil'), U('vnew')], writes=[sk])
                P.op('pool', lambda e, u=u: e.tensor_tensor(out=S32[:], in0=S32[:], in1=bc(cols[u][:, 18:24]), op=ALU.mult), reads=['c_S32', U('cols')], writes=['c_S32'])
                P.op('dve', lambda e, sl=sl: e.tensor_tensor(out=S32[:], in0=S32[:], in1=sl[:, 0:6, :], op=ALU.add), reads=[sk, 'c_S32'], writes=['c_S32'])
                P.op('act', lambda e: e.activation(out=Sb[:], in_=S32[:], func=AF.Copy), reads=['c_S32'], writes=['c_Sb'])
                yield
                P.op('act', lambda e, u=u: e.activation(out=sqt[u][:], in_=o32[u][:], func=AF.Square), reads=[U('egcb')], writes=[U('sqt')])
                sl, sk = nslot()
                P.op('pe', lambda e, u=u, sl=sl: e.matmul(sl[:, 0:4, :], lhsT=C.ones_b[:], rhs=sqt[u][:, 0:4, :], start=True, stop=True),
                     reads=['g_ones_b', U('sqt')], writes=[sk])
                P.op('pe', lambda e, u=u, sl=sl: e.matmul(sl[:, 4:6, :], lhsT=C.ones_b[:], rhs=sqt[u][:, 4:6, :], start=True, stop=True),
                     reads=['g_ones_b', U('sqt')], writes=[sk])
                P.op('act', lambda e, u=u, sl=sl: e.activation(out=ro[u][:], in_=sl[:, 0:6, :], func=AF.Ln, scale=1.0 / 128, bias=C.eps_col[:, 0:1]), reads=[sk], writes=[U('rn')])
                yield
                P.op('act', lambda e, u=u: e.activation(out=ro[u][:], in_=ro[u][:], func=AF.Exp, scale=-0.5), reads=[U('rn')], writes=[U('rn')])
                P.op('dve', lambda e, u=u: e.scalar_tensor_tensor(out=o32[u][:], in0=o32[u][:], scalar=onw[:, 0:1], in1=ro[u][:], op0=ALU.mult, op1=ALU.mult),
                     reads=[U('egcb'), U('rn'), 'c_onw'], writes=[U('egcb')])
                P.op('pool', lambda e, u=u, c4=c4, y2=y2: e.tensor_tensor(out=yc[y2][:, :, c4 * 128:(c4 + 1) * 128], in0=o32[u][:], in1=zc[y2][:, :, c4 * 128:(c4 + 1) * 128], op=ALU.mult),
                     reads=[U('egcb'), 'c_zc%d' % y2], writes=[ykey], acc=True)
                if c4 == 3:
                    t512 = s * T + (c // 4) * 512
                    P.dma('sp', C.yT[YC0:YC0 + 768, t512:t512 + 512].rearrange("(h p) t -> p h t", p=128), yc[y2][:], reads=[ykey], writes=['yT'], acc=True)
                yield

            def merge(ga, gb, ratio):
                da = db = False
                while not (da and db):
                    if not da:
                        try:
                            next(ga)
                        except StopIteration:
                            da = True
                    for _ in range(ratio):
                        if db:
                            break
                        try:
                            next(gb)
                        except StopIteration:
                            db = True

            for _ in pre(0, 0):
                pass
            for c in range(16):
                nxt = pre(c + 1, (c + 1) % 2) if c + 1 < 16 else iter(())
                merge(scan(c, c % 2), nxt, 5)


def yc_sel(yc, s, y2):
    return yc[s]


def build_program(debug=None):
    nc = bass.Bass("TRN2", target_bir_lowering=False)
    C = Ctx()
    din = lambda n, s, d: nc.dram_tensor(n, s, d, kind="ExternalInput").ap()
    C.x = din("x", [NTOK, D], F32)
    C.pos = din("pos", [NSEQ, T], I32)
    C.norm_w = din("norm_w", [DEPTH, D], F32)
    C.w_fm = din("w_fm", [DEPTH, NCH, 128, 16, 128], F32)
    C.w_ba = din("w_ba", [DEPTH, 128, 16, 12], F32)
    C.w_va = din("w_va", [DEPTH, 128, 16, 256], F32)
    C.w_o = din("w_o", [DEPTH, 128, 16, D], F32)
    C.ident_f_d = din("ident_f", [128, 128], F32)
    C.a_invf = din("a_invf", [64, 1], F32)
    C.a_rt = din("a_rt", [64, 64], F32)
    C.a_mask = din("a_mask", [128, 2, 128], F32)
    C.a_qw = din("a_qw", [DEPTH, 64, 1], F32)
    C.a_kw = din("a_kw", [DEPTH, 64, 1], F32)
    C.a_snk = din("a_snk", [DEPTH, 1, 12], F32)
    C.c_cw = din("c_cw", [DEPTH, 128, 18, 4], F32)
    C.c_onw = din("c_onw", [DEPTH, 128, 1], F32)
    C.c_alog = din("c_alog", [DEPTH, 6, 1], F32)
    C.c_dtb = din("c_dtb", [DEPTH, 6, 1], F32)
    C.c_lm = din("c_lm", [128, 7, 128], F32)
    C.c_ngm = din("c_ngm", [128, 128], F32)
    C.c_rsm = din("c_rsm", [6, T], F32)
    C.c_sel = din("c_sel", [6, 6, 128], F32)
    C.b_cw = din("b_cw", [DEPTH, 128, 4, 31], F32)
    C.b_cb = din("b_cb", [DEPTH, 128, 4], F32)
    C.b_lnw = din("b_lnw", [DEPTH, 128, 4], F32)
    C.b_lnb = din("b_lnb", [DEPTH, 128, 4], F32)
    C.b_pwb = din("b_pwb", [DEPTH, 128, 4], F32)
    C.b_pw = din("b_pw", [DEPTH, 128, 4, 512], F32)
    C.out = nc.dram_tensor("out", [NTOK, D], F32, kind="ExternalOutput").ap()
    dbg_in, dbg_out = debug[1] if debug else ((), ())
    scr = lambda n, s, d: nc.dram_tensor(n, s, d, kind=("ExternalInput" if n in dbg_in else "ExternalOutput" if n in dbg_out else "Internal")).ap()
    stages = debug[0] if debug else 'ALL'
    C.pT = scr("pT", [NFM, NTOK], BF16)
    C.baT = scr("baT", [12, NTOK], F32)
    C.va = scr("va", [NTOK, 256], BF16)
    C.yT = scr("yT", [D, NTOK], BF16)
    C.x1 = scr("x1", [NTOK, D], F32)
    names = {id(C.x.tensor): 'x_in', id(C.x1.tensor): 'x1', id(C.out.tensor): 'out'}
    C.xres_key = lambda ap: names[id(ap.tensor)]

    with ExitStack() as st:
        P = Prog(nc, st)
        SB = lambda n, s, d: st.enter_context(nc.sbuf_tensor("g_" + n, s, d))
        C.ident_f = SB("ident_f", [128, 128], F32)
        C.ident_b = SB("ident_b", [128, 128], BF16)
        C.eps_col = SB("eps_col", [128, 1], F32)
        P.dma('sp', C.ident_f[:], C.ident_f_d, writes=['g_ident_f'])
        P.op('dve', lambda e: e.tensor_copy(out=C.ident_b[:], in_=C.ident_f[:]), reads=['g_ident_f'], writes=['g_ident_b'])
        P.op('dve', lambda e: e.memset(C.eps_col[:], EPS), writes=['g_eps'])
        C.one_col = SB("one_col", [128, 1], F32)
        P.op('dve', lambda e: e.memset(C.one_col[:], 1.0), writes=['g_one'])
        C.ones_b = SB("ones_b", [128, 128], BF16)
        P.op('dve', lambda e: e.memset(C.ones_b[:], 1.0), writes=['g_ones_b'])

        P.barrier()
        if stages == 'ALL':
            srcs = [C.x, C.x1]
            dsts = [C.x1, C.out]
            for l in range(DEPTH):
                stage_inproj(P, nc, C, l, srcs[l])
                P.barrier()
                stage_attn(P, nc, C, l)
                P.barrier()
                stage_conformer(P, nc, C, l)
                P.barrier()
                stage_gdn(P, nc, C, l)
                P.barrier()
                stage_outproj(P, nc, C, l, srcs[l], dsts[l])
                P.barrier()
        elif stages == 'P':
            stage_inproj(P, nc, C, 0, C.x)
        elif stages == 'A':
            stage_attn(P, nc, C, 0)
        elif stages == 'C':
            stage_gdn(P, nc, C, 0)
        elif stages == 'B':
            stage_conformer(P, nc, C, 0)
        elif stages == 'O':
            stage_outproj(P, nc, C, 0, C.x, C.out)
        P.finish()
        P.build()
    return nc


def host_inputs(inputs, debug=None):
    w_in = np.asarray(inputs["w_in"], dtype=np.float32)
    w_out = np.asarray(inputs["w_out"], dtype=np.float32)
    cols = np.concatenate([
        np.arange(0, 768), np.arange(768, 1024), np.arange(1280, 2048),
        np.arange(2048, 2560), np.arange(2560, 3072), np.arange(3072, 3584),
        np.arange(3584, 4352), np.arange(4352, 5120), np.arange(5120, 5888),
        np.arange(5900, 6668)])
    assert cols.size == NFM
    wfm = w_in[:, :, cols].reshape(DEPTH, 16, 128, NCH, 128).transpose(0, 3, 2, 1, 4)
    wba = w_in[:, :, 5888:5900].reshape(DEPTH, 16, 128, 12).transpose(0, 2, 1, 3)
    wva = w_in[:, :, 1024:1280].reshape(DEPTH, 16, 128, 256).transpose(0, 2, 1, 3)
    wo = w_out.reshape(DEPTH, 16, 128, D).transpose(0, 2, 1, 3)
    shared = {
        "norm_w": np.ascontiguousarray(inputs["norm_w"], dtype=np.float32),
        "w_fm": np.ascontiguousarray(wfm), "w_ba": np.ascontiguousarray(wba),
        "w_va": np.ascontiguousarray(wva), "w_o": np.ascontiguousarray(wo),
        "ident_f": np.eye(128, dtype=np.float32),
    }
    f32 = lambda k: np.asarray(inputs[k], dtype=np.float32)
    shared["c_cw"] = np.ascontiguousarray(f32("c_conv_w").transpose(0, 2, 1).reshape(DEPTH, 18, 128, 4).transpose(0, 2, 1, 3))
    shared["c_onw"] = np.ascontiguousarray(f32("c_onorm_w").reshape(DEPTH, 128, 1))
    shared["c_alog"] = np.ascontiguousarray(f32("c_a_log").reshape(DEPTH, 6, 1))
    shared["c_dtb"] = np.ascontiguousarray(f32("c_dt_bias").reshape(DEPTH, 6, 1))
    jj_ = np.arange(128)[:, None]
    ii_ = np.arange(128)[None, :]
    lm = np.zeros((128, 7, 128), np.float32)
    for lv in range(7):
        bsz = 1 << lv
        lm[:, lv, :] = ((ii_ // (2 * bsz) == jj_ // (2 * bsz)) & (jj_ % (2 * bsz) < bsz) & (ii_ % (2 * bsz) >= bsz)).astype(np.float32)
    shared["c_lm"] = lm
    shared["c_ngm"] = np.where(ii_ >= jj_, 0.0, -30000.0).astype(np.float32)
    rsm = np.ones((6, T), np.float32)
    rsm[:, 0::128] = 0.0
    shared["c_rsm"] = rsm
    selm = np.zeros((6, 6, 128), np.float32)
    for hh in range(6):
        selm[hh, hh, :] = 1.0
    shared["c_sel"] = selm
    invf = np.zeros((64, 1), np.float32)
    fr = (np.float32(500000.0) ** (-np.arange(0, 16, 2, dtype=np.float32) / np.float32(16))).astype(np.float32)
    invf[0:8, 0] = fr
    invf[8:16, 0] = fr
    rt = np.zeros((64, 64), np.float32)
    for mm in range(8):
        rt[mm + 8, mm] = -1.0
        rt[mm, mm + 8] = 1.0
    jj = np.arange(128)[:, None]
    ii = np.arange(128)[None, :]
    shared["a_invf"] = invf
    shared["a_rt"] = rt
    shared["a_mask"] = np.ascontiguousarray(np.stack([(jj > ii), (jj <= ii)], axis=1).astype(np.float32))
    shared["a_qw"] = np.ascontiguousarray(f32("q_norm_w").reshape(DEPTH, 64, 1))
    shared["a_kw"] = np.ascontiguousarray(f32("k_norm_w").reshape(DEPTH, 64, 1))
    shared["a_snk"] = np.ascontiguousarray(f32("sinks").reshape(DEPTH, 1, 12))
    col4 = lambda a: np.ascontiguousarray(a.reshape(DEPTH, 4, 128).transpose(0, 2, 1))
    shared["b_cw"] = np.ascontiguousarray(f32("b_conv_w").transpose(0, 2, 1).reshape(DEPTH, 4, 128, 31).transpose(0, 2, 1, 3))
    shared["b_cb"] = col4(f32("b_conv_b"))
    shared["b_lnw"] = col4(f32("b_ln_w"))
    shared["b_lnb"] = col4(f32("b_ln_b"))
    shared["b_pwb"] = col4(f32("b_pw_b"))
    shared["b_pw"] = np.ascontiguousarray(f32("b_pw_w").reshape(DEPTH, 4, 128, 512).transpose(0, 2, 1, 3))
    x = np.asarray(inputs["x"], dtype=np.float32)
    pos = np.asarray(inputs["positions"], dtype=np.int32)
    in_maps = []
    for c in range(NCORES):
        m = dict(shared)
        m["x"] = np.ascontiguousarray(x[c * NSEQ:(c + 1) * NSEQ].reshape(NTOK, D))
        m["pos"] = np.ascontiguousarray(pos[c * NSEQ:(c + 1) * NSEQ])
        in_maps.append(m)
    return in_maps


def kernel(**inputs):
    in_maps = host_inputs(inputs)
    nc = build_program()
    res = run_bass_kernel_spmd(nc, in_maps, core_ids=list(range(NCORES)))
    out = np.stack([r["out"].reshape(NSEQ, T, D) for r in res.results], axis=0)
    return out.reshape(NCORES * NSEQ, T, D).astype(np.float32)
```
